# Optimizing a Trainium2 kernel written in Bass

```python
import math
import jax
import jax.numpy as jnp
from jax import lax
import numpy as np

D_MODEL = 2048
BATCH = 2
SEQ = 4096
DEPTH = 4

GRID_W = 64
CTX_LEN = 256
N_MIXERS = 3
EPS = 1e-6
D_FF = ((8 * D_MODEL // 3 + 255) // 256) * 256

S5_GROUP = 16
S5_STATE = 64
S5_GROUPS = D_MODEL // S5_GROUP

HY_ORDER = 2
HY_CONV_W = 3
HY_BANDS = 16
HY_EMB = 1 + 2 * HY_BANDS
HY_FILTER_HIDDEN = 64
HY_DECAY_TARGET = 1e-2
HY_FAST_DECAY_PCT = 0.3
HY_SLOW_DECAY_PCT = 1.5
HY_MAX_DECAY = math.log(HY_DECAY_TARGET) / HY_FAST_DECAY_PCT
HY_MIN_DECAY = math.log(HY_DECAY_TARGET) / HY_SLOW_DECAY_PCT

GDN_K_HEADS = 16
GDN_V_HEADS = 32
GDN_HEAD_K = D_MODEL // GDN_K_HEADS
GDN_HEAD_V = 128
GDN_QK = GDN_K_HEADS * GDN_HEAD_K
GDN_V = GDN_V_HEADS * GDN_HEAD_V
GDN_CONV_DIM = 2 * GDN_QK + GDN_V
GDN_PROJ = GDN_CONV_DIM + GDN_V + 4 * GDN_V_HEADS
GDN_CONV_W = 5
GDN_CHUNK = 64

N_S5 = (DEPTH + 2) // 3
N_HY = (DEPTH + 1) // 3
N_GDN = DEPTH // 3

kernel_name = 'hybrid_s5_hyena_gdn_prefix_dit'

F32 = jnp.float32


def _rmsnorm(x, g):
    xf = x.astype(F32)
    y = xf * lax.rsqrt(jnp.mean(xf * xf, axis=-1, keepdims=True) + EPS)
    return (y * g.astype(F32)).astype(x.dtype)


def _l2norm(x):
    return x * lax.rsqrt(jnp.sum(x * x, axis=-1, keepdims=True) + 1e-6)


def _flip_time(t, rev):
    return jnp.flip(t, axis=1) if rev else t


def _grid_order(x, col_major):
    if not col_major:
        return x
    b, n, d = x.shape
    rows = n // GRID_W
    return x.reshape(b, rows, GRID_W, d).transpose(0, 2, 1, 3).reshape(b, n, d)


def _grid_unorder(x, col_major):
    if not col_major:
        return x
    b, n, d = x.shape
    rows = n // GRID_W
    return x.reshape(b, GRID_W, rows, d).transpose(0, 2, 1, 3).reshape(b, n, d)


def _dwconv(x, w):
    k_w = w.shape[0]
    n = x.shape[1]
    p = k_w // 2
    xp = jnp.pad(x, ((0, 0), (p, p), (0, 0)))
    y = xp[:, 0:n] * w[0]
    for t in range(1, k_w):
        y = y + xp[:, t:t + n] * w[t]
    return y


def _swiglu(h, wg, wu, wd):
    return (jax.nn.silu(h @ wg) * (h @ wu)) @ wd


def _s5_discretise(a_re, a_im, log_dt, b_re, b_im):
    lam = lax.complex(a_re.astype(F32), a_im.astype(F32))
    dt = jnp.exp(log_dt.astype(F32))[:, None]
    lam_bar = jnp.exp(lam * dt)
    b = lax.complex(b_re.astype(F32), b_im.astype(F32))
    b_bar = ((lam_bar - 1.0) / lam)[..., None] * b
    return lam_bar, b_bar


def _s5_scan(u, lam_bar, b_bar, h0):
    bu = jnp.einsum('blgc,gpc->lbgp', u.astype(jnp.complex64), b_bar)
    a = jnp.broadcast_to(lam_bar[None, None], (u.shape[1], 1) + lam_bar.shape)

    def combine(e1, e2):
        a1, b1 = e1
        a2, b2 = e2
        return a2 * a1, a2 * b1 + b2

    a_cum, hs = lax.associative_scan(combine, (a, bu), axis=0)
    hs = hs + a_cum * h0[None]
    return hs, hs[-1]


def _s5_readout(hs, cmat):
    y = jnp.einsum('lbgp,gcp->blgc', hs, cmat).real
    return y.reshape(y.shape[0], y.shape[1], -1)


def _s5_mixer(h_ctx, h_lat, a_re, a_im, log_dt, b_re, b_im, c_re, c_im, d, glu_w, glu_b):
    def groups(t):
        return t.astype(F32).reshape(t.shape[0], t.shape[1], S5_GROUPS, S5_GROUP)

    u_ctx, u_lat = groups(h_ctx), groups(h_lat)
    y_ctx = h_ctx.astype(F32) * d.astype(F32)
    y_lat = h_lat.astype(F32) * d.astype(F32)
    h0 = jnp.zeros((h_lat.shape[0], S5_GROUPS, S5_STATE), jnp.complex64)
    for direction in range(2):
        rev = direction == 1
        lam_bar, b_bar = _s5_discretise(a_re[direction], a_im[direction], log_dt[direction],
                                        b_re[direction], b_im[direction])
        cmat = lax.complex(c_re[direction].astype(F32), c_im[direction].astype(F32))
        hs_ctx, h_ctx_last = _s5_scan(_flip_time(u_ctx, rev), lam_bar, b_bar, h0)
        hs_lat, _ = _s5_scan(_flip_time(u_lat, rev), lam_bar, b_bar, h_ctx_last)
        y_ctx = y_ctx + _flip_time(_s5_readout(hs_ctx, cmat), rev)
        y_lat = y_lat + _flip_time(_s5_readout(hs_lat, cmat), rev)

    def glu(y):
        z = jax.nn.gelu(y)
        gz = z @ glu_w.astype(F32) + glu_b.astype(F32)
        return gz[..., :D_MODEL] * jax.nn.sigmoid(gz[..., D_MODEL:])

    return glu(y_ctx).astype(h_ctx.dtype), glu(y_lat).astype(h_lat.dtype)


def _hyena_filters(n, w1, b1, w2, b2, w3, b3, w4, freq):
    t = jnp.arange(n, dtype=F32)
    t_unit = t / max(n - 1, 1)
    bands = jnp.linspace(1e-4, HY_BANDS - 1, HY_BANDS, dtype=F32)
    ang = (2.0 * math.pi / n) * t[:, None] * bands[None, :]
    feats = jnp.concatenate([t_unit[:, None], jnp.cos(ang), -jnp.sin(ang)], axis=-1)
    fq = freq.astype(F32)
    hdn = jnp.sin(fq * (feats @ w1.astype(F32) + b1.astype(F32)))
    hdn = jnp.sin(fq * (hdn @ w2.astype(F32) + b2.astype(F32)))
    hdn = jnp.sin(fq * (hdn @ w3.astype(F32) + b3.astype(F32)))
    filt = (hdn @ w4.astype(F32)).reshape(n, HY_ORDER, 2, D_MODEL)
    deltas = jnp.abs(jnp.linspace(HY_MIN_DECAY, HY_MAX_DECAY, D_MODEL, dtype=F32))
    filt = filt * jnp.exp(-t_unit[:, None] * deltas[None, :])[:, None, None, :]
    two_sided = jnp.concatenate([filt[:, :, 0],
                                 jnp.zeros((1, HY_ORDER, D_MODEL), F32),
                                 jnp.flip(filt[1:, :, 1], axis=0)], axis=0)
    two_sided = two_sided / jnp.sum(jnp.abs(two_sided), axis=0, keepdims=True)
    return jnp.fft.rfft(two_sided, axis=0)


def _fft_conv(z, hf, bias):
    n = z.shape[1]
    zf = jnp.fft.rfft(z, n=2 * n, axis=1)
    y = jnp.fft.irfft(zf * hf[None], n=2 * n, axis=1)[:, :n]
    return y + z * bias.astype(F32)


def _hyena_mixer(h_ctx, h_lat, in_w, in_b, conv_w, conv_b, f_w1, f_b1, f_w2, f_b2, f_w3, f_b3,
                 f_w4, f_freq, bias, out_w, out_b):
    def one_sequence(h):
        n = h.shape[1]
        u = (h @ in_w + in_b).astype(F32)
        u = _dwconv(u, conv_w.astype(F32)) + conv_b.astype(F32)
        v, x1, x2 = u[..., :D_MODEL], u[..., D_MODEL:2 * D_MODEL], u[..., 2 * D_MODEL:]
        hf = _hyena_filters(n, f_w1, f_b1, f_w2, f_b2, f_w3, f_b3, f_w4, f_freq)
        z = x1 * _fft_conv(v, hf[:, 0], bias[0])
        z = x2 * _fft_conv(z, hf[:, 1], bias[1])
        return (z @ out_w.astype(F32) + out_b.astype(F32)).astype(h.dtype)

    return one_sequence(h_ctx), one_sequence(h_lat)


def _gdn_features(h, in_w, conv_w, a_log, dt_bias):
    bsz, n, _ = h.shape
    proj = (h @ in_w).astype(F32)
    qkv = jax.nn.silu(_dwconv(proj[..., :GDN_CONV_DIM], conv_w.astype(F32)))
    q = qkv[..., :GDN_QK].reshape(bsz, n, GDN_K_HEADS, GDN_HEAD_K)
    k = qkv[..., GDN_QK:2 * GDN_QK].reshape(bsz, n, GDN_K_HEADS, GDN_HEAD_K)
    v = qkv[..., 2 * GDN_QK:].reshape(bsz, n, GDN_V_HEADS, GDN_HEAD_V)
    rep = GDN_V_HEADS // GDN_K_HEADS
    q = jnp.repeat(_l2norm(q), rep, axis=2)
    k = jnp.repeat(_l2norm(k), rep, axis=2)
    z = proj[..., GDN_CONV_DIM:GDN_CONV_DIM + GDN_V].reshape(bsz, n, GDN_V_HEADS, GDN_HEAD_V)
    ab = proj[..., GDN_CONV_DIM + GDN_V:].reshape(bsz, n, 2, 2, GDN_V_HEADS)
    g = -jnp.exp(a_log.astype(F32)) * jax.nn.softplus(ab[:, :, 0] + dt_bias.astype(F32))
    beta = jax.nn.sigmoid(ab[:, :, 1])
    return q, k, v, z, g, beta


def _gdn_chunked(q, k, v, g, beta, h0):
    bsz, n, heads, dk = k.shape
    dv = v.shape[-1]
    nc = n // GDN_CHUNK

    def chunks(t):
        t = t.reshape((bsz, nc, GDN_CHUNK, heads) + t.shape[3:])
        return jnp.moveaxis(t, 3, 1)

    q = chunks(q) * (dk ** -0.5)
    k = chunks(k)
    v = chunks(v)
    g = chunks(g)
    beta = chunks(beta)
    gc = jnp.cumsum(g, axis=-1)
    idx = jnp.arange(GDN_CHUNK)
    incl = idx[:, None] >= idx[None, :]
    strict = idx[:, None] > idx[None, :]
    decay = jnp.exp(jnp.where(incl, gc[..., :, None] - gc[..., None, :], -jnp.inf))
    kb = k * beta[..., None]
    lower = jnp.where(strict, jnp.einsum('bhnid,bhnjd->bhnij', kb, k) * decay, 0.0)
    tri = lower + jnp.eye(GDN_CHUNK, dtype=F32)
    u = lax.linalg.triangular_solve(tri, v * beta[..., None], left_side=True, lower=True,
                                    unit_diagonal=True)
    w = lax.linalg.triangular_solve(tri, kb * jnp.exp(gc)[..., None], left_side=True, lower=True,
                                    unit_diagonal=True)
    attn = jnp.where(incl, jnp.einsum('bhnid,bhnjd->bhnij', q, k) * decay, 0.0)
    qg = q * jnp.exp(gc)[..., None]
    kg = k * jnp.exp(gc[..., -1:] - gc)[..., None]
    gl = jnp.exp(gc[..., -1])
    xs = tuple(jnp.moveaxis(t, 2, 0) for t in (qg, kg, u, w, attn, gl))

    def step(state, inp):
        qg_i, kg_i, u_i, w_i, attn_i, gl_i = inp
        v_new = u_i - jnp.einsum('bhcd,bhde->bhce', w_i, state)
        o_i = (jnp.einsum('bhcd,bhde->bhce', qg_i, state)
               + jnp.einsum('bhcs,bhse->bhce', attn_i, v_new))
        state = state * gl_i[..., None, None] + jnp.einsum('bhcd,bhce->bhde', kg_i, v_new)
        return state, o_i

    h_last, o = lax.scan(step, h0, xs)
    o = jnp.transpose(o, (1, 0, 3, 2, 4)).reshape(bsz, n, heads, dv)
    return o, h_last


def _gdn_mixer(h_ctx, h_lat, in_w, conv_w, a_log, dt_bias, norm_g, out_w):
    qc, kc, vc, zc, gc, bc = _gdn_features(h_ctx, in_w, conv_w, a_log, dt_bias)
    ql, kl, vl, zl, gl, bl = _gdn_features(h_lat, in_w, conv_w, a_log, dt_bias)
    h0 = jnp.zeros((h_lat.shape[0], GDN_V_HEADS, GDN_HEAD_K, GDN_HEAD_V), F32)
    o_ctx = jnp.zeros_like(vc)
    o_lat = jnp.zeros_like(vl)
    for direction in range(2):
        rev = direction == 1
        oc, h_ctx_last = _gdn_chunked(_flip_time(qc, rev), _flip_time(kc, rev), _flip_time(vc, rev),
                                      _flip_time(gc[:, :, direction], rev),
                                      _flip_time(bc[:, :, direction], rev), h0)
        ol, _ = _gdn_chunked(_flip_time(ql, rev), _flip_time(kl, rev), _flip_time(vl, rev),
                             _flip_time(gl[:, :, direction], rev),
                             _flip_time(bl[:, :, direction], rev), h_ctx_last)
        o_ctx = o_ctx + _flip_time(oc, rev)
        o_lat = o_lat + _flip_time(ol, rev)

    def gated_out(o, z):
        o = o * lax.rsqrt(jnp.mean(o * o, axis=-1, keepdims=True) + EPS)
        o = o * norm_g.astype(F32) * jax.nn.silu(z)
        return o.reshape(o.shape[0], o.shape[1], GDN_V) @ out_w.astype(F32)

    return gated_out(o_ctx, zc).astype(h_ctx.dtype), gated_out(o_lat, zl).astype(h_lat.dtype)


def setup_inputs(seed: int = 0) -> dict:
    key = jax.random.key(seed)
    ks = iter(jax.random.split(key, 48))

    def nrm(shape, std):
        return jax.random.normal(next(ks), shape, F32) * std

    def uni(shape, lo, hi):
        return jax.random.uniform(next(ks), shape, F32, lo, hi)

    d = D_MODEL
    g_, p_, hg = S5_GROUPS, S5_STATE, S5_GROUP
    inp = {}
    inp['x'] = nrm((BATCH, SEQ, d), 1.0)
    inp['c'] = nrm((BATCH, d), 1.0)
    inp['ctx'] = nrm((BATCH, CTX_LEN, d), 1.0)
    inp['c_ctx'] = nrm((d,), 1.0)
    inp['ada_w'] = nrm((DEPTH, d, 6 * d), 0.5 * d ** -0.5)
    inp['ada_b'] = nrm((DEPTH, 6 * d), 0.01)
    inp['norm_g'] = 1.0 + nrm((DEPTH, 2, d), 0.01)
    inp['final_g'] = 1.0 + nrm((d,), 0.01)
    inp['ffn_w_gate'] = nrm((DEPTH, d, D_FF), d ** -0.5)
    inp['ffn_w_up'] = nrm((DEPTH, d, D_FF), d ** -0.5)
    inp['ffn_w_down'] = nrm((DEPTH, D_FF, d), D_FF ** -0.5)
    inp['s5_a_re'] = -0.5 + nrm((N_S5, 2, g_, p_), 0.01)
    inp['s5_a_im'] = jnp.pi * jnp.arange(p_, dtype=F32) + nrm((N_S5, 2, g_, p_), 0.01)
    inp['s5_log_dt'] = uni((N_S5, 2, g_), math.log(1e-3), math.log(1e-1))
    inp['s5_b_re'] = nrm((N_S5, 2, g_, p_, hg), (2 * hg) ** -0.5)
    inp['s5_b_im'] = nrm((N_S5, 2, g_, p_, hg), (2 * hg) ** -0.5)
    inp['s5_c_re'] = nrm((N_S5, 2, g_, hg, p_), p_ ** -0.5)
    inp['s5_c_im'] = nrm((N_S5, 2, g_, hg, p_), p_ ** -0.5)
    inp['s5_d'] = nrm((N_S5, d), 0.5)
    inp['s5_glu_w'] = nrm((N_S5, d, 2 * d), d ** -0.5)
    inp['s5_glu_b'] = nrm((N_S5, 2 * d), 0.01)
    inp['hy_in_w'] = nrm((N_HY, d, 3 * d), d ** -0.5)
    inp['hy_in_b'] = nrm((N_HY, 3 * d), 0.01)
    inp['hy_conv_w'] = nrm((N_HY, HY_CONV_W, 3 * d), HY_CONV_W ** -0.5)
    inp['hy_conv_b'] = nrm((N_HY, 3 * d), 0.01)
    inp['hy_f_w1'] = nrm((N_HY, HY_EMB, HY_FILTER_HIDDEN), HY_EMB ** -0.5)
    inp['hy_f_b1'] = nrm((N_HY, HY_FILTER_HIDDEN), 0.02)
    inp['hy_f_w2'] = nrm((N_HY, HY_FILTER_HIDDEN, HY_FILTER_HIDDEN), HY_FILTER_HIDDEN ** -0.5)
    inp['hy_f_b2'] = nrm((N_HY, HY_FILTER_HIDDEN), 0.02)
    inp['hy_f_w3'] = nrm((N_HY, HY_FILTER_HIDDEN, HY_FILTER_HIDDEN), HY_FILTER_HIDDEN ** -0.5)
    inp['hy_f_b3'] = nrm((N_HY, HY_FILTER_HIDDEN), 0.02)
    inp['hy_f_w4'] = nrm((N_HY, HY_FILTER_HIDDEN, HY_ORDER * 2 * d), HY_FILTER_HIDDEN ** -0.5)
    inp['hy_f_freq'] = 1.0 + nrm((N_HY, HY_FILTER_HIDDEN), 0.01)
    inp['hy_bias'] = nrm((N_HY, HY_ORDER, d), 1.0)
    inp['hy_out_w'] = nrm((N_HY, d, d), d ** -0.5)
    inp['hy_out_b'] = nrm((N_HY, d), 0.01)
    inp['gdn_in_w'] = nrm((N_GDN, d, GDN_PROJ), d ** -0.5)
    inp['gdn_conv_w'] = nrm((N_GDN, GDN_CONV_W, GDN_CONV_DIM), GDN_CONV_W ** -0.5)
    inp['gdn_a_log'] = jnp.log(uni((N_GDN, 2, GDN_V_HEADS), 1.0, 16.0))
    dt = jnp.exp(uni((N_GDN, 2, GDN_V_HEADS), math.log(1e-3), math.log(1e-1)))
    inp['gdn_dt_bias'] = dt + jnp.log(-jnp.expm1(-dt))
    inp['gdn_norm_g'] = 1.0 + nrm((N_GDN, GDN_HEAD_V), 0.01)
    inp['gdn_out_w'] = nrm((N_GDN, GDN_V, d), GDN_V ** -0.5)
    return inp


def reference(x, c, ctx, c_ctx, ada_w, ada_b, norm_g, final_g, ffn_w_gate, ffn_w_up, ffn_w_down,
              s5_a_re, s5_a_im, s5_log_dt, s5_b_re, s5_b_im, s5_c_re, s5_c_im, s5_d, s5_glu_w,
              s5_glu_b, hy_in_w, hy_in_b, hy_conv_w, hy_conv_b, hy_f_w1, hy_f_b1, hy_f_w2, hy_f_b2,
              hy_f_w3, hy_f_b3, hy_f_w4, hy_f_freq, hy_bias, hy_out_w, hy_out_b, gdn_in_w,
              gdn_conv_w, gdn_a_log, gdn_dt_bias, gdn_norm_g, gdn_out_w):
    silu_c = jax.nn.silu(c)
    silu_cc = jax.nn.silu(c_ctx)
    for i in range(DEPTH):
        kind, j = i % N_MIXERS, i // N_MIXERS
        col_major = (j % 2) == 1
        sh1, sc1, gt1, sh2, sc2, gt2 = jnp.split((silu_c @ ada_w[i] + ada_b[i])[:, None, :], 6, axis=-1)
        csh1, csc1, cgt1, csh2, csc2, cgt2 = jnp.split(silu_cc @ ada_w[i] + ada_b[i], 6, axis=-1)
        h_lat = _grid_order(_rmsnorm(x, norm_g[i, 0]) * (1.0 + sc1) + sh1, col_major)
        h_ctx = _rmsnorm(ctx, norm_g[i, 0]) * (1.0 + csc1) + csh1
        if kind == 0:
            o_ctx, o_lat = _s5_mixer(h_ctx, h_lat, s5_a_re[j], s5_a_im[j], s5_log_dt[j], s5_b_re[j],
                                     s5_b_im[j], s5_c_re[j], s5_c_im[j], s5_d[j], s5_glu_w[j],
                                     s5_glu_b[j])
        elif kind == 1:
            o_ctx, o_lat = _hyena_mixer(h_ctx, h_lat, hy_in_w[j], hy_in_b[j], hy_conv_w[j],
                                        hy_conv_b[j], hy_f_w1[j], hy_f_b1[j], hy_f_w2[j], hy_f_b2[j],
                                        hy_f_w3[j], hy_f_b3[j], hy_f_w4[j], hy_f_freq[j], hy_bias[j],
                                        hy_out_w[j], hy_out_b[j])
        else:
            o_ctx, o_lat = _gdn_mixer(h_ctx, h_lat, gdn_in_w[j], gdn_conv_w[j], gdn_a_log[j],
                                      gdn_dt_bias[j], gdn_norm_g[j], gdn_out_w[j])
        x = x + gt1 * _grid_unorder(o_lat, col_major)
        x = x + gt2 * _swiglu(_rmsnorm(x, norm_g[i, 1]) * (1.0 + sc2) + sh2,
                              ffn_w_gate[i], ffn_w_up[i], ffn_w_down[i])
        if i < DEPTH - 1:
            ctx = ctx + cgt1 * o_ctx
            ctx = ctx + cgt2 * _swiglu(_rmsnorm(ctx, norm_g[i, 1]) * (1.0 + csc2) + csh2,
                                       ffn_w_gate[i], ffn_w_up[i], ffn_w_down[i])
    return _rmsnorm(x, final_g)
```

```python
import contextlib
import math
import numpy as np
import concourse.bass as bass
import concourse.mybir as mybir
from concourse.bass_utils import run_bass_kernel_spmd

F32 = mybir.dt.float32
BF16 = mybir.dt.bfloat16
AF = mybir.ActivationFunctionType
ALU = mybir.AluOpType

D = 2048
KT = 16
B = 2
SEQ = 4096
CTX = 256
DEPTH = 4
DFF = 5632
NTOK = 1088
LSEQ = CTX + SEQ
EPS = 1e-6
PI = math.pi
MAGIC = 12582912.0
TT = ((0, 64, 1), (64, 512, 0), (576, 512, 0))

ENGS = ("sync", "scalar", "vector", "gpsimd", "tensor")
N_DMA_SEMS = 24


SIM_FULLW = False
AW = 51200


class _Res:
    __slots__ = ("last_w", "readers")

    def __init__(self):
        self.last_w = None
        self.readers = []


class Prog:
    def __init__(self):
        self.nc = bass.Bass("TRN2", target_bir_lowering=False)
        self.stack = contextlib.ExitStack()
        self.q = {e: [] for e in ENGS}
        self.cnt = {e: 0 for e in ENGS}
        self.seen = {e: {} for e in ENGS}
        self.res = {}
        self.sem = {e: self.nc.alloc_semaphore(name="p_" + e) for e in ENGS}
        self.dsem = [self.nc.alloc_semaphore(name="d%d" % i) for i in range(N_DMA_SEMS)]
        self.dval = [0] * N_DMA_SEMS
        self.dnext = 0
        self.dnext_g = 0
        self.ccsem = self.nc.alloc_semaphore(name="cc")
        self.ccn = 0
        self.out_events = []
        self.n_t = 0
        self.banks = [self.ps([128, 512], F32, name="bank%d" % i) for i in range(8)]
        self.bank_i = 0
        self.arena = self.sb([128, AW], F32, name="arena")
        self.aoff = 0
        self.abase = 0
        self.pid = self.nc.partition_id()
        p8 = self.pid % 8
        self.cb = p8 // 4
        self.cq = p8 % 4
        self.x_mlat = (self.cb * 512) * LSEQ + self.cq * 1024
        self.x_mctx = (self.cb * 512) * LSEQ + self.cq * 64 + SEQ
        self.x_h5 = ((self.cq // 2) * 8192 + self.cb * 4096 + (self.cq % 2) * 512) * NTOK
        self.x_hb = (self.cb * 4096) * NTOK
        self.x_q = self.cq * 512
        self.D = {}
        self.inputs = {}
        self.nscr = 0

    def dram(self, name, shape, dtype=F32, kind="Internal", addr_space="Local"):
        return self.nc.dram_tensor(name, list(shape), dtype, kind=kind, addr_space=addr_space).ap()

    def inp(self, name, shape, dtype=F32):
        self.inputs[name] = (tuple(shape), dtype)
        return self.dram(name, shape, dtype, kind="ExternalInput")

    def outp(self, name, shape, dtype=F32):
        return self.dram(name, shape, dtype, kind="ExternalOutput")

    def sb(self, shape, dtype=F32, name=None):
        self.n_t += 1
        return self.stack.enter_context(
            self.nc.sbuf_tensor("s_" + (name or ("t%d" % self.n_t)), list(shape), dtype))

    def ps(self, shape, dtype=F32, name=None):
        self.n_t += 1
        return self.stack.enter_context(
            self.nc.psum_tensor("ps_" + (name or ("p%d" % self.n_t)), list(shape), dtype))

    def bank(self):
        b = self.banks[self.bank_i]
        self.bank_i = (self.bank_i + 1) % 8
        return b

    def f32(self, n):
        off = (self.aoff + 15) // 16 * 16
        self.aoff = off + n
        assert self.aoff <= AW, ("SBUF arena overflow", self.aoff)
        return self.arena[:, off:off + n]

    def bf(self, n):
        w = (n + 1) // 2
        return self.f32(w).bitcast(BF16)[:, 0:n]

    def scratch_pool(self, n, width=512):
        self.scratch = [self.f32(width) for _ in range(n)]
        self.scr_i = 0

    def scr(self):
        t = self.scratch[self.scr_i]
        self.scr_i = (self.scr_i + 1) % len(self.scratch)
        return (t, "scr%d" % self.scr_i)

    def phase_end(self, reset=True):
        for e in ENGS:
            wl = []
            for e2 in ENGS:
                if e2 != e and self.cnt[e2] > self.seen[e].get(e2, 0):
                    self.seen[e][e2] = self.cnt[e2]
                    wl.append((e2, self.cnt[e2]))
            for s_ in range(N_DMA_SEMS):
                key = ("d", s_)
                if self.dval[s_] > self.seen[e].get(key, 0):
                    self.seen[e][key] = self.dval[s_]
                    wl.append((key, self.dval[s_]))
            if self.ccn > self.seen[e].get("cc", 0):
                self.seen[e]["cc"] = self.ccn
                wl.append(("cc", self.ccn))
            self.q[e].append((wl, None, None))
        if not reset:
            return
        self.res = {}
        self.aoff = self.abase

    def _key(self, k):
        if isinstance(k, tuple):
            return (self._key(k[0]),) + tuple(k[1:])
        if isinstance(k, (str, int)):
            return k
        return "T:" + k.name

    def _r(self, key):
        key = self._key(key)
        r = self.res.get(key)
        if r is None:
            r = self.res[key] = _Res()
        return r

    def _collect(self, eng, reads, writes, nosame=False):
        waits = {}

        def need(ev):
            if ev is None:
                return
            k, v = ev
            if waits.get(k, 0) < v:
                waits[k] = v

        for r in reads:
            need(self._r(r).last_w)
        for w in writes:
            rr = self._r(w)
            need(rr.last_w)
            for ev in rr.readers:
                need(ev)
        wl = []
        for k, v in waits.items():
            if self.seen[eng].get(k, 0) >= v:
                continue
            if k == eng and nosame:
                continue
            self.seen[eng][k] = v
            wl.append((k, v))
        return wl

    def _commit(self, ev, reads, writes):
        for r in reads:
            rr = self._r(r)
            rr.readers.append(ev)
            if len(rr.readers) > 64:
                best = {}
                for (k, v) in rr.readers:
                    if best.get(k, 0) < v:
                        best[k] = v
                rr.readers = list(best.items())
        for w in writes:
            rr = self._r(w)
            rr.last_w = ev
            rr.readers = []

    def op(self, eng, fn, reads=(), writes=(), nosame=False):
        wl = self._collect(eng, reads, writes, nosame)
        self.cnt[eng] += 1
        ev = (eng, self.cnt[eng])
        self.q[eng].append((wl, fn, ("e", eng)))
        self._commit(ev, reads, writes)
        return ev

    def dma(self, out, in_, reads=(), writes=(), eng="sync", is_output=False, **kw):
        if eng == "gpsimd":
            s = 16 + self.dnext_g
            self.dnext_g = (self.dnext_g + 1) % 8
        else:
            s = self.dnext
            self.dnext = (self.dnext + 1) % 16
        wl = self._collect(eng, reads, writes)
        key = ("d", s)
        prev = self.dval[s]
        if prev > 0 and self.seen[eng].get(key, 0) < prev:
            self.seen[eng][key] = prev
            wl.append((key, prev))
        self.dval[s] = prev + 16
        ev = (key, prev + 16)

        def fn(e, out=out, in_=in_, kw=kw):
            try:
                return e.dma_start(out=out, in_=in_, **kw)
            except Exception:
                print("DMA FAILED out=", out.tensor.name, out.shape, out.ap, out.offset, " in=", in_.tensor.name, in_.shape, in_.ap, str(in_.offset)[-300:])
                raise

        self.q[eng].append((wl, fn, ("d", s)))
        self._commit(ev, reads, writes)
        if is_output:
            self.out_events.append(ev)
        return ev

    def gather(self, src, dst, reads=(), writes=()):
        eng = "gpsimd"
        wl = self._collect(eng, reads, writes)
        if self.ccn > 0 and self.seen[eng].get("cc", 0) < self.ccn:
            self.seen[eng]["cc"] = self.ccn
            wl.append(("cc", self.ccn))
        self.ccn += 1
        ev = ("cc", self.ccn)

        def fn(e, src=src, dst=dst):
            return e.collective_compute("AllGather", ALU.bypass, replica_groups=[list(range(8))],
                                        ins=[src.opt()], outs=[dst.opt()])

        self.q[eng].append((wl, fn, ("c",)))
        self._commit(ev, reads, writes)
        return ev

    def _semof(self, k):
        if isinstance(k, tuple):
            return self.dsem[k[1]]
        if k == "cc":
            return self.ccsem
        return self.sem[k]

    def build(self):
        best = {}
        for (k, v) in self.out_events:
            if best.get(k, 0) < v:
                best[k] = v
        self.q["sync"].append((list(best.items()), None, None))
        nc = self.nc
        with nc.Block() as block:
            def mk(ename):
                def body(e):
                    for (wl, fn, kind) in self.q[ename]:
                        for (k, v) in wl:
                            e.wait_ge(self._semof(k), v)
                        if fn is None:
                            continue
                        ins = fn(e)
                        if kind[0] == "e":
                            ins.then_inc(self.sem[kind[1]], 1)
                        elif kind[0] == "d":
                            ins.then_inc(self.dsem[kind[1]], 16)
                        else:
                            ins.then_inc(self.ccsem)
                return body
            block.sync(mk("sync"))
            block.scalar(mk("scalar"))
            block.vector(mk("vector"))
            block.gpsimd(mk("gpsimd"))
            block.tensor(mk("tensor"))
        self.stack.close()
        return nc


def dyn2d(t, off, rowstride, nrows, ncols):
    return bass.AP(t.tensor, off, [[rowstride, nrows], [1, ncols]])


def mm(P, out, lhsT, rhs, start, stop, reads, writes):
    return P.op("tensor", lambda e: e.matmul(out, lhsT, rhs, start=start, stop=stop),
                reads=reads, writes=writes, nosame=True)


class WRep:
    def __init__(self, P, name, R, C, cws):
        self.name, self.R, self.C = name, R, C
        self.chunks = []
        c0 = 0
        for j, cw in enumerate(cws):
            if SIM_FULLW:
                full = P.dram("%s_g%d" % (name, j), [R, cw], BF16)
                P.dma(full, P.inp("%s_%d" % (name, j), [R, cw]), writes=[("wg", name, j)], eng="gpsimd")
                self.chunks.append((c0, cw, full))
                c0 += cw
                continue
            sh = P.inp("%s_%d" % (name, j), [R // 8, cw])
            bo = P.dram("%s_b%d" % (name, j), [R // 8, cw], BF16)
            full = P.dram("%s_g%d" % (name, j), [R, cw], BF16)
            P.dma(bo, sh, writes=[("wb", name, j)], eng="gpsimd")
            P.gather(bo, full, reads=[("wb", name, j)], writes=[("wg", name, j)])
            self.chunks.append((c0, cw, full))
            c0 += cw
        assert c0 == C

    def src(self, k0, ktn, n0, ncols, chunk=None):
        if chunk is None:
            for j, (c0, cw, full) in enumerate(self.chunks):
                if c0 <= n0 and n0 + ncols <= c0 + cw:
                    chunk, n0 = j, n0 - c0
                    break
            assert chunk is not None, (self.name, n0, ncols)
            cs = slice(n0, n0 + ncols)
        else:
            cs = bass.ds(n0, ncols)
        full = self.chunks[chunk][2]
        return (full[k0 * 128:(k0 + ktn) * 128, cs].rearrange("(kt p) n -> p kt n", p=128),
                ("wg", self.name, chunk))


def shard_cols(W, cws, k):
    R = W.shape[0]
    out = []
    c0 = 0
    for cw in cws:
        out.append(np.ascontiguousarray(W[k * (R // 8):(k + 1) * (R // 8), c0:c0 + cw]))
        c0 += cw
    return out


class WStream:
    def __init__(self, P, ktn, ncols, nbuf=4, tag="w"):
        self.P = P
        self.bufs = [(P.bf(ktn * ncols).rearrange("p (k n) -> p k n", k=ktn), "%s%d" % (tag, i)) for i in range(nbuf)]
        self.i = 0

    def load(self, wrep, k0, ktn, n0, ncols, chunk=None):
        P = self.P
        buf, key = self.bufs[self.i]
        self.i = (self.i + 1) % len(self.bufs)
        src, skey = wrep.src(k0, ktn, n0, ncols, chunk)
        P.dma(buf[:, 0:ktn, 0:ncols], src, reads=[skey], writes=[key])
        return buf, key


def dense_group(P, wbuf, wkey, ktn, nchunks, act, act_res, epilogue, tiles=TT):
    for c in range(nchunks):
        for ti, (t0, tn, col) in enumerate(tiles):
            bank = P.bank()
            for kt in range(ktn):
                mm(P, bank[:, 0:tn], wbuf[:, kt, c * 128:(c + 1) * 128], act[:, kt, t0:t0 + tn],
                   kt == 0, kt == ktn - 1, reads=[wkey] + list(act_res), writes=[bank])
            epilogue(c, ti, bank)


def rmsnorm_rstd(P, xT, ones, rstd, eps_t):
    for ti, (t0, tn, col) in enumerate(TT):
        bank = P.bank()
        for kt in range(KT):
            s, sk = P.scr()
            P.op("scalar", lambda e, s=s, kt=kt, t0=t0, tn=tn: e.activation(
                out=s[:, 0:tn], in_=xT[:, kt, t0:t0 + tn], func=AF.Square), reads=[("x", kt)], writes=[sk])
            mm(P, bank[:, 0:tn], ones, s[:, 0:tn], kt == 0, kt == KT - 1, reads=[sk, "ones"], writes=[bank])
        P.op("scalar", lambda e, bank=bank, t0=t0, tn=tn: e.activation(
            out=rstd[:, t0:t0 + tn], in_=bank[:, 0:tn], func=AF.Sqrt, bias=eps_t, scale=1.0 / D),
            reads=[bank, "eps"], writes=[("rstd", ti)])
        P.op("vector", lambda e, t0=t0, tn=tn: e.reciprocal(out=rstd[:, t0:t0 + tn], in_=rstd[:, t0:t0 + tn]),
             reads=[("rstd", ti)], writes=[("rstd", ti)])


KINDS = ("s5", "hy", "gdn", "s5")
S5_TILES = (0, 1, 2, 3)
FFN_CWS = (2816, 2816)


class State:
    pass


def cinp(P, S, name, shape):
    if name not in S.cin:
        S.cin[name] = P.inp(name, shape)
    return S.cin[name]


def setup_state(P):
    S = State()
    S.cin = {}
    S.ones = P.f32(128)
    S.eps = P.f32(1)
    S.mod = [P.f32(192).rearrange("p (f c) -> p f c", c=2) for _ in range(2)]
    ones_d = P.inp("ones", [128, 128])
    P.dma(S.ones, ones_d, writes=["ones"])
    P.op("vector", lambda e: e.memset(S.eps, EPS), writes=["eps"])
    P.abase = P.aoff
    S.Xs = P.dram("Xs", [128, KT, NTOK], F32)
    S.Hs = P.dram("Hs", [D, NTOK], BF16)
    S.Hg = P.dram("Hg", [2 * 8 * 1024, NTOK], BF16)
    S.Ms = [P.dram("Ms%d" % t, [128, LSEQ], BF16) for t in range(8)]
    S.Mg = [P.dram("Mg%d" % t, [8 * 128, LSEQ], BF16) for t in range(8)]
    S.Mloc = P.dram("Mloc", [32 * 128, NTOK], BF16)
    S.Hloc = P.dram("Hloc", [2 * 4 * 1024, NTOK], BF16)
    S.W = {}
    return S


def weights_for_A(P, S, i):
    if i > 0:
        prev = KINDS[i - 1]
        li = i - 1
        if prev == "s5":
            S.W["mix%d" % li] = WRep(P, "glu%d" % li, D, 2 * D, (4096,))
        elif prev == "hy":
            S.W["mix%d" % li] = WRep(P, "hyo", D, D, (2048,))
        else:
            S.W["mix%d" % li] = WRep(P, "gdo", 2 * D, D, (2048,))
        S.W["wg%d" % li] = WRep(P, "wg%d" % li, D, DFF, FFN_CWS)
        S.W["wu%d" % li] = WRep(P, "wu%d" % li, D, DFF, FFN_CWS)
        S.W["wd%d" % li] = WRep(P, "wd%d" % li, DFF, D, (2048,))
    if i < DEPTH:
        S.W["ada%d" % i] = WRep(P, "ada%d" % i, D, 6 * D, (4096, 4096, 4096))


def phase_A(P, S, i):
    prev = KINDS[i - 1] if i > 0 else None
    li = i - 1
    final = (i == DEPTH)
    xT = P.f32(KT * NTOK).rearrange("p (k t) -> p k t", k=KT)
    bufA = P.bf(28 * NTOK)
    mT = bufA[:, 0:16 * NTOK].rearrange("p (k t) -> p k t", k=16)
    aT = bufA[:, 16 * NTOK:28 * NTOK].rearrange("p (k t) -> p k t", k=12)
    rstd = P.f32(NTOK)
    P.scratch_pool(6)
    ws = WStream(P, 16, 256, nbuf=4, tag="w")
    ones, eps_t = S.ones, S.eps
    XR = lambda kt: ("x", kt)
    MRES = [("m", kt) for kt in range(16)]
    if i == 0:
        x_src = P.inp("xT0", [128, KT, NTOK])
    else:
        x_src = S.Xs
    for kt in range(KT):
        P.dma(xT[:, kt, :], x_src[:, kt, :], reads=["Xs"], writes=[XR(kt)])

    if prev is not None:
        modp = S.mod[li % 2]
        gt1 = lambda j, col: modp[:, 2 * 16 + j, col:col + 1]
        sh2 = lambda j, col: modp[:, 3 * 16 + j, col:col + 1]
        gt2 = lambda j, col: modp[:, 5 * 16 + j, col:col + 1]
        km = 32 if prev == "gdn" else 16
        tpc = km // 4
        Wm = S.W["mix%d" % li]

        for t in range(tpc):
            dst = S.Mloc[0:km * 128, :].rearrange("(q t p) n -> p q t n", t=tpc, p=128)[:, :, t, :]
            P.dma(dst[:, :, 64:NTOK], bass.AP(S.Mg[t].tensor, P.x_mlat, [[LSEQ, 128], [128 * LSEQ, 4], [1, 1024]]),
                  reads=[("Mg", t)], writes=["Mloc"])
            P.dma(dst[:, :, 0:64], bass.AP(S.Mg[t].tensor, P.x_mctx, [[LSEQ, 128], [128 * LSEQ, 4], [1, 64]]),
                  reads=[("Mg", t)], writes=["Mloc"])

        def load_m(half):
            P.dma(mT, S.Mloc[half * 2048:(half + 1) * 2048, :].rearrange("(k p) n -> p k n", p=128),
                  reads=["Mloc"], writes=MRES)

        if prev == "s5":
            load_m(0)
            bm = P.f32(32)
            P.dma(bm, P.inp("glu_b%d" % li, [128, 32]), writes=["bm"])
            for j in range(KT):
                wv, wvk = ws.load(Wm, 0, 16, j * 128, 128)
                wg, wgk = ws.load(Wm, 0, 16, D + j * 128, 128)
                for ti, (t0, tn, col) in enumerate(TT):
                    bv = P.bank()
                    bg = P.bank()
                    for kt in range(16):
                        mm(P, bg[:, 0:tn], wg[:, kt, 0:128], mT[:, kt, t0:t0 + tn], kt == 0, kt == 15,
                           reads=[wgk] + MRES, writes=[bg])
                    for kt in range(16):
                        mm(P, bv[:, 0:tn], wv[:, kt, 0:128], mT[:, kt, t0:t0 + tn], kt == 0, kt == 15,
                           reads=[wvk] + MRES, writes=[bv])
                    sg, sgk = P.scr()
                    tm, tmk = P.scr()
                    P.op("scalar", lambda e, sg=sg, bg=bg, j=j, tn=tn: e.activation(
                        out=sg[:, 0:tn], in_=bg[:, 0:tn], func=AF.Sigmoid, bias=bm[:, 16 + j:17 + j], scale=1.0),
                        reads=[bg, "bm"], writes=[sgk])
                    P.op("vector", lambda e, tm=tm, bv=bv, sg=sg, j=j, tn=tn: e.scalar_tensor_tensor(
                        out=tm[:, 0:tn], in0=bv[:, 0:tn], scalar=bm[:, j:j + 1], in1=sg[:, 0:tn],
                        op0=ALU.add, op1=ALU.mult), reads=[bv, sgk, "bm"], writes=[tmk])
                    P.op("vector", lambda e, tm=tm, j=j, t0=t0, tn=tn, col=col: e.scalar_tensor_tensor(
                        out=xT[:, j, t0:t0 + tn], in0=tm[:, 0:tn], scalar=gt1(j, col), in1=xT[:, j, t0:t0 + tn],
                        op0=ALU.mult, op1=ALU.add), reads=[tmk, ("mod", li % 2), XR(j)], writes=[XR(j)])
        else:
            if prev == "hy":
                bm = P.f32(16)
                P.dma(bm, P.inp("hy_out_b", [128, 16]), writes=["bm"])
            for half in range(km // 16):
                load_m(half)
                for jg in range(KT // 2):
                    wb, wbk = ws.load(Wm, half * 16, 16, jg * 256, 256)

                    def epi(c, ti, bank, jg=jg):
                        j = jg * 2 + c
                        t0, tn, col = TT[ti]
                        if prev == "hy":
                            tm, tmk = P.scr()
                            P.op("vector", lambda e: e.tensor_scalar(
                                out=tm[:, 0:tn], in0=bank[:, 0:tn], scalar1=bm[:, j:j + 1], scalar2=None, op0=ALU.add),
                                reads=[bank, "bm"], writes=[tmk])
                            src, srck = tm, tmk
                        else:
                            src, srck = bank, bank
                        P.op("vector", lambda e: e.scalar_tensor_tensor(
                            out=xT[:, j, t0:t0 + tn], in0=src[:, 0:tn], scalar=gt1(j, col), in1=xT[:, j, t0:t0 + tn],
                            op0=ALU.mult, op1=ALU.add), reads=[srck, ("mod", li % 2), XR(j)], writes=[XR(j)])
                    dense_group(P, wb, wbk, 16, 2, mT, MRES, epi)

        g2 = P.f32(KT)
        P.dma(g2, P.inp("g2_%d" % li, [128, KT]), writes=["g2"])
        coef2 = P.f32(2 * KT).rearrange("p (k c) -> p k c", c=2)
        for col in range(2):
            P.op("vector", lambda e, col=col: e.scalar_tensor_tensor(
                out=coef2[:, :, col], in0=modp[:, 64:80, col], scalar=1.0, in1=g2,
                op0=ALU.add, op1=ALU.mult), reads=[("mod", li % 2), "g2"], writes=["coef2"])
        Wg, Wu, Wd = S.W["wg%d" % li], S.W["wu%d" % li], S.W["wd%d" % li]
        rmsnorm_rstd(P, xT, ones, rstd, eps_t)
        h2 = mT
        for kt in range(KT):
            for ti, (t0, tn, col) in enumerate(TT):
                tm, tmk = P.scr()
                P.op("vector", lambda e, tm=tm, kt=kt, t0=t0, tn=tn: e.tensor_tensor(
                    out=tm[:, 0:tn], in0=xT[:, kt, t0:t0 + tn], in1=rstd[:, t0:t0 + tn], op=ALU.mult),
                    reads=[XR(kt), ("rstd", ti)], writes=[tmk])
                P.op("vector", lambda e, tm=tm, kt=kt, t0=t0, tn=tn, col=col: e.tensor_scalar(
                    out=h2[:, kt, t0:t0 + tn], in0=tm[:, 0:tn], scalar1=coef2[:, kt, col:col + 1],
                    scalar2=sh2(kt, col), op0=ALU.mult, op1=ALU.add),
                    reads=[tmk, "coef2", ("mod", li % 2)], writes=[("m", kt)])
        passes = ((0, 12), (12, 12), (24, 12), (36, 8))
        for (f0, nf) in passes:
            for g in range(nf // 2):
                fc0 = f0 + g * 2
                wgb, wgk = ws.load(Wg, 0, 16, fc0 * 128, 256)
                wub, wuk = ws.load(Wu, 0, 16, fc0 * 128, 256)
                for c in range(2):
                    a_idx = g * 2 + c
                    for ti, (t0, tn, col) in enumerate(TT):
                        bg = P.bank()
                        bu = P.bank()
                        for kt in range(16):
                            mm(P, bg[:, 0:tn], wgb[:, kt, c * 128:(c + 1) * 128], h2[:, kt, t0:t0 + tn],
                               kt == 0, kt == 15, reads=[wgk] + MRES, writes=[bg])
                        for kt in range(16):
                            mm(P, bu[:, 0:tn], wub[:, kt, c * 128:(c + 1) * 128], h2[:, kt, t0:t0 + tn],
                               kt == 0, kt == 15, reads=[wuk] + MRES, writes=[bu])
                        sg, sgk = P.scr()
                        P.op("scalar", lambda e, sg=sg, bg=bg, tn=tn: e.activation(
                            out=sg[:, 0:tn], in_=bg[:, 0:tn], func=AF.Silu), reads=[bg], writes=[sgk])
                        P.op("vector", lambda e, sg=sg, bu=bu, a_idx=a_idx, t0=t0, tn=tn: e.tensor_tensor(
                            out=aT[:, a_idx, t0:t0 + tn], in0=sg[:, 0:tn], in1=bu[:, 0:tn], op=ALU.mult),
                            reads=[sgk, bu], writes=[("a", a_idx)])
            ARES = [("a", k) for k in range(nf)]
            for jg in range(KT // 2):
                wdb, wdk = ws.load(Wd, f0, nf, jg * 256, 256)

                def epi2(c, ti, bank, jg=jg):
                    j = jg * 2 + c
                    t0, tn, col = TT[ti]
                    P.op("vector", lambda e: e.scalar_tensor_tensor(
                        out=xT[:, j, t0:t0 + tn], in0=bank[:, 0:tn], scalar=gt2(j, col), in1=xT[:, j, t0:t0 + tn],
                        op0=ALU.mult, op1=ALU.add), reads=[bank, ("mod", li % 2), XR(j)], writes=[XR(j)])
                dense_group(P, wdb, wdk, nf, 2, aT, ARES, epi2)

    if not final:
        for kt in range(KT):
            P.dma(S.Xs[:, kt, :], xT[:, kt, :], reads=[XR(kt)], writes=["Xs"])
        adaW = S.W["ada%d" % i]
        adab = P.f32(96)
        scT = P.f32(2 * KT).rearrange("p (k c) -> p k c", c=2)
        scTb = P.bf(2 * KT).rearrange("p (k c) -> p k c", c=2)
        g1 = P.f32(KT)
        modn = S.mod[i % 2]
        P.dma(adab, P.inp("ada_b%d" % i, [128, 96]), writes=["adab"])
        if i == 0:
            S.silu_in = P.inp("silu_in", [128, KT, 2])
        P.dma(scT, S.silu_in, writes=["scT"])
        P.dma(g1, P.inp("g1_%d" % i, [128, KT]), writes=["g1"])
        P.op("scalar", lambda e: e.activation(out=scTb, in_=scT, func=AF.Silu), reads=["scT"], writes=["scTb"])
        for gi in range(48):
            ab, abk = ws.load(adaW, 0, 16, gi * 256, 256)
            for c in range(2):
                fc = gi * 2 + c
                bank = P.bank()
                for kt in range(KT):
                    mm(P, bank[:, 0:2], ab[:, kt, c * 128:(c + 1) * 128], scTb[:, kt, :], kt == 0, kt == KT - 1,
                       reads=[abk, "scTb"], writes=[bank])
                P.op("vector", lambda e, bank=bank, fc=fc: e.tensor_scalar(
                    out=modn[:, fc, :], in0=bank[:, 0:2], scalar1=adab[:, fc:fc + 1], scalar2=None, op0=ALU.add),
                    reads=[bank, "adab"], writes=[("mod", i % 2)])
        coef1 = P.f32(2 * KT).rearrange("p (k c) -> p k c", c=2)
        for col in range(2):
            P.op("vector", lambda e, col=col: e.scalar_tensor_tensor(
                out=coef1[:, :, col], in0=modn[:, 16:32, col], scalar=1.0, in1=g1,
                op0=ALU.add, op1=ALU.mult), reads=[("mod", i % 2), "g1"], writes=["coef1"])
        rmsnorm_rstd(P, xT, ones, rstd, eps_t)
        hsb = [(P.bf(512), "hsb%d" % n) for n in range(3)]
        nh = 0
        for kt in range(KT):
            for ti, (t0, tn, col) in enumerate(TT):
                tm, tmk = P.scr()
                hs, hsk = hsb[nh % 3]
                nh += 1
                P.op("vector", lambda e, tm=tm, kt=kt, t0=t0, tn=tn: e.tensor_tensor(
                    out=tm[:, 0:tn], in0=xT[:, kt, t0:t0 + tn], in1=rstd[:, t0:t0 + tn], op=ALU.mult),
                    reads=[XR(kt), ("rstd", ti)], writes=[tmk])
                P.op("vector", lambda e, tm=tm, hs=hs, kt=kt, t0=t0, tn=tn, col=col: e.tensor_scalar(
                    out=hs[:, 0:tn], in0=tm[:, 0:tn], scalar1=coef1[:, kt, col:col + 1],
                    scalar2=modn[:, kt, col:col + 1], op0=ALU.mult, op1=ALU.add),
                    reads=[tmk, "coef1", ("mod", i % 2)], writes=[hsk])
                P.dma(S.Hs[kt * 128:(kt + 1) * 128, t0:t0 + tn], hs[:, 0:tn], reads=[hsk, ("Hg", kt // 8)],
                      writes=[("Hs", kt // 8)])
        P.phase_end()
        for c in range(2):
            P.gather(S.Hs[c * 1024:(c + 1) * 1024, :], S.Hg[c * 8192:(c + 1) * 8192, :])
    else:
        gf = P.f32(KT)
        P.dma(gf, P.inp("gf", [128, KT]), writes=["gf"])
        yT_out = P.outp("yT", [128, KT, SEQ // 4])
        rmsnorm_rstd(P, xT, ones, rstd, eps_t)
        for kt in range(KT):
            for ti, (t0, tn, col) in enumerate(TT):
                if ti == 0:
                    continue
                hs, hsk = P.scr()
                P.op("vector", lambda e, hs=hs, kt=kt, t0=t0, tn=tn: e.scalar_tensor_tensor(
                    out=hs[:, 0:tn], in0=xT[:, kt, t0:t0 + tn], scalar=gf[:, kt:kt + 1], in1=rstd[:, t0:t0 + tn],
                    op0=ALU.mult, op1=ALU.mult), reads=[XR(kt), ("rstd", ti), "gf"], writes=[hsk])
                P.dma(yT_out[:, kt, t0 - 64:t0 - 64 + tn], hs[:, 0:tn], reads=[hsk], is_output=True)
    P.phase_end()

NCORES = 8


def fm(a):
    ntok, C = a.shape
    return np.ascontiguousarray(a.T.reshape(C // 128, 128, ntok).transpose(1, 0, 2))


def unfm(a):
    p, kt, ntok = a.shape
    return np.ascontiguousarray(a.transpose(1, 0, 2).reshape(kt * 128, ntok).T)


def vec_fm(v):
    return np.ascontiguousarray(v.reshape(-1, 128).T)


def tok_shard(lat, ctx, k):
    b, r = k // 4, k % 4
    return np.concatenate([ctx[b, 64 * r:64 * r + 64], lat[b, 1024 * r:1024 * r + 1024]], axis=0)


def tok_unshard(parts, C):
    lat = np.empty((B, SEQ, C), np.float32)
    ctx = np.empty((B, CTX, C), np.float32)
    for k in range(NCORES):
        b, r = k // 4, k % 4
        ctx[b, 64 * r:64 * r + 64] = parts[k][:64]
        lat[b, 1024 * r:1024 * r + 1024] = parts[k][64:]
    return lat, ctx


_PROGS = {}


def get_prog(key, builder):
    return builder()


def launch(nc, in_maps):
    res = run_bass_kernel_spmd(nc, in_maps, core_ids=list(range(len(in_maps))))
    return res.results


def dump(P, name, ap, shape, dtype, reads):
    o = P.outp(name, shape, dtype)
    P.dma(o, ap, reads=reads, is_output=True)


def build_test_A0():
    P = Prog()
    S = setup_state(P)
    weights_for_A(P, S, 0)
    P.phase_end()
    phase_A(P, S, 0)
    dump(P, "o_Hg", S.Hg, [2 * 8 * 1024, NTOK], BF16, [("Hg", 0), ("Hg", 1)])
    dump(P, "o_Xs", S.Xs, [128, KT, NTOK], F32, ["Xs"])
    return P.build(), P


def build_test_A1(i=1):
    P = Prog()
    S = setup_state(P)
    li = i - 1
    P.dma(S.Xs, P.inp("inj_Xs", [128, KT, NTOK]), writes=["Xs"])
    P.dma(S.mod[li % 2], P.inp("inj_mod", [128, 96, 2]), writes=[("mod", li % 2)])
    nt = 8 if KINDS[li] == "gdn" else 4
    inj = P.inp("inj_Ms", [nt * 128, LSEQ], BF16)
    for t in range(nt):
        P.dma(S.Ms[t], inj[t * 128:(t + 1) * 128, :], writes=[("Ms", t)])
        P.gather(S.Ms[t], S.Mg[t], reads=[("Ms", t)], writes=[("Mg", t)])
    S.silu_in = P.inp("silu_in", [128, KT, 2])
    weights_for_A(P, S, i)
    P.phase_end()
    phase_A(P, S, i)
    if i < DEPTH:
        dump(P, "o_Hs", S.Hs, [D, NTOK], BF16, [("Hs", 0), ("Hs", 1)])
        dump(P, "o_Xs", S.Xs, [128, KT, NTOK], F32, ["Xs"])
    return P.build(), P


def _mkap(base, off, dims):
    return bass.AP(base.tensor, base.offset + off, [[base.ap[0][0], 128]] + [list(d_) for d_ in dims])


def s5_views(base, colmajor):
    fw, bw = [], []
    fw.append((_mkap(base, 0, [[1, 256]]), 0, 256, False))
    bw.append((_mkap(base, 255, [[-1, 256]]), 0, 256, False))
    for m in range(8):
        t0 = 256 + 512 * m
        if not colmajor:
            fw.append((_mkap(base, 256 + 512 * m, [[1, 512]]), t0, 512, False))
            bw.append((_mkap(base, 256 + 4095 - 512 * m, [[-1, 512]]), t0, 512, False))
        else:
            fw.append((_mkap(base, 256 + 8 * m, [[1, 8], [64, 64]]), t0, 512, True))
            bw.append((_mkap(base, 256 + 63 * 64 + 63 - 8 * m, [[-1, 8], [-64, 64]]), t0, 512, True))
    return fw, bw


def phase_M_S5(P, S, i):
    j = i // 3
    colmajor = (j % 2) == 1
    L = LSEQ
    par_d = P.inp("s5par%d" % j, [16 * 128, 1216])
    S.iota_d = cinp(P, S, "iota", [128, L])
    S.rowmask_d = cinp(P, S, "rowmask", [128, 8])
    S.sgn_d = cinp(P, S, "sgn1", [128, 1])
    iota = P.f32(L)
    rowmask = P.f32(8)
    sgn1 = P.f32(1)
    negmagic = P.f32(1)
    halfpi = P.f32(1)
    P.dma(iota, S.iota_d, writes=["iota"])
    P.dma(rowmask, S.rowmask_d, writes=["rowmask"])
    P.dma(sgn1, S.sgn_d, writes=["sgn1"])
    P.op("vector", lambda e: e.memset(negmagic, -MAGIC), writes=["negmagic"])
    P.op("vector", lambda e: e.memset(halfpi, PI / 2), writes=["halfpi"])
    u_all = P.bf(4 * L).rearrange("p (t l) -> p t l", t=4)
    yacc = P.f32(L)
    COS, SIN, TA, TB = P.f32(L), P.f32(L), P.f32(L), P.f32(L)
    par_all = P.f32(4 * 1216).rearrange("p (t w) -> p t w", t=4)
    blk_all = par_all[:, :, 0:640].rearrange("p t (d f s) -> p t d f s", d=2, f=5)
    st_all = par_all[:, :, 640:688].rearrange("p t (d g f) -> p t d g f", d=2, g=8)
    ct_all = par_all[:, :, 688:1200].rearrange("p t (d a m) -> p t d a m", d=2, a=2)
    d_all = par_all[:, :, 1200]
    nsm = [0]

    def sm():
        nsm[0] += 1
        return P.f32(128).rearrange("p (d s) -> p d s", d=2), "sm%d" % nsm[0]
    LA = P.f32(256).rearrange("p (d m) -> p d m", d=2)
    LB = P.f32(256).rearrange("p (d m) -> p d m", d=2)
    C1f = P.f32(256).rearrange("p (d m) -> p d m", d=2)
    C2f = P.f32(256).rearrange("p (d m) -> p d m", d=2)
    thp = P.f32(16)
    rrc = P.f32(16)
    stmp = [P.f32(16) for _ in range(3)]
    gw = [[(P.bf(128), "gw%d_%d" % (n, a)) for a in range(4)] for n in range(2)]
    rrt = [(P.f32(512), "rrt%d" % n) for n in range(2)]
    T1 = [(P.f32(512), "T1_0")] * 2
    T2 = [(P.f32(512), "T2_0")] * 2
    Wt = [(P.f32(512), "Wt_0")] * 2
    Gt = [(P.f32(512), "Gt_0")] * 2
    C1t = [(P.bf(512), "C1t_%d" % n) for n in range(2)]
    S1t = [(P.bf(512), "S1t_%d" % n) for n in range(2)]
    tmp_bl = [sm() for _ in range(16)]

    def vop(fn, reads, writes):
        return P.op("vector", fn, reads=reads, writes=writes)

    def aop(fn, reads, writes):
        return P.op("scalar", fn, reads=reads, writes=writes)

    def tt(out, a, b_, op, reads, writes):
        return vop(lambda e: e.tensor_tensor(out=out, in0=a, in1=b_, op=op), reads, writes)

    P.dma(S.Hloc[0:2048, :].rearrange("(r c) n -> c r n", r=4),
          bass.AP(S.Hg.tensor, P.x_h5, [[NTOK, 512], [1024 * NTOK, 4], [1, NTOK]]),
          reads=[("Hg", 0), ("Hg", 1)], writes=["Hloc"])
    for rp in range(4):
        srcv = S.Hloc[rp * 512:(rp + 1) * 512, :].rearrange("(t p) n -> p t n", p=128)
        P.dma(u_all[:, :, CTX + 1024 * rp:CTX + 1024 * rp + 1024], srcv[:, :, 64:NTOK], reads=["Hloc"], writes=["u_all"])
        P.dma(u_all[:, :, 64 * rp:64 * rp + 64], srcv[:, :, 0:64], reads=["Hloc"], writes=["u_all"])
    P.dma(par_all, bass.AP(par_d.tensor, P.x_q * 1216, [[1216, 128], [128 * 1216, 4], [1, 1216]]),
          writes=["blk", "stt", "ct", "dcol"])
    def s5_tile(t):
        ut, uk = u_all[:, t, :], "u_all"
        blk, stt, ct, dcol = blk_all[:, t], st_all[:, t], ct_all[:, t], d_all[:, t:t + 1]
        a_re, a_im, ldt, bTr, bTi = (blk[:, :, f, :] for f in range(5))
        (dt, dtk), (xr, xrk), (th, thk), (rr_, rrk), (y, yk), (k1, k1k), (fr, frk), (sn, snk), (cs, csk), (lbr, lbrk), \
            (lbi, lbik), (den, denk), (cre, crek), (cim, cimk), (q1, q1k), (q2, q2k) = tmp_bl
        aop(lambda e: e.activation(out=dt, in_=ldt, func=AF.Exp), ["blk"], [dtk])
        tt(xr, a_re, dt, ALU.mult, ["blk", dtk], [xrk])
        tt(th, a_im, dt, ALU.mult, ["blk", dtk], [thk])
        aop(lambda e: e.activation(out=rr_, in_=xr, func=AF.Exp), [xrk], [rrk])
        vop(lambda e: e.tensor_scalar(out=y, in0=th, scalar1=1.0 / (2 * PI), scalar2=None, op0=ALU.mult), [thk], [yk])
        vop(lambda e: e.tensor_scalar(out=k1, in0=y, scalar1=MAGIC, scalar2=None, op0=ALU.add), [yk], [k1k])
        vop(lambda e: e.tensor_scalar(out=k1, in0=k1, scalar1=-MAGIC, scalar2=None, op0=ALU.add), [k1k], [k1k])
        tt(fr, y, k1, ALU.subtract, [yk, k1k], [frk])
        aop(lambda e: e.activation(out=sn, in_=fr, func=AF.Sin, scale=2 * PI), [frk], [snk])
        aop(lambda e: e.activation(out=fr, in_=fr, func=AF.Abs), [frk, snk], [frk])
        aop(lambda e: e.activation(out=cs, in_=fr, func=AF.Sin, scale=-2 * PI, bias=halfpi), [frk, "halfpi"], [csk])
        tt(lbr, rr_, cs, ALU.mult, [rrk, csk], [lbrk])
        vop(lambda e: e.tensor_scalar(out=lbr, in0=lbr, scalar1=-1.0, scalar2=None, op0=ALU.add), [lbrk], [lbrk])
        tt(lbi, rr_, sn, ALU.mult, [rrk, snk], [lbik])
        tt(den, a_re, a_re, ALU.mult, ["blk"], [denk])
        tt(q1, a_im, a_im, ALU.mult, ["blk"], [q1k])
        tt(den, den, q1, ALU.add, [denk, q1k], [denk])
        vop(lambda e: e.reciprocal(out=den, in_=den), [denk], [denk])
        tt(cre, lbr, a_re, ALU.mult, [lbrk, "blk"], [crek])
        tt(q1, lbi, a_im, ALU.mult, [lbik, "blk"], [q1k])
        tt(cre, cre, q1, ALU.add, [crek, q1k], [crek])
        tt(cre, cre, den, ALU.mult, [crek, denk], [crek])
        tt(cim, lbi, a_re, ALU.mult, [lbik, "blk"], [cimk])
        tt(q1, lbr, a_im, ALU.mult, [lbrk, "blk"], [q1k])
        tt(cim, cim, q1, ALU.subtract, [cimk, q1k], [cimk])
        tt(cim, cim, den, ALU.mult, [cimk, denk], [cimk])
        tt(q1, cre, bTr, ALU.mult, [crek, "blk"], [q1k])
        tt(q2, cim, bTi, ALU.mult, [cimk, "blk"], [q2k])
        tt(LA[:, :, 0:64], q1, q2, ALU.subtract, [q1k, q2k], ["LA"])
        vop(lambda e: e.tensor_scalar(out=LB[:, :, 64:128], in0=LA[:, :, 0:64], scalar1=-1.0, scalar2=None, op0=ALU.mult),
            ["LA"], ["LB"])
        tt(q1, cre, bTi, ALU.mult, [crek, "blk"], [q1k])
        tt(q2, cim, bTr, ALU.mult, [cimk, "blk"], [q2k])
        tt(LA[:, :, 64:128], q1, q2, ALU.add, [q1k, q2k], ["LA"])
        vop(lambda e: e.tensor_copy(out=LB[:, :, 0:64], in_=LA[:, :, 64:128]), ["LA"], ["LB"])
        s0, s1_, s2 = (x_.rearrange("p (d g) -> p d g", d=2) for x_ in stmp)
        thp3 = thp.rearrange("p (d g) -> p d g", d=2)
        rrc3 = rrc.rearrange("p (d g) -> p d g", d=2)
        aop(lambda e: e.activation(out=s0, in_=stt[:, :, :, 2], func=AF.Exp), ["stt"], ["s0"])
        tt(s1_, stt[:, :, :, 1], s0, ALU.mult, ["stt", "s0"], ["s1"])
        vop(lambda e: e.tensor_scalar(out=thp3, in0=s1_, scalar1=1.0 / (2 * PI), scalar2=None, op0=ALU.mult), ["s1"], ["thp"])
        tt(s2, stt[:, :, :, 0], s0, ALU.mult, ["stt", "s0"], ["s2"])
        aop(lambda e: e.activation(out=rrc3, in_=s2, func=AF.Exp), ["s2"], ["rrc"])
        vop(lambda e: e.tensor_scalar(out=C1f, in0=ct[:, :, 0, :], scalar1=sgn1[:, 0:1], scalar2=None, op0=ALU.mult),
            ["ct", "sgn1"], ["C1f"])
        vop(lambda e: e.tensor_scalar(out=C2f, in0=ct[:, :, 1, :], scalar1=-1.0, scalar2=None, op0=ALU.mult), ["ct"], ["C2f"])
        vop(lambda e, ut=ut: e.tensor_scalar(out=yacc, in0=ut, scalar1=dcol[:, 0:1], scalar2=None, op0=ALU.mult),
            [uk, "dcol"], ["yacc"])
        ufw, ubw = s5_views(ut, colmajor)
        yfw, ybw = s5_views(yacc, colmajor)
        ng = 0
        for dr in range(2):
            uv = ufw if dr == 0 else ubw
            yv = yfw if dr == 0 else ybw
            for g8 in range(8):
                col = dr * 8 + g8
                (la, lak), (lb, lbk), (c1g, c1gk), (c2g, c2gk) = gw[ng % 2]
                rt, rtk = rrt[ng % 2]
                ng += 1
                la2, lb2 = la.rearrange("p (a m) -> p a m", a=1)[:, 0, :], lb
                vop(lambda e, la=la, dr=dr, g8=g8: e.tensor_scalar(out=la, in0=LA[:, dr, :], scalar1=rowmask[:, g8:g8 + 1],
                                                                    scalar2=None, op0=ALU.mult), ["LA", "rowmask"], [lak])
                vop(lambda e, lb=lb, dr=dr, g8=g8: e.tensor_scalar(out=lb, in0=LB[:, dr, :], scalar1=rowmask[:, g8:g8 + 1],
                                                                    scalar2=None, op0=ALU.mult), ["LB", "rowmask"], [lbk])
                vop(lambda e, c1g=c1g: e.memset(c1g, 0.0), [], [c1gk])
                vop(lambda e, c2g=c2g: e.memset(c2g, 0.0), [], [c2gk])
                vop(lambda e, c1g=c1g, dr=dr, g8=g8: e.tensor_copy(out=c1g[:, 16 * g8:16 * g8 + 16], in_=C1f[:, dr, 16 * g8:16 * g8 + 16]),
                    ["C1f"], [c1gk])
                vop(lambda e, c2g=c2g, dr=dr, g8=g8: e.tensor_copy(out=c2g[:, 16 * g8:16 * g8 + 16], in_=C2f[:, dr, 16 * g8:16 * g8 + 16]),
                    ["C2f"], [c2gk])
                vop(lambda e, rt=rt, col=col: e.tensor_scalar(out=rt, in0=iota[:, 0:512], scalar1=0.0, scalar2=rrc[:, col:col + 1],
                                                              op0=ALU.mult, op1=ALU.add), ["iota", "rrc"], [rtk])
                vop(lambda e, col=col: e.tensor_scalar(out=TA, in0=iota, scalar1=thp[:, col:col + 1], scalar2=MAGIC,
                                                       op0=ALU.mult, op1=ALU.add), ["iota", "thp"], ["TA"])
                aop(lambda e: e.activation(out=TA, in_=TA, func=AF.Identity, bias=negmagic, scale=1.0), ["TA", "negmagic"], ["TA"])
                vop(lambda e, col=col: e.scalar_tensor_tensor(out=TB, in0=iota, scalar=thp[:, col:col + 1], in1=TA,
                                                              op0=ALU.mult, op1=ALU.subtract), ["iota", "thp", "TA"], ["TB"])
                aop(lambda e: e.activation(out=SIN, in_=TB, func=AF.Sin, scale=2 * PI), ["TB"], ["SIN"])
                aop(lambda e: e.activation(out=TA, in_=TB, func=AF.Abs), ["TB"], ["TA"])
                aop(lambda e: e.activation(out=COS, in_=TA, func=AF.Sin, scale=-2 * PI, bias=halfpi), ["TA", "halfpi"], ["COS"])
                prevG = None
                for ti in range(9):
                    useg, tau0, n, is3 = uv[ti]
                    yseg = yv[ti][0]
                    ba, bb = P.bank(), P.bank()
                    v3 = (lambda ap: ap.rearrange("p (c r) -> p c r", c=8)) if is3 else (lambda ap: ap)
                    mm(P, v3(ba[:, 0:n]), la, useg, True, True, reads=[lak, uk], writes=[ba])
                    mm(P, v3(bb[:, 0:n]), lb, useg, True, True, reads=[lbk, uk], writes=[bb])
                    (t1, t1k), (t2, t2k), (w_, wk), (G, Gk) = T1[ti % 2], T2[ti % 2], Wt[ti % 2], Gt[ti % 2]
                    (c1, c1k), (s1, s1k) = C1t[ti % 2], S1t[ti % 2]
                    tt(t1[:, 0:n], ba[:, 0:n], COS[:, tau0:tau0 + n], ALU.mult, [ba, "COS"], [t1k])
                    tt(t2[:, 0:n], bb[:, 0:n], SIN[:, tau0:tau0 + n], ALU.mult, [bb, "SIN"], [t2k])
                    tt(w_[:, 0:n], t1[:, 0:n], t2[:, 0:n], ALU.add, [t1k, t2k], [wk])
                    init = 0.0 if prevG is None else prevG[0]
                    vop(lambda e, G=G, rt=rt, w_=w_, n=n, init=init: e.tensor_tensor_scan(
                        out=G[:, 0:n], data0=rt[:, 0:n], data1=w_[:, 0:n], initial=init, op0=ALU.mult, op1=ALU.add),
                        [rtk, wk] + ([prevG[1]] if prevG else []), [Gk])
                    prevG = (G[:, n - 1:n], Gk)
                    tt(c1[:, 0:n], G[:, 0:n], COS[:, tau0:tau0 + n], ALU.mult, [Gk, "COS"], [c1k])
                    tt(s1[:, 0:n], G[:, 0:n], SIN[:, tau0:tau0 + n], ALU.mult, [Gk, "SIN"], [s1k])
                    by = P.bank()
                    mm(P, by[:, 0:n], c1g, c1[:, 0:n], True, False, reads=[c1gk, c1k], writes=[by])
                    mm(P, by[:, 0:n], c2g, s1[:, 0:n], False, True, reads=[c2gk, s1k], writes=[by])
                    tt(yseg, v3(by[:, 0:n]), yseg, ALU.add, [by, "yacc"], ["yacc"])
        aop(lambda e: e.activation(out=ut, in_=yacc, func=AF.Gelu), ["yacc"], [uk])

    for t in S5_TILES:
        s5_tile(t)
    P.phase_end(reset=False)
    for t in S5_TILES:
        P.dma(S.Ms[t][:, 0:SEQ], u_all[:, t, CTX:L])
        P.dma(S.Ms[t][:, SEQ:L], u_all[:, t, 0:CTX])
    P.phase_end()
    for t in range(4):
        P.gather(S.Ms[t], S.Mg[t])
    P.phase_end()


def s5_host(I, j):
    f = np.float32
    a_re, a_im, ldt = I["s5_a_re"][j], I["s5_a_im"][j], I["s5_log_dt"][j]
    b_re, b_im, c_re, c_im = I["s5_b_re"][j], I["s5_b_im"][j], I["s5_c_re"][j], I["s5_c_im"][j]
    blk = np.empty((16, 8, 16, 2, 5, 64), f)
    A = lambda a: a.reshape(2, 16, 8, 64).transpose(1, 2, 0, 3)[:, :, None, :, :]
    blk[:, :, :, :, 0, :] = A(a_re)
    blk[:, :, :, :, 1, :] = A(a_im)
    blk[:, :, :, :, 2, :] = ldt.reshape(2, 16, 8).transpose(1, 2, 0)[:, :, None, :, None]
    Bt = lambda b_: b_.reshape(2, 16, 8, 64, 16).transpose(1, 2, 4, 0, 3)
    blk[:, :, :, :, 3, :] = Bt(b_re)
    blk[:, :, :, :, 4, :] = Bt(b_im)
    st = np.empty((16, 2, 64, 2, 8, 3), f)
    As = lambda a: a.reshape(2, 16, 8, 64).transpose(1, 3, 0, 2)[:, None, :, :, :]
    st[..., 0] = As(a_re)
    st[..., 1] = As(a_im)
    st[..., 2] = ldt.reshape(2, 16, 8).transpose(1, 0, 2)[:, None, None, :, :]
    Ct = lambda c_: c_.reshape(2, 16, 8, 16, 64).transpose(1, 4, 0, 2, 3).reshape(16, 64, 2, 128)
    c = np.empty((16, 2, 64, 2, 2, 128), f)
    c[:, 0, :, :, 0, :] = Ct(c_re)
    c[:, 1, :, :, 0, :] = Ct(c_im)
    c[:, 0, :, :, 1, :] = Ct(c_im)
    c[:, 1, :, :, 1, :] = Ct(c_re)
    par = np.zeros((16 * 128, 1216), f)
    par[:, 0:640] = blk.reshape(16 * 128, 640)
    par[:, 640:688] = st.reshape(16 * 128, 48)
    par[:, 688:1200] = c.reshape(16 * 128, 512)
    par[:, 1200] = I["s5_d"][j].reshape(16 * 128)
    return {"s5par%d" % j: par}


def const_host():
    p = np.arange(128)
    return {"ones": np.ones((128, 128), np.float32),
            "iota": np.ascontiguousarray(np.broadcast_to(np.arange(LSEQ, dtype=np.float32), (128, LSEQ))),
            "rowmask": (p[:, None] // 16 == np.arange(8)[None, :]).astype(np.float32),
            "sgn1": np.where(p < 64, 1.0, -1.0).astype(np.float32)[:, None]}


def build_test_M(i):
    P = Prog()
    S = setup_state(P)
    P.dma(S.Hs, P.inp("inj_Hs", [D, NTOK], BF16), writes=[("Hs", 0), ("Hs", 1)])
    for c in range(2):
        P.gather(S.Hs[c * 1024:(c + 1) * 1024, :], S.Hg[c * 8192:(c + 1) * 8192, :], reads=[("Hs", c)], writes=[("Hg", c)])
    P.phase_end()
    kind = KINDS[i]
    if kind == "s5":
        phase_M_S5(P, S, i)
        nch = 2
    elif kind == "hy":
        weights_for_M(P, S, i)
        phase_M_HY(P, S, i)
        nch = 2
    else:
        weights_for_M(P, S, i)
        phase_M_GDN(P, S, i)
        nch = 4
    o = P.outp("o_Ms", [nch * 256, LSEQ], BF16)
    for t in range(nch * 2):
        P.dma(o[t * 128:(t + 1) * 128, :], S.Ms[t], reads=[("Ms", t)], is_output=True)
    return P.build(), P


HY_NCOL = 4224
HY_MAX_DECAY = math.log(1e-2) / 0.3
HY_MIN_DECAY = math.log(1e-2) / 1.5


def weights_for_M(P, S, i):
    kind = KINDS[i]
    if kind == "hy":
        S.W["hyin"] = WRep(P, "hyin", D, 3 * D, (2048, 2048, 2048))
    elif kind == "gdn":
        S.W["gdin"] = WRep(P, "gdin", D, 12288, (4096, 4096, 4096))


def trig_tables(P, tabc, tabs, nrt, ncol, rowsc, iota, negmagic, halfpi, bufs):
    TA, TB, SIN, COS, Cb, Sb = bufs

    def one(a):
        P.op("vector", lambda e: e.tensor_scalar(out=TA[:, 0:ncol], in0=iota[:, 0:ncol], scalar1=rowsc[:, a:a + 1], scalar2=MAGIC,
                                                 op0=ALU.mult, op1=ALU.add), reads=["iota", "rowsc"], writes=["TA"])
        P.op("scalar", lambda e: e.activation(out=TA[:, 0:ncol], in_=TA[:, 0:ncol], func=AF.Identity, bias=negmagic, scale=1.0),
             reads=["TA", "negmagic"], writes=["TA"])
        P.op("vector", lambda e: e.scalar_tensor_tensor(out=TB[:, 0:ncol], in0=iota[:, 0:ncol], scalar=rowsc[:, a:a + 1], in1=TA[:, 0:ncol],
                                                        op0=ALU.mult, op1=ALU.subtract), reads=["iota", "rowsc", "TA"], writes=["TB"])
        P.op("scalar", lambda e: e.activation(out=Sb[:, 0:ncol], in_=TB[:, 0:ncol], func=AF.Sin, scale=2 * PI), reads=["TB"], writes=["Sb"])
        P.op("scalar", lambda e: e.activation(out=TA[:, 0:ncol], in_=TB[:, 0:ncol], func=AF.Abs), reads=["TB"], writes=["TA"])
        P.op("scalar", lambda e: e.activation(out=Cb[:, 0:ncol], in_=TA[:, 0:ncol], func=AF.Sin, scale=-2 * PI, bias=halfpi),
             reads=["TA", "halfpi"], writes=["Cb"])
        P.dma(tabc[a * 128:(a + 1) * 128, :], Cb[:, 0:ncol], reads=["Cb"], writes=["tabc"])
        P.dma(tabs[a * 128:(a + 1) * 128, :], Sb[:, 0:ncol], reads=["Sb"], writes=["tabs"])
    for a in range(nrt):
        one(a)


def phase_M_HY(P, S, i):
    L = LSEQ
    Win = S.W["hyin"]
    NC = HY_NCOL
    U = P.dram("hyU", [12 * 128, L], F32)
    Z1 = P.dram("hyZ1", [4 * 128, L], F32)
    TABC = P.dram("hyTC", [NC, NC], BF16)
    TABS = P.dram("hyTS", [NC, NC], BF16)
    TABC2 = P.dram("hyTC2", [384, 384], BF16)
    TABS2 = P.dram("hyTS2", [384, 384], BF16)
    FL = P.dram("hyFL", [2 * 2 * SEQ, 512], BF16)
    FC = P.dram("hyFC", [2 * 2 * CTX, 512], BF16)
    par_d = P.inp("hypar", [4 * 128, 4 * 17])
    w4_d = P.inp("hyw4", [64, 4 * 2048])
    w1_d = P.inp("hyw1", [33, 64])
    w23_d = P.inp("hyw23", [64, 128])
    fe_l = P.inp("hyfeL", [33, SEQ])
    fe_c = P.inp("hyfeC", [33, CTX])
    tun_d = P.inp("hytun", [128, 34])
    dl_d = P.inp("hydelta", [4 * 128, 512])
    rs_d = P.inp("hyrowsc", [128, 36])
    wk_d = P.inp("hywk", [128, 36])
    id_d = cinp(P, S, "ident", [128, 128])
    io_d = cinp(P, S, "iota", [128, L])
    iota = P.f32(NC)
    negmagic, halfpi = P.f32(1), P.f32(1)
    rowsc = P.f32(36)
    P.dma(iota, io_d[:, 0:NC], writes=["iota"])
    P.dma(rowsc, rs_d, writes=["rowsc"])
    P.op("vector", lambda e: e.memset(negmagic, -MAGIC), writes=["negmagic"])
    P.op("vector", lambda e: e.memset(halfpi, PI / 2), writes=["halfpi"])
    bufs = (P.f32(NC), P.f32(NC), None, None, P.bf(NC), P.bf(NC))
    trig_tables(P, TABC, TABS, 33, NC, rowsc, iota, negmagic, halfpi, bufs)
    trig_tables(P, TABC2, TABS2, 3, 384, rowsc[:, 33:36], iota, negmagic, halfpi, bufs)
    P.phase_end()

    for c in range(2):
        P.dma(S.Hloc[c * 4096:(c + 1) * 4096, :].rearrange("(r k) n -> k r n", r=4),
              bass.AP(S.Hg.tensor, P.x_hb + c * 8192 * NTOK, [[NTOK, 1024], [1024 * NTOK, 4], [1, NTOK]]),
              writes=["Hloc"], eng="scalar")
    par = P.f32(68).rearrange("p (c f) -> p c f", c=4)
    P.dma(par, bass.AP(par_d.tensor, P.cq * (128 * 68), [[68, 128], [1, 68]]).rearrange("p (c f) -> p c f", c=4),
          writes=["par"], eng="scalar")
    wall = P.bf(16 * 1536).rearrange("p (k n) -> p k n", k=16)
    for s_ in range(3):
        src, skey = Win.src(0, 16, P.x_q, 512, chunk=s_)
        P.dma(wall[:, :, s_ * 512:(s_ + 1) * 512], src, reads=[skey], writes=["wall"], eng="scalar")
    P.scratch_pool(4)
    hb = [(P.bf(16 * 512).rearrange("p (k n) -> p k n", k=16), "hb%d" % n) for n in range(2)]
    stiles = [(0, 256, [(rp, 0, 64) for rp in range(4)])] + \
             [(256 + 512 * m, 512, [(m // 2, 64 + (m % 2) * 512, 512)]) for m in range(8)]

    def inproj_tile(n_, c0, w, pieces):
        h, hk = hb[n_ % 2]
        off = 0
        for (rp, lc, pw) in pieces:
            for c in range(2):
                P.dma(h[:, c * 8:(c + 1) * 8, off:off + pw],
                      S.Hloc[c * 4096 + rp * 1024:c * 4096 + (rp + 1) * 1024, lc:lc + pw].rearrange("(k p) n -> p k n", p=128),
                      reads=["Hloc"], writes=[hk])
            off += pw
        for j in range(12):
            bank = P.bank()
            for kt in range(16):
                mm(P, bank[:, 0:w], wall[:, kt, j * 128:(j + 1) * 128], h[:, kt, 0:w], kt == 0, kt == 15,
                   reads=["wall", hk], writes=[bank])
            st, stk = P.scr()
            s_, ct = j // 4, j % 4
            P.op("vector", lambda e, bank=bank, st=st, s_=s_, ct=ct: e.tensor_scalar(
                out=st[:, 0:w], in0=bank[:, 0:w], scalar1=par[:, ct, s_:s_ + 1], scalar2=None, op0=ALU.add),
                reads=[bank, "par"], writes=[stk])
            P.dma(U[j * 128:(j + 1) * 128, c0:c0 + w], st[:, 0:w], reads=[stk], writes=[("U", j)])
    for n_, (c0, w, pieces) in enumerate(stiles):
        inproj_tile(n_, c0, w, pieces)
    P.phase_end(reset=False)

    ua = [(P.f32(L), "ua%d" % n) for n in range(2)]
    ub = [(P.f32(L), "ub%d" % n) for n in range(2)]

    def dw_tile(j):
        (a, ak), (b_, bk) = ua[j % 2], ub[j % 2]
        s_, ct = j // 4, j % 4
        cw = lambda k: par[:, ct, 3 + 3 * k + s_:4 + 3 * k + s_]
        cb = par[:, ct, 12 + s_:13 + s_]
        P.dma(a, U[j * 128:(j + 1) * 128, :], reads=[("U", j)], writes=[ak])
        P.op("vector", lambda e: e.tensor_scalar(out=b_, in0=a, scalar1=cw(1), scalar2=cb, op0=ALU.mult, op1=ALU.add),
             reads=[ak, "par"], writes=[bk])
        for (s0, s1) in ((0, CTX), (CTX, L)):
            P.op("vector", lambda e, s0=s0, s1=s1: e.scalar_tensor_tensor(
                out=b_[:, s0 + 1:s1], in0=a[:, s0:s1 - 1], scalar=cw(0), in1=b_[:, s0 + 1:s1], op0=ALU.mult, op1=ALU.add),
                reads=[ak, bk, "par"], writes=[bk])
            P.op("vector", lambda e, s0=s0, s1=s1: e.scalar_tensor_tensor(
                out=b_[:, s0:s1 - 1], in0=a[:, s0 + 1:s1], scalar=cw(2), in1=b_[:, s0:s1 - 1], op0=ALU.mult, op1=ALU.add),
                reads=[ak, bk, "par"], writes=[bk])
        P.dma(U[j * 128:(j + 1) * 128, :], b_, reads=[bk], writes=[("U", j)])
    for j in range(12):
        dw_tile(j)
    P.phase_end()

    par = P.f32(68).rearrange("p (c f) -> p c f", c=4)
    P.dma(par, bass.AP(par_d.tensor, P.cq * (128 * 68), [[68, 128], [1, 68]]).rearrange("p (c f) -> p c f", c=4),
          writes=["par"], eng="scalar")
    ident = P.f32(128)
    w1 = P.f32(64)
    w23 = P.f32(128)
    mlpb = P.f32(8)
    tun = P.f32(34)
    wk = P.f32(36)
    w4s = P.f32(4 * 512).rearrange("p (o n) -> p o n", o=4)
    delta = P.f32(512)
    mask0 = P.f32(1)
    P.dma(ident, id_d, writes=["ident"])
    P.dma(w1[0:33, :], w1_d, writes=["w1"])
    P.dma(w23[0:64, :], w23_d, writes=["w23"])
    P.dma(mlpb[0:64, 0:4], P.inp("hymlpb", [64, 4]), writes=["mlpb"])
    P.op("vector", lambda e: e.tensor_scalar(out=mlpb[0:64, 3:4], in0=mlpb[0:64, 3:4], scalar1=1.0 / (2 * PI), scalar2=None,
                                             op0=ALU.mult), reads=["mlpb"], writes=["mlpb"])
    P.dma(tun, tun_d, writes=["tun"])
    P.dma(wk, wk_d, writes=["wk"])
    P.dma(w4s[0:64], bass.AP(w4_d.tensor, P.x_q, [[8192, 64], [2048, 4], [1, 512]]), writes=["w4s"], eng="scalar")
    P.dma(delta, bass.AP(dl_d.tensor, P.cq * (128 * 512), [[512, 128], [1, 512]]), writes=["delta"], eng="scalar")
    P.dma(mask0, P.inp("hymask0", [128, 1]), writes=["mask0"])
    abase_save = P.aoff

    def seq_stage(n, fe_d, tcol0, wcol0, TC, TS, NK, F_d, c_seq0, ms_col0):
        NT = n // 128
        WC = min(512, n)
        P.scratch_pool(6)
        Ha, Hb = P.f32(n), P.f32(n)
        fe = P.f32(n)
        P.dma(fe[0:33, :], fe_d, writes=["fe"])

        def layer(li_, X, K, Wt_, Hout, hkey_in, hkey_out):
            for c0 in range(0, n, WC):
                bank = P.bank()
                mm(P, bank[0:64, 0:WC], Wt_, X[0:K, c0:c0 + WC], True, True, reads=[hkey_in, "w1", "w23"], writes=[bank])
                y_, yk = P.scr()
                k_, kk = P.scr()
                P.op("vector", lambda e, bank=bank, y_=y_: e.tensor_scalar(
                    out=y_[0:64, 0:WC], in0=bank[0:64, 0:WC], scalar1=mlpb[0:64, li_:li_ + 1], scalar2=mlpb[0:64, 3:4],
                    op0=ALU.add, op1=ALU.mult), reads=[bank, "mlpb"], writes=[yk])
                P.op("vector", lambda e, y_=y_, k_=k_: e.tensor_scalar(out=k_[0:64, 0:WC], in0=y_[0:64, 0:WC], scalar1=MAGIC,
                                                                      scalar2=None, op0=ALU.add), reads=[yk], writes=[kk])
                P.op("vector", lambda e, k_=k_: e.tensor_scalar(out=k_[0:64, 0:WC], in0=k_[0:64, 0:WC], scalar1=-MAGIC,
                                                                scalar2=None, op0=ALU.add), reads=[kk], writes=[kk])
                P.op("vector", lambda e, y_=y_, k_=k_: e.tensor_tensor(out=y_[0:64, 0:WC], in0=y_[0:64, 0:WC], in1=k_[0:64, 0:WC],
                                                                       op=ALU.subtract), reads=[yk, kk], writes=[yk])
                P.op("scalar", lambda e, y_=y_, c0=c0: e.activation(out=Hout[0:64, c0:c0 + WC], in_=y_[0:64, 0:WC], func=AF.Sin,
                                                                    scale=2 * PI), reads=[yk], writes=[hkey_out])
        layer(0, fe, 33, w1[0:33, :], Ha, "fe", "Ha")
        layer(1, Ha, 64, w23[0:64, 0:64], Hb, "Ha", "Hb")
        layer(2, Hb, 64, w23[0:64, 64:128], Ha, "Hb", "Ha")
        nb = [P.banks[6], P.banks[7]]
        P.bank_i = 0
        old_bank = P.bank

        def bank6():
            b_ = P.banks[P.bank_i]
            P.bank_i = (P.bank_i + 1) % 6
            return b_
        P.bank = bank6
        dec = [(P.f32(512), "dec%d" % k) for k in range(2)]
        rn = [P.f32(512), P.f32(512)]
        fsd = [(P.bf(512), "fsd%d" % k) for k in range(4)]
        nfs = [0]

        def filt_tile(lt, pass_):
            dc, dck = dec[lt % 2]
            P.op("scalar", lambda e: e.activation(out=dc, in_=delta, func=AF.Exp, scale=tun[:, tcol0 + lt:tcol0 + lt + 1]),
                 reads=["delta", "tun"], writes=[dck])
            ft = []
            for od in range(4):
                bank = P.bank()
                mm(P, bank[:, 0:512], Ha[0:64, lt * 128:(lt + 1) * 128], w4s[0:64, od, :], True, True, reads=["Ha", "w4s"], writes=[bank])
                f_, fk = P.scr()
                P.op("vector", lambda e, bank=bank, f_=f_: e.tensor_tensor(out=f_, in0=bank[:, 0:512], in1=dc, op=ALU.mult),
                     reads=[bank, dck], writes=[fk])
                if lt == 0 and od % 2 == 1:
                    P.op("vector", lambda e, f_=f_: e.tensor_scalar(out=f_, in0=f_, scalar1=mask0[:, 0:1], scalar2=None, op0=ALU.mult),
                         reads=[fk, "mask0"], writes=[fk])
                ft.append((f_, fk, od))
                if pass_ == 1:
                    P.op("scalar", lambda e, f_=f_: e.activation(out=f_, in_=f_, func=AF.Abs), reads=[fk], writes=[fk])
                    o_ = od // 2
                    mm(P, nb[o_][:, 0:512], S.ones, f_, lt == 0 and od % 2 == 0, lt == NT - 1 and od % 2 == 1,
                       reads=[fk, "ones"], writes=[nb[o_]])
                elif od % 2 == 1:
                    o_ = od // 2
                    (fw, fwk, _), (bw, bwk, _) = ft[od - 1], ft[od]
                    for sd, op_ in ((0, ALU.add), (1, ALU.subtract)):
                        t_, tk = P.scr()
                        ob, obk = fsd[nfs[0] % 4]
                        nfs[0] += 1
                        P.op("vector", lambda e, t_=t_, fw=fw, bw=bw, op_=op_: e.tensor_tensor(out=t_, in0=fw, in1=bw, op=op_),
                             reads=[fwk, bwk], writes=[tk])
                        P.op("vector", lambda e, t_=t_, ob=ob, o_=o_: e.tensor_tensor(out=ob, in0=t_, in1=rn[o_], op=ALU.mult),
                             reads=[tk, "rn"], writes=[obk])
                        P.dma(F_d[(o_ * 2 + sd) * n + lt * 128:(o_ * 2 + sd) * n + (lt + 1) * 128, :], ob, reads=[obk], writes=["F"])
        for lt in range(NT):
            filt_tile(lt, 1)
        for o_ in range(2):
            P.op("vector", lambda e, o_=o_: e.reciprocal(out=rn[o_], in_=nb[o_][:, 0:512]), reads=[nb[o_]], writes=["rn"])
        for lt in range(NT):
            filt_tile(lt, 2)
        P.bank = old_bank
        P.phase_end(reset=False)
        P.res = {}
        P.aoff = abase_save
        P.scratch_pool(6)
        fsd = [(P.bf(512), "fsd%d" % k) for k in range(4)]

        ztok = P.bf(NT * 256).rearrange("p (t c) -> p t c", t=NT)
        fs = P.bf(NT * 256).rearrange("p (t c) -> p t c", t=NT)
        fd = P.bf(NT * 256).rearrange("p (t c) -> p t c", t=NT)
        Am = P.bf(NK * 256).rearrange("p (k c) -> p k c", k=NK)
        Bm = P.bf(NK * 256).rearrange("p (k c) -> p k c", k=NK)
        zT = [(P.f32(n), "zT%d" % k) for k in range(2)]
        tabf = [[(P.bf(NT * 128).rearrange("p (t c) -> p t c", t=NT), "tf%d_%d" % (k, cs)) for cs in range(2)] for k in range(2)]
        tabi = [[(P.bf(WC), "ti%d_%d" % (k, cs)) for cs in range(2)] for k in range(2)]
        hcs = [(P.f32(256), "hcs%d" % k) for k in range(2)]
        NTT = n // WC

        def conv(o_, hh):
            zsrc = U if o_ == 0 else Z1
            gsrc_row0 = (1 + o_) * 4 * 128
            for c in range(2):
                ct = 2 * hh + c
                z_, zk = zT[c]
                P.dma(z_, zsrc[ct * 128:(ct + 1) * 128, c_seq0:c_seq0 + n], reads=[("U", ct), ("Z1", ct)], writes=[zk])
                for lt in range(NT):
                    bank = P.bank()
                    P.op("tensor", lambda e, bank=bank, z_=z_, lt=lt: e.transpose(bank[:, 0:128], z_[:, lt * 128:(lt + 1) * 128], ident),
                         reads=[zk, "ident"], writes=[bank], nosame=True)
                    P.op("vector", lambda e, bank=bank, lt=lt, c=c: e.tensor_copy(out=ztok[:, lt, c * 128:(c + 1) * 128], in_=bank[:, 0:128]),
                         reads=[bank], writes=["ztok"])
            P.dma(fs, F_d[(o_ * 2) * n:(o_ * 2 + 1) * n, hh * 256:(hh + 1) * 256].rearrange("(t p) c -> p t c", p=128),
                  reads=["F"], writes=["fs"])
            P.dma(fd, F_d[(o_ * 2 + 1) * n:(o_ * 2 + 2) * n, hh * 256:(hh + 1) * 256].rearrange("(t p) c -> p t c", p=128),
                  reads=["F"], writes=["fd"])

            def fwd(kt):
                (tc, tck), (ts_, tsk) = tabf[kt % 2]
                P.dma(tc, TC[0:n, kt * 128:(kt + 1) * 128].rearrange("(t p) c -> p t c", p=128), reads=["tabc"], writes=[tck])
                P.dma(ts_, TS[0:n, kt * 128:(kt + 1) * 128].rearrange("(t p) c -> p t c", p=128), reads=["tabs"], writes=[tsk])
                bz, bs, bh, bg = P.bank(), P.bank(), P.bank(), P.bank()
                for (bk_, tb_, tbk, rh, rhk) in ((bz, tc, tck, ztok, "ztok"), (bs, ts_, tsk, ztok, "ztok"),
                                                 (bh, tc, tck, fs, "fs"), (bg, ts_, tsk, fd, "fd")):
                    for lt in range(NT):
                        mm(P, bk_[:, 0:256], tb_[:, lt, :], rh[:, lt, :], lt == 0, lt == NT - 1, reads=[tbk, rhk], writes=[bk_])
                (hc, hck), (hs_, hsk) = hcs
                wcol = wk[:, wcol0 + kt:wcol0 + kt + 1]
                P.op("scalar", lambda e: e.activation(out=hc, in_=bh[:, 0:256], func=AF.Copy, scale=wcol), reads=[bh, "wk"], writes=[hck])
                P.op("scalar", lambda e: e.activation(out=hs_, in_=bg[:, 0:256], func=AF.Copy, scale=wcol), reads=[bg, "wk"], writes=[hsk])
                t1, t1k = P.scr()
                t2, t2k = P.scr()
                P.op("vector", lambda e: e.tensor_tensor(out=t1[:, 0:256], in0=bz[:, 0:256], in1=hc, op=ALU.mult), reads=[bz, hck], writes=[t1k])
                P.op("vector", lambda e: e.tensor_tensor(out=t2[:, 0:256], in0=bs[:, 0:256], in1=hs_, op=ALU.mult), reads=[bs, hsk], writes=[t2k])
                P.op("vector", lambda e: e.tensor_tensor(out=Am[:, kt, :], in0=t1[:, 0:256], in1=t2[:, 0:256], op=ALU.subtract),
                     reads=[t1k, t2k], writes=["Am"])
                t3, t3k = P.scr()
                t4, t4k = P.scr()
                P.op("vector", lambda e: e.tensor_tensor(out=t3[:, 0:256], in0=bz[:, 0:256], in1=hs_, op=ALU.mult), reads=[bz, hsk], writes=[t3k])
                P.op("vector", lambda e: e.tensor_tensor(out=t4[:, 0:256], in0=bs[:, 0:256], in1=hc, op=ALU.mult), reads=[bs, hck], writes=[t4k])
                P.op("vector", lambda e: e.tensor_tensor(out=Bm[:, kt, :], in0=t3[:, 0:256], in1=t4[:, 0:256], op=ALU.add),
                     reads=[t3k, t4k], writes=["Bm"])
            for kt in range(NK):
                fwd(kt)

            def inv(tt_):
                bo = [P.bank(), P.bank()]
                for kt in range(NK):
                    (tc, tck), (ts_, tsk) = tabi[kt % 2]
                    P.dma(tc, TC[kt * 128:(kt + 1) * 128, tt_ * WC:(tt_ + 1) * WC], reads=["tabc"], writes=[tck])
                    P.dma(ts_, TS[kt * 128:(kt + 1) * 128, tt_ * WC:(tt_ + 1) * WC], reads=["tabs"], writes=[tsk])
                    for c in range(2):
                        mm(P, bo[c][:, 0:WC], Am[:, kt, c * 128:(c + 1) * 128], tc, kt == 0, False, reads=["Am", tck], writes=[bo[c]])
                        mm(P, bo[c][:, 0:WC], Bm[:, kt, c * 128:(c + 1) * 128], ts_, False, kt == NK - 1, reads=["Bm", tsk], writes=[bo[c]])
                for c in range(2):
                    ct = 2 * hh + c
                    z_, zk = zT[c]
                    g_, gk = P.scr()
                    y_, yk = P.scr()
                    cols = slice(c_seq0 + tt_ * WC, c_seq0 + (tt_ + 1) * WC)
                    P.dma(g_[:, 0:WC], U[gsrc_row0 + ct * 128:gsrc_row0 + (ct + 1) * 128, cols], reads=[("U", 4 * (1 + o_) + ct)], writes=[gk])
                    P.op("vector", lambda e, c=c, z_=z_, y_=y_, ct=ct: e.scalar_tensor_tensor(
                        out=y_[:, 0:WC], in0=z_[:, tt_ * WC:(tt_ + 1) * WC], scalar=par[:, ct, 15 + o_:16 + o_], in1=bo[c][:, 0:WC],
                        op0=ALU.mult, op1=ALU.add), reads=[zk, "par", bo[c]], writes=[yk])
                    if o_ == 0:
                        P.op("vector", lambda e, y_=y_, g_=g_: e.tensor_tensor(out=y_[:, 0:WC], in0=y_[:, 0:WC], in1=g_[:, 0:WC], op=ALU.mult),
                             reads=[yk, gk], writes=[yk])
                        P.dma(Z1[ct * 128:(ct + 1) * 128, cols], y_[:, 0:WC], reads=[yk], writes=[("Z1w", ct)])
                    else:
                        ob, obk = fsd[nfs[0] % 4]
                        nfs[0] += 1
                        P.op("vector", lambda e, y_=y_, g_=g_, ob=ob: e.tensor_tensor(out=ob[:, 0:WC], in0=y_[:, 0:WC], in1=g_[:, 0:WC], op=ALU.mult),
                             reads=[yk, gk], writes=[obk])
                        P.dma(S.Ms[ct][:, ms_col0 + tt_ * WC:ms_col0 + (tt_ + 1) * WC], ob[:, 0:WC], reads=[obk], writes=[("Ms", ct)])
            for tt_ in range(NTT):
                inv(tt_)
        for o_ in range(2):
            for hh in range(2):
                conv(o_, hh)
            P.phase_end(reset=False)
        P.phase_end(reset=False)
        P.res = {}
        P.aoff = abase_save

    seq_stage(SEQ, fe_l, 0, 0, TABC, TABS, 33, FL, CTX, 0)
    seq_stage(CTX, fe_c, 32, 33, TABC2, TABS2, 3, FC, 0, SEQ)
    P.phase_end()
    for t in range(4):
        P.gather(S.Ms[t], S.Mg[t])
    P.phase_end()


def hy_host(I):
    f = np.float32
    out = {}
    in_b, cw, cb, bias = I["hy_in_b"][0], I["hy_conv_w"][0], I["hy_conv_b"][0], I["hy_bias"][0]
    par = np.zeros((4, 128, 4, 17), f)
    for q in range(4):
        for ct in range(4):
            ch = 512 * q + 128 * ct + np.arange(128)
            for s_ in range(3):
                par[q, :, ct, s_] = in_b[s_ * 2048 + ch]
                for k in range(3):
                    par[q, :, ct, 3 + 3 * k + s_] = cw[k, s_ * 2048 + ch]
                par[q, :, ct, 12 + s_] = cb[s_ * 2048 + ch]
            for o in range(2):
                par[q, :, ct, 15 + o] = bias[o, ch]
    out["hypar"] = par.reshape(4 * 128, 68)
    out["hyw4"] = np.ascontiguousarray(I["hy_f_w4"][0])
    out["hyw1"] = np.ascontiguousarray(I["hy_f_w1"][0])
    out["hyw23"] = np.ascontiguousarray(np.concatenate([I["hy_f_w2"][0], I["hy_f_w3"][0]], axis=1))
    out["hymlpb"] = np.ascontiguousarray(np.stack([I["hy_f_b1"][0], I["hy_f_b2"][0], I["hy_f_b3"][0], I["hy_f_freq"][0]], axis=1))

    def feats(n):
        t = np.arange(n, dtype=np.float64)
        tu = t / max(n - 1, 1)
        bands = np.linspace(1e-4, 15, 16)
        ang = (2.0 * np.pi / n) * t[:, None] * bands[None, :]
        return np.concatenate([tu[:, None], np.cos(ang), -np.sin(ang)], axis=-1).T.astype(f), tu
    feL, tuL = feats(SEQ)
    feC, tuC = feats(CTX)
    out["hyfeL"], out["hyfeC"] = np.ascontiguousarray(feL), np.ascontiguousarray(feC)
    tun = np.zeros((128, 34), f)
    tun[:, 0:32] = -tuL.reshape(32, 128).T
    tun[:, 32:34] = -tuC.reshape(2, 128).T
    out["hytun"] = tun
    deltas = np.abs(np.linspace(HY_MIN_DECAY, HY_MAX_DECAY, D)).astype(f)
    out["hydelta"] = np.ascontiguousarray(np.broadcast_to(deltas.reshape(4, 1, 512), (4, 128, 512))).reshape(512, 512)
    p = np.arange(128)
    rs = np.zeros((128, 36), f)
    wk = np.zeros((128, 36), f)
    for a in range(33):
        k = a * 128 + p
        rs[:, a] = k / 8192.0
        wk[:, a] = np.where(k > 4096, 0.0, np.where((k == 0) | (k == 4096), 1.0 / 8192, 2.0 / 8192))
    for a in range(3):
        k = a * 128 + p
        rs[:, 33 + a] = k / 512.0
        wk[:, 33 + a] = np.where(k > 256, 0.0, np.where((k == 0) | (k == 256), 1.0 / 512, 2.0 / 512))
    out["hyrowsc"], out["hywk"] = rs, wk
    out["ident"] = np.eye(128, dtype=f)
    m0 = np.ones((128, 1), f)
    m0[0, 0] = 0.0
    out["hymask0"] = m0
    return out


def mix_weight_shards(I, i, k):
    kind = KINDS[i]
    out = {}
    if kind == "hy":
        for j, a in enumerate(shard_cols(I["hy_in_w"][0], (2048, 2048, 2048), k)):
            out["hyin_%d" % j] = a
    elif kind == "gdn":
        for j, a in enumerate(shard_cols(I["gdn_in_w"][0][:, 0:12288], (4096, 4096, 4096), k)):
            out["gdin_%d" % j] = a
    return out


GD_NCH = LSEQ // 128


def gd_pchunk(dr, c):
    if dr == 0:
        return 128 * c, 1
    if c < 2:
        return 255 - 128 * c, -1
    return LSEQ - 1 - 128 * (c - 2), -1


def phase_M_GDN(P, S, i):
    L = LSEQ
    Win = S.W["gdin"]
    G = P.dram("gdG", [25 * 128, L], F32)
    par_d = P.inp("gdpar", [4 * 128, 16 * 5])
    gp_d = P.inp("gdgp", [4 * 64, 2])
    ng_d = P.inp("gdng", [128, 1])
    abw_d = P.inp("gdabw", [4 * 2048, 64])
    id_d = cinp(P, S, "ident", [128, 128])
    tri_d = P.inp("triu", [128, 128])
    mlo_d = P.inp("mlow", [128, 128])

    for c in range(2):
        P.dma(S.Hloc[c * 4096:(c + 1) * 4096, :].rearrange("(r k) n -> k r n", r=4),
              bass.AP(S.Hg.tensor, P.x_hb + c * 8192 * NTOK, [[NTOK, 1024], [1024 * NTOK, 4], [1, NTOK]]),
              writes=["Hloc"], eng="scalar")
    P.scratch_pool(4)
    wall = P.bf(16 * 1088).rearrange("p (k n) -> p k n", k=16)
    hb = [(P.bf(16 * 512).rearrange("p (k n) -> p k n", k=16), "hb%d" % n) for n in range(2)]
    stiles = [(0, 256, [(rp, 0, 64) for rp in range(4)])] + \
             [(256 + 512 * m, 512, [(m // 2, 64 + (m % 2) * 512, 512)]) for m in range(8)]
    x2q = P.cq * 1024
    passes = [
        ([(0, P.x_q, 512, 0), (0, P.x_q + 2048, 512, 512)], 0, 8, True),
        ([(1, x2q, 1024, 0)], 8, 8, False),
        ([(2, x2q, 1024, 0)], 16, 8, False),
    ]
    nh = [0]
    for (wsrc, tile0, ntile, with_ab) in passes:
        for (chunk, coff, ncols, d0) in wsrc:
            src, skey = Win.src(0, 16, coff, ncols, chunk=chunk)
            P.dma(wall[:, :, d0:d0 + ncols], src, reads=[skey], writes=["wall"], eng="scalar")
        if with_ab:
            P.dma(wall[:, :, 1024:1088], bass.AP(abw_d.tensor, P.cq * (2048 * 64), [[64, 128], [128 * 64, 16], [1, 64]]),
                  writes=["wall"], eng="gpsimd")

        def inproj_tile(c0, w, pieces, tile0=tile0, ntile=ntile, with_ab=with_ab):
            h, hk = hb[nh[0] % 2]
            nh[0] += 1
            off = 0
            for (rp, lc, pw) in pieces:
                for c in range(2):
                    P.dma(h[:, c * 8:(c + 1) * 8, off:off + pw],
                          S.Hloc[c * 4096 + rp * 1024:c * 4096 + (rp + 1) * 1024, lc:lc + pw].rearrange("(k p) n -> p k n", p=128),
                          reads=["Hloc"], writes=[hk])
                off += pw
            for j in range(ntile + (1 if with_ab else 0)):
                bank = P.bank()
                m_ = 64 if j == ntile else 128
                for kt in range(16):
                    mm(P, bank[0:m_, 0:w], wall[:, kt, j * 128:j * 128 + m_], h[:, kt, 0:w], kt == 0, kt == 15,
                       reads=["wall", hk], writes=[bank])
                st, stk = P.scr()
                P.op("scalar", lambda e, bank=bank, st=st, m_=m_: e.activation(out=st[0:m_, 0:w], in_=bank[0:m_, 0:w], func=AF.Copy),
                     reads=[bank], writes=[stk])
                row0 = (24 if j == ntile else tile0 + j) * 128
                P.dma(G[row0:row0 + m_, c0:c0 + w], st[0:m_, 0:w], reads=[stk], writes=["G"])
        for (c0, w, pieces) in stiles:
            inproj_tile(c0, w, pieces)
        P.phase_end(reset=False)
    P.phase_end()

    ones, eps_t = S.ones, S.eps
    par = P.f32(80).rearrange("p (t k) -> p t k", t=16)
    P.dma(par, bass.AP(par_d.tensor, P.cq * (128 * 80), [[80, 128], [1, 80]]).rearrange("p (t k) -> p t k", t=16),
          writes=["par"], eng="scalar")
    ua = [(P.f32(L), "ua%d" % n) for n in range(2)]
    ub = [(P.f32(L), "ub%d" % n) for n in range(2)]
    P.scratch_pool(4)
    e6 = P.f32(1)
    P.op("vector", lambda e: e.memset(e6, 1e-6), writes=["e6"])

    def dw_tile(j):
        (a, ak), (b_, bk) = ua[j % 2], ub[j % 2]
        P.dma(a, G[j * 128:(j + 1) * 128, :], reads=["G"], writes=[ak])
        P.op("vector", lambda e: e.tensor_scalar(out=b_, in0=a, scalar1=par[:, j, 2:3], scalar2=None, op0=ALU.mult),
             reads=[ak, "par"], writes=[bk])
        for (s0, s1) in ((0, CTX), (CTX, L)):
            for k in (0, 1, 3, 4):
                sh = k - 2
                if sh < 0:
                    o_sl, i_sl = slice(s0 - sh, s1), slice(s0, s1 + sh)
                else:
                    o_sl, i_sl = slice(s0, s1 - sh), slice(s0 + sh, s1)
                P.op("vector", lambda e, o_sl=o_sl, i_sl=i_sl, k=k: e.scalar_tensor_tensor(
                    out=b_[:, o_sl], in0=a[:, i_sl], scalar=par[:, j, k:k + 1], in1=b_[:, o_sl], op0=ALU.mult, op1=ALU.add),
                    reads=[ak, bk, "par"], writes=[bk])
        P.op("scalar", lambda e: e.activation(out=b_, in_=b_, func=AF.Silu), reads=[bk], writes=[bk])
        if j < 8:
            for c0 in range(0, L, 512):
                w = min(512, L - c0)
                sq, sqk = P.scr()
                rs, rsk = P.scr()
                bank = P.bank()
                P.op("scalar", lambda e, sq=sq, c0=c0, w=w: e.activation(out=sq[:, 0:w], in_=b_[:, c0:c0 + w], func=AF.Square),
                     reads=[bk], writes=[sqk])
                mm(P, bank[:, 0:w], ones, sq[:, 0:w], True, True, reads=[sqk, "ones"], writes=[bank])
                P.op("scalar", lambda e, rs=rs, bank=bank, w=w: e.activation(out=rs[:, 0:w], in_=bank[:, 0:w], func=AF.Sqrt, bias=e6, scale=1.0),
                     reads=[bank, "e6"], writes=[rsk])
                P.op("vector", lambda e, rs=rs, w=w: e.reciprocal(out=rs[:, 0:w], in_=rs[:, 0:w]), reads=[rsk], writes=[rsk])
                sc_ = (128.0 ** -0.5) if j < 4 else 1.0
                P.op("vector", lambda e, rs=rs, c0=c0, w=w, sc_=sc_: e.scalar_tensor_tensor(
                    out=b_[:, c0:c0 + w], in0=b_[:, c0:c0 + w], scalar=sc_, in1=rs[:, 0:w], op0=ALU.mult, op1=ALU.mult),
                    reads=[bk, rsk], writes=[bk])
        P.dma(G[j * 128:(j + 1) * 128, :], b_, reads=[bk], writes=["G"])
    for j in range(16):
        dw_tile(j)
    gt, gtk = ua[0]
    gp = P.f32(2)
    nA = P.f32(1)
    P.dma(gt[0:64, :], G[24 * 128:24 * 128 + 64, :], reads=["G"], writes=[gtk])
    P.dma(gp[0:64, :], bass.AP(gp_d.tensor, P.cq * 128, [[2, 64], [1, 2]]), writes=["gp"], eng="scalar")
    P.op("scalar", lambda e: e.activation(out=nA[0:16, :], in_=gp[0:16, 0:1], func=AF.Exp), reads=["gp"], writes=["nA"])
    P.op("vector", lambda e: e.tensor_scalar(out=nA[0:16, :], in0=nA[0:16, :], scalar1=-1.0, scalar2=None, op0=ALU.mult), reads=["nA"], writes=["nA"])
    P.op("scalar", lambda e: e.activation(out=gt[0:16, :], in_=gt[0:16, :], func=AF.Exp, bias=gp[0:16, 1:2], scale=1.0), reads=[gtk, "gp"], writes=[gtk])
    P.op("scalar", lambda e: e.activation(out=gt[0:16, :], in_=gt[0:16, :], func=AF.Ln, bias=ones[0:16, 0:1], scale=1.0), reads=[gtk, "ones"], writes=[gtk])
    P.op("vector", lambda e: e.tensor_scalar(out=gt[0:16, :], in0=gt[0:16, :], scalar1=nA[0:16, 0:1], scalar2=None, op0=ALU.mult), reads=[gtk, "nA"], writes=[gtk])
    P.op("scalar", lambda e: e.activation(out=gt[32:48, :], in_=gt[32:48, :], func=AF.Sigmoid), reads=[gtk], writes=[gtk])
    P.dma(G[24 * 128:24 * 128 + 64, :], gt[0:64, :], reads=[gtk], writes=["G"])
    P.phase_end()

    ones, eps_t = S.ones, S.eps
    ident, triu, mlow = P.f32(128), P.f32(128), P.f32(128)
    ngc = P.f32(1)
    P.dma(ident, id_d, writes=["ident"])
    P.dma(triu, tri_d, writes=["triu"])
    P.dma(mlow, mlo_d, writes=["mlow"])
    P.dma(ngc, ng_d, writes=["ngc"])
    GT = [P.f32(GD_NCH * 64).rearrange("p (c m) -> p c m", c=GD_NCH) for _ in range(2)]
    gsb = P.f32(L)
    P.dma(gsb[0:64, :], G[24 * 128:24 * 128 + 64, :], reads=["G"], writes=["gsb"])
    P.scratch_pool(8, width=128)

    def rev128(ap, start, step):
        return _mkap(ap, start, [[step, 128]])
    for dr in range(2):
        for c in range(GD_NCH):
            st_, sp_ = gd_pchunk(dr, c)
            bank = P.bank()
            gtmp, gtmpk = P.scr()
            P.op("vector", lambda e, gtmp=gtmp, st_=st_, sp_=sp_: e.tensor_copy(out=gtmp[0:64, :], in_=rev128(gsb, st_, sp_)[0:64, :]),
                 reads=["gsb"], writes=[gtmpk])
            P.op("tensor", lambda e, bank=bank, gtmp=gtmp: e.transpose(bank[:, 0:64], gtmp[0:64, :], ident[0:64, 0:64]),
                 reads=[gtmpk, "ident"], writes=[bank], nosame=True)
            P.op("vector", lambda e, bank=bank, dr=dr, c=c: e.tensor_copy(out=GT[dr][:, c, :], in_=bank[:, 0:64]), reads=[bank], writes=["GT"])
    oacc = P.f32(L)
    Sst = P.f32(128)
    zt = P.f32(L)
    kin, qin, vin = [(P.f32(L), nm) for nm in ("kin", "qin", "vin")]
    nt_ = [0]

    def T():
        return P.scr()

    def ev(eng, out, in_, reads, writes):
        if eng == "scalar":
            P.op("scalar", lambda e: e.activation(out=out, in_=in_, func=AF.Copy), reads=reads, writes=writes)
        else:
            P.op("vector", lambda e: e.tensor_copy(out=out, in_=in_), reads=reads, writes=writes)

    def tt(out, a, b_, op, reads, writes):
        P.op("vector", lambda e: e.tensor_tensor(out=out, in0=a, in1=b_, op=op), reads=reads, writes=writes)

    def ts(out, a, s1, op, reads, writes):
        P.op("vector", lambda e: e.tensor_scalar(out=out, in0=a, scalar1=s1, scalar2=None, op0=op), reads=reads, writes=writes)

    def mmf(out, lhsT, rhs, reads, writes, start=True, stop=True):
        mm(P, out, lhsT, rhs, start, stop, reads=reads, writes=writes)

    def pt(nm):
        return P.f32(128), nm
    names = ["kT", "qT", "vT", "ktok", "vtok", "gcc", "gb", "gcrow", "Xp", "Xm", "E", "ET", "Lm", "U0", "LpA", "UpA", "LpB", "UpB",
             "Pm", "egr", "qgT", "attnT", "kbg", "vb", "u", "wT", "vnew", "kg", "otok", "sc1", "sc2", "sc3"]
    TL = {nm: pt(nm) for nm in names}

    def chunk(h8, dr, c, first):
        st_, sp_ = gd_pchunk(dr, c)
        (kT, kTk), (qT, qTk), (vT, vTk) = TL["kT"], TL["qT"], TL["vT"]
        for (dst, dk_), (src, sk_) in (((kT, kTk), kin), ((qT, qTk), qin), ((vT, vTk), vin)):
            P.op("vector", lambda e, dst=dst, src=src: e.tensor_copy(out=dst, in_=rev128(src, st_, sp_)), reads=[sk_], writes=[dk_])
        gcol = GT[dr][:, c, dr * 8 + h8:dr * 8 + h8 + 1]
        bcol = GT[dr][:, c, 32 + dr * 8 + h8:32 + dr * 8 + h8 + 1]
        (ktok, ktokk), (vtok, vtokk) = TL["ktok"], TL["vtok"]
        for (src, sk_, dst, dk_) in ((kT, kTk, ktok, ktokk), (vT, vTk, vtok, vtokk)):
            bank = P.bank()
            P.op("tensor", lambda e, bank=bank, src=src: e.transpose(bank[:, 0:128], src, ident), reads=[sk_, "ident"], writes=[bank], nosame=True)
            ev("scalar", dst, bank[:, 0:128], [bank], [dk_])
        (gcc, gcck), (gb, gbk), (gcrow, gcrowk) = TL["gcc"], TL["gb"], TL["gcrow"]
        bank = P.bank()
        mmf(bank[:, 0:1], triu, gcol, ["triu", "GT"], [bank])
        ev("vector", gcc[:, 0:1], bank[:, 0:1], [bank], [gcck])
        ts(gb, ones, gcol, ALU.mult, ["ones", "GT"], [gbk])
        bank = P.bank()
        mmf(bank[:, 0:128], gb, triu, [gbk, "triu"], [bank])
        ev("scalar", gcrow, bank[:, 0:128], [bank], [gcrowk])
        (Xp, Xpk), (Xm, Xmk), (E, Ek), (ET, ETk) = TL["Xp"], TL["Xm"], TL["E"], TL["ET"]
        P.op("vector", lambda e: e.tensor_scalar(out=Xp, in0=gcrow, scalar1=gcc[:, 0:1], scalar2=0.0, op0=ALU.subtract, op1=ALU.max),
             reads=[gcrowk, gcck], writes=[Xpk])
        P.op("vector", lambda e: e.tensor_scalar(out=Xm, in0=gcrow, scalar1=gcc[:, 0:1], scalar2=0.0, op0=ALU.subtract, op1=ALU.min),
             reads=[gcrowk, gcck], writes=[Xmk])
        P.op("scalar", lambda e: e.activation(out=E, in_=Xp, func=AF.Exp, scale=-1.0), reads=[Xpk], writes=[Ek])
        P.op("scalar", lambda e: e.activation(out=ET, in_=Xm, func=AF.Exp), reads=[Xmk], writes=[ETk])
        tt(E, E, mlow, ALU.mult, [Ek, "mlow"], [Ek])
        tt(ET, ET, triu, ALU.mult, [ETk, "triu"], [ETk])
        (Lm, Lmk), (U0, U0k) = TL["Lm"], TL["U0"]
        bank = P.bank()
        mmf(bank[:, 0:128], kT, kT, [kTk], [bank])
        P.op("vector", lambda e, bank=bank: e.scalar_tensor_tensor(out=Lm, in0=bank[:, 0:128], scalar=bcol, in1=E, op0=ALU.mult, op1=ALU.mult),
             reads=[bank, "GT", Ek], writes=[Lmk])
        bank = P.bank()
        P.op("tensor", lambda e, bank=bank: e.transpose(bank[:, 0:128], Lm, ident), reads=[Lmk, "ident"], writes=[bank], nosame=True)
        ev("scalar", U0, bank[:, 0:128], [bank], [U0k])
        (Pm, Pmk) = TL["Pm"]
        tt(Pm, ident, U0, ALU.subtract, ["ident", U0k], [Pmk])
        Lp, Lpk, Up, Upk = Lm, Lmk, U0, U0k
        pp = [(TL["LpA"], TL["UpA"]), (TL["LpB"], TL["UpB"])]
        for s_ in range(6):
            (Ln_, Lnk), (Un_, Unk) = pp[s_ % 2]
            b1 = P.bank()
            mmf(b1[:, 0:128], Up, Lp, [Upk, Lpk], [b1])
            ev("scalar", Ln_, b1[:, 0:128], [b1], [Lnk])
            if s_ < 5:
                b2 = P.bank()
                mmf(b2[:, 0:128], Lp, Up, [Lpk, Upk], [b2])
                ev("vector", Un_, b2[:, 0:128], [b2], [Unk])
            b3 = P.bank()
            mmf(b3[:, 0:128], Ln_, Pm, [Lnk, Pmk], [b3])
            tt(Pm, Pm, b3[:, 0:128], ALU.add, [Pmk, b3], [Pmk])
            Lp, Lpk, Up, Upk = Ln_, Lnk, Un_, Unk
        (sc1, sc1k), (sc2, sc2k), (sc3, sc3k) = TL["sc1"], TL["sc2"], TL["sc3"]
        P.op("scalar", lambda e: e.activation(out=sc1[:, 0:1], in_=gcc[:, 0:1], func=AF.Exp), reads=[gcck], writes=[sc1k])
        tt(sc1[:, 1:2], sc1[:, 0:1], bcol, ALU.mult, [sc1k, "GT"], [sc1k])
        (kbg, kbgk), (vb, vbk), (u_, uk_), (wT, wTk) = TL["kbg"], TL["vb"], TL["u"], TL["wT"]
        ts(kbg, ktok, sc1[:, 1:2], ALU.mult, [ktokk, sc1k], [kbgk])
        ts(vb, vtok, bcol, ALU.mult, [vtokk, "GT"], [vbk])
        bank = P.bank()
        mmf(bank[:, 0:128], Pm, vb, [Pmk, vbk], [bank])
        ev("scalar", u_, bank[:, 0:128], [bank], [uk_])
        bank = P.bank()
        mmf(bank[:, 0:128], kbg, Pm, [kbgk, Pmk], [bank])
        ev("vector", wT, bank[:, 0:128], [bank], [wTk])
        (attnT, attnTk), (egr, egrk), (qgT, qgTk), (kg, kgk) = TL["attnT"], TL["egr"], TL["qgT"], TL["kg"]
        bank = P.bank()
        mmf(bank[:, 0:128], kT, qT, [kTk, qTk], [bank])
        tt(attnT, bank[:, 0:128], ET, ALU.mult, [bank, ETk], [attnTk])
        P.op("scalar", lambda e: e.activation(out=egr, in_=gcrow, func=AF.Exp), reads=[gcrowk], writes=[egrk])
        tt(qgT, qT, egr, ALU.mult, [qTk, egrk], [qgTk])
        P.op("scalar", lambda e: e.activation(out=sc2[:, 0:1], in_=gcc[:, 0:1], func=AF.Exp, scale=-1.0, bias=gcrow[:, 127:128]),
             reads=[gcck, gcrowk], writes=[sc2k])
        P.op("scalar", lambda e: e.activation(out=sc2[:, 1:2], in_=gcrow[:, 127:128], func=AF.Exp), reads=[gcrowk, sc2k], writes=[sc2k])
        ts(kg, ktok, sc2[:, 0:1], ALU.mult, [ktokk, sc2k], [kgk])
        (vnew, vnewk), (otok, otokk) = TL["vnew"], TL["otok"]
        bank = P.bank()
        mmf(bank[:, 0:128], wT, Sst, [wTk, "S"], [bank])
        tt(vnew, u_, bank[:, 0:128], ALU.subtract, [uk_, bank], [vnewk])
        bank = P.bank()
        mmf(bank[:, 0:128], qgT, Sst, [qgTk, "S"], [bank], True, False)
        mmf(bank[:, 0:128], attnT, vnew, [attnTk, vnewk], [bank], False, True)
        ev("scalar", otok, bank[:, 0:128], [bank], [otokk])
        bank = P.bank()
        mmf(bank[:, 0:128], kg, vnew, [kgk, vnewk], [bank])
        P.op("vector", lambda e, bank=bank: e.scalar_tensor_tensor(out=Sst, in0=Sst, scalar=sc2[:, 1:2], in1=bank[:, 0:128],
                                                                   op0=ALU.mult, op1=ALU.add), reads=["S", sc2k, bank], writes=["S"])
        bank = P.bank()
        P.op("tensor", lambda e, bank=bank: e.transpose(bank[:, 0:128], otok, ident), reads=[otokk, "ident"], writes=[bank], nosame=True)
        ov = rev128(oacc, st_, sp_)
        tt(ov, ov, bank[:, 0:128], ALU.add, ["oacc", bank], ["oacc"])

    def head(h8):
        t_ = h8 // 2
        P.dma(kin[0], G[(4 + t_) * 128:(5 + t_) * 128, :], reads=["G"], writes=["kin"])
        P.dma(qin[0], G[t_ * 128:(t_ + 1) * 128, :], reads=["G"], writes=["qin"])
        P.dma(vin[0], G[(8 + h8) * 128:(9 + h8) * 128, :], reads=["G"], writes=["vin"])
        P.dma(zt, G[(16 + h8) * 128:(17 + h8) * 128, :], reads=["G"], writes=["zt"])
        P.op("vector", lambda e: e.memset(oacc, 0.0), writes=["oacc"])
        for dr in range(2):
            P.op("vector", lambda e: e.memset(Sst, 0.0), writes=["S"])
            for c in range(GD_NCH):
                chunk(h8, dr, c, c == 0)
        ob = vin[0].bitcast(BF16)[:, 0:L]
        P.op("scalar", lambda e: e.activation(out=zt, in_=zt, func=AF.Silu), reads=["zt"], writes=["zt"])
        for c0 in range(0, L, 128):
            sq, sqk = P.scr()
            rs, rsk = P.scr()
            bank = P.bank()
            P.op("scalar", lambda e, sq=sq, c0=c0: e.activation(out=sq, in_=oacc[:, c0:c0 + 128], func=AF.Square), reads=["oacc"], writes=[sqk])
            mm(P, bank[:, 0:128], ones, sq, True, True, reads=[sqk, "ones"], writes=[bank])
            P.op("scalar", lambda e, rs=rs, bank=bank: e.activation(out=rs, in_=bank[:, 0:128], func=AF.Sqrt, bias=eps_t, scale=1.0 / 128),
                 reads=[bank, "eps"], writes=[rsk])
            P.op("vector", lambda e, rs=rs: e.reciprocal(out=rs, in_=rs), reads=[rsk], writes=[rsk])
            P.op("vector", lambda e, rs=rs, c0=c0: e.scalar_tensor_tensor(out=rs, in0=oacc[:, c0:c0 + 128], scalar=ngc[:, 0:1], in1=rs,
                                                                           op0=ALU.mult, op1=ALU.mult), reads=["oacc", rsk, "ngc"], writes=[rsk])
            P.op("vector", lambda e, rs=rs, c0=c0: e.tensor_tensor(out=ob[:, c0:c0 + 128], in0=rs, in1=zt[:, c0:c0 + 128], op=ALU.mult),
                 reads=[rsk, "zt", "vin"], writes=["ob"])
        P.phase_end(reset=False)
        P.dma(S.Ms[h8][:, 0:SEQ], ob[:, CTX:L], reads=["ob"], writes=[("Ms", h8)])
        P.dma(S.Ms[h8][:, SEQ:L], ob[:, 0:CTX], reads=["ob"], writes=[("Ms", h8)])
        P.phase_end(reset=False)
    for h8 in range(GD_HEADS):
        head(h8)
    P.phase_end()
    for t in range(8):
        P.gather(S.Ms[t], S.Mg[t])
    P.phase_end()


GD_HEADS = 8


def gdn_host(I):
    f = np.float32
    out = {}
    cw = I["gdn_conv_w"][0]
    par = np.zeros((4, 128, 16, 5), f)
    for q in range(4):
        for j in range(16):
            if j < 4:
                ch = 512 * q + 128 * j + np.arange(128)
            elif j < 8:
                ch = 2048 + 512 * q + 128 * (j - 4) + np.arange(128)
            else:
                ch = 4096 + 1024 * q + 128 * (j - 8) + np.arange(128)
            par[q, :, j, :] = cw[:, ch].T
    out["gdpar"] = par.reshape(4 * 128, 80)
    gp = np.zeros((4, 64, 2), f)
    a_log, dtb = I["gdn_a_log"][0], I["gdn_dt_bias"][0]
    for q in range(4):
        for dr in range(2):
            gp[q, dr * 8:dr * 8 + 8, 0] = a_log[dr, 8 * q:8 * q + 8]
            gp[q, dr * 8:dr * 8 + 8, 1] = dtb[dr, 8 * q:8 * q + 8]
    out["gdgp"] = gp.reshape(4 * 64, 2)
    out["gdng"] = np.ascontiguousarray(I["gdn_norm_g"][0].reshape(128, 1))
    W = I["gdn_in_w"][0]
    abw = np.zeros((4, 2048, 64), f)
    for q in range(4):
        for dr in range(2):
            abw[q, :, dr * 8:dr * 8 + 8] = W[:, 12288 + dr * 32 + 8 * q:12288 + dr * 32 + 8 * q + 8]
            abw[q, :, 32 + dr * 8:32 + dr * 8 + 8] = W[:, 12288 + 64 + dr * 32 + 8 * q:12288 + 64 + dr * 32 + 8 * q + 8]
    out["gdabw"] = abw.reshape(4 * 2048, 64)
    p = np.arange(128)
    out["ident"] = np.eye(128, dtype=f)
    out["triu"] = (p[:, None] <= p[None, :]).astype(f)
    out["mlow"] = (p[:, None] > p[None, :]).astype(f)
    return out


def build_full():
    P = Prog()
    S = setup_state(P)
    for i in range(DEPTH + 1):
        weights_for_A(P, S, i)
    for i in range(DEPTH):
        if KINDS[i] != "s5":
            weights_for_M(P, S, i)
    P.phase_end()
    for i in range(DEPTH):
        phase_A(P, S, i)
        {"s5": phase_M_S5, "hy": phase_M_HY, "gdn": phase_M_GDN}[KINDS[i]](P, S, i)
    phase_A(P, S, DEPTH)
    return P.build(), P


def host_inputs(I):
    shared = {}
    shared.update(const_host())
    shared.update(hy_host(I))
    shared.update(gdn_host(I))
    shared.update(s5_host(I, 0))
    shared.update(s5_host(I, 1))
    for i in range(DEPTH):
        shared["ada_b%d" % i] = vec_fm(I["ada_b"][i])
        shared["g1_%d" % i] = vec_fm(I["norm_g"][i, 0])
        shared["g2_%d" % i] = vec_fm(I["norm_g"][i, 1])
    shared["gf"] = vec_fm(I["final_g"])
    shared["glu_b0"] = vec_fm(I["s5_glu_b"][0])
    shared["glu_b3"] = vec_fm(I["s5_glu_b"][1])
    shared["hy_out_b"] = vec_fm(I["hy_out_b"][0])
    big = []
    for i in range(DEPTH):
        big.append(("ada%d" % i, I["ada_w"][i], (4096, 4096, 4096)))
        big.append(("wg%d" % i, I["ffn_w_gate"][i], FFN_CWS))
        big.append(("wu%d" % i, I["ffn_w_up"][i], FFN_CWS))
        big.append(("wd%d" % i, I["ffn_w_down"][i], (2048,)))
    big += [("glu0", I["s5_glu_w"][0], (4096,)), ("glu3", I["s5_glu_w"][1], (4096,)), ("hyo", I["hy_out_w"][0], (2048,)),
            ("gdo", I["gdn_out_w"][0], (2048,)), ("hyin", I["hy_in_w"][0], (2048, 2048, 2048)),
            ("gdin", I["gdn_in_w"][0][:, 0:12288], (4096, 4096, 4096))]
    maps = []
    for k in range(NCORES):
        b = k // 4
        m = dict(shared)
        m["xT0"] = fm(tok_shard(I["x"], I["ctx"], k))
        cc = np.stack([I["c"][b], I["c_ctx"]], axis=1)
        m["silu_in"] = np.ascontiguousarray(cc.reshape(16, 128, 2).transpose(1, 0, 2))
        for (nm, W, cws) in big:
            for j, a in enumerate(shard_cols(W, cws, k)):
                m["%s_%d" % (nm, j)] = a
        maps.append(m)
    return maps


def kernel(**inputs):
    I = {k: np.asarray(v) for k, v in inputs.items()}
    nc, P = get_prog("full", build_full)
    maps = host_inputs(I)
    maps = [{k: np.ascontiguousarray(m[k], dtype=np.float32) for k in P.inputs} for m in maps]
    res = run_bass_kernel_spmd(nc, maps, core_ids=list(range(NCORES))).results
    out = np.empty((B, SEQ, D), np.float32)
    for k in range(NCORES):
        b, r = k // 4, k % 4
        out[b, 1024 * r:1024 * (r + 1), :] = unfm(res[k]["yT"])
    return out
```

```python
import contextlib
import math
import numpy as np
import concourse.bass as bass
import concourse.mybir as mybir
from concourse.bass_utils import run_bass_kernel_spmd

F32 = mybir.dt.float32
BF16 = mybir.dt.bfloat16
AF = mybir.ActivationFunctionType
ALU = mybir.AluOpType

D = 2048
KT = 16
B = 2
SEQ = 4096
CTX = 256
DEPTH = 4
DFF = 5632
NTOK = 1088
LSEQ = CTX + SEQ
EPS = 1e-6
PI = math.pi
MAGIC = 12582912.0
TT = ((0, 64, 1), (64, 512, 0), (576, 512, 0))

ENGS = ("sync", "scalar", "vector", "gpsimd", "tensor")
N_DMA_SEMS = 24


SIM_FULLW = False
AW = 51200


class _Res:
    __slots__ = ("last_w", "readers")

    def __init__(self):
        self.last_w = None
        self.readers = []


class Prog:
    def __init__(self):
        self.nc = bass.Bass("TRN2", target_bir_lowering=False)
        self.stack = contextlib.ExitStack()
        self.q = {e: [] for e in ENGS}
        self.cnt = {e: 0 for e in ENGS}
        self.seen = {e: {} for e in ENGS}
        self.res = {}
        self.sem = {e: self.nc.alloc_semaphore(name="p_" + e) for e in ENGS}
        self.dsem = [self.nc.alloc_semaphore(name="d%d" % i) for i in range(N_DMA_SEMS)]
        self.dval = [0] * N_DMA_SEMS
        self.dnext = 0
        self.dnext_g = 0
        self.ccsem = self.nc.alloc_semaphore(name="cc")
        self.ccn = 0
        self.out_events = []
        self.n_t = 0
        self.banks = [self.ps([128, 512], F32, name="bank%d" % i) for i in range(8)]
        self.bank_i = 0
        self.arena = self.sb([128, AW], F32, name="arena")
        self.aoff = 0
        self.abase = 0
        self.pid = self.nc.partition_id()
        p8 = self.pid % 8
        self.cb = p8 // 4
        self.cq = p8 % 4
        self.x_mlat = (self.cb * 512) * LSEQ + self.cq * 1024
        self.x_mctx = (self.cb * 512) * LSEQ + self.cq * 64 + SEQ
        self.x_h5 = ((self.cq // 2) * 8192 + self.cb * 4096 + (self.cq % 2) * 512) * NTOK
        self.x_hb = (self.cb * 4096) * NTOK
        self.x_q = self.cq * 512
        self.D = {}
        self.inputs = {}
        self.nscr = 0

    def dram(self, name, shape, dtype=F32, kind="Internal", addr_space="Local"):
        return self.nc.dram_tensor(name, list(shape), dtype, kind=kind, addr_space=addr_space).ap()

    def inp(self, name, shape, dtype=F32):
        self.inputs[name] = (tuple(shape), dtype)
        return self.dram(name, shape, dtype, kind="ExternalInput")

    def outp(self, name, shape, dtype=F32):
        return self.dram(name, shape, dtype, kind="ExternalOutput")

    def sb(self, shape, dtype=F32, name=None):
        self.n_t += 1
        return self.stack.enter_context(
            self.nc.sbuf_tensor("s_" + (name or ("t%d" % self.n_t)), list(shape), dtype))

    def ps(self, shape, dtype=F32, name=None):
        self.n_t += 1
        return self.stack.enter_context(
            self.nc.psum_tensor("ps_" + (name or ("p%d" % self.n_t)), list(shape), dtype))

    def bank(self):
        b = self.banks[self.bank_i]
        self.bank_i = (self.bank_i + 1) % 8
        return b

    def f32(self, n):
        off = (self.aoff + 15) // 16 * 16
        self.aoff = off + n
        assert self.aoff <= AW, ("SBUF arena overflow", self.aoff)
        return self.arena[:, off:off + n]

    def bf(self, n):
        w = (n + 1) // 2
        return self.f32(w).bitcast(BF16)[:, 0:n]

    def scratch_pool(self, n, width=512):
        self.scratch = [self.f32(width) for _ in range(n)]
        self.scr_i = 0

    def scr(self):
        t = self.scratch[self.scr_i]
        self.scr_i = (self.scr_i + 1) % len(self.scratch)
        return (t, "scr%d" % self.scr_i)

    def phase_end(self, reset=True):
        for e in ENGS:
            wl = []
            for e2 in ENGS:
                if e2 != e and self.cnt[e2] > self.seen[e].get(e2, 0):
                    self.seen[e][e2] = self.cnt[e2]
                    wl.append((e2, self.cnt[e2]))
            for s_ in range(N_DMA_SEMS):
                key = ("d", s_)
                if self.dval[s_] > self.seen[e].get(key, 0):
                    self.seen[e][key] = self.dval[s_]
                    wl.append((key, self.dval[s_]))
            if self.ccn > self.seen[e].get("cc", 0):
                self.seen[e]["cc"] = self.ccn
                wl.append(("cc", self.ccn))
            self.q[e].append((wl, None, None))
        if not reset:
            return
        self.res = {}
        self.aoff = self.abase

    def _key(self, k):
        if isinstance(k, tuple):
            return (self._key(k[0]),) + tuple(k[1:])
        if isinstance(k, (str, int)):
            return k
        return "T:" + k.name

    def _r(self, key):
        key = self._key(key)
        r = self.res.get(key)
        if r is None:
            r = self.res[key] = _Res()
        return r

    def _collect(self, eng, reads, writes, nosame=False):
        waits = {}

        def need(ev):
            if ev is None:
                return
            k, v = ev
            if waits.get(k, 0) < v:
                waits[k] = v

        for r in reads:
            need(self._r(r).last_w)
        for w in writes:
            rr = self._r(w)
            need(rr.last_w)
            for ev in rr.readers:
                need(ev)
        wl = []
        for k, v in waits.items():
            if self.seen[eng].get(k, 0) >= v:
                continue
            if k == eng and nosame:
                continue
            self.seen[eng][k] = v
            wl.append((k, v))
        return wl

    def _commit(self, ev, reads, writes):
        for r in reads:
            rr = self._r(r)
            rr.readers.append(ev)
            if len(rr.readers) > 64:
                best = {}
                for (k, v) in rr.readers:
                    if best.get(k, 0) < v:
                        best[k] = v
                rr.readers = list(best.items())
        for w in writes:
            rr = self._r(w)
            rr.last_w = ev
            rr.readers = []

    def op(self, eng, fn, reads=(), writes=(), nosame=False):
        wl = self._collect(eng, reads, writes, nosame)
        self.cnt[eng] += 1
        ev = (eng, self.cnt[eng])
        self.q[eng].append((wl, fn, ("e", eng)))
        self._commit(ev, reads, writes)
        return ev

    def dma(self, out, in_, reads=(), writes=(), eng="sync", is_output=False, **kw):
        if eng == "gpsimd":
            s = 16 + self.dnext_g
            self.dnext_g = (self.dnext_g + 1) % 8
        else:
            s = self.dnext
            self.dnext = (self.dnext + 1) % 16
        wl = self._collect(eng, reads, writes)
        key = ("d", s)
        prev = self.dval[s]
        if prev > 0 and self.seen[eng].get(key, 0) < prev:
            self.seen[eng][key] = prev
            wl.append((key, prev))
        self.dval[s] = prev + 16
        ev = (key, prev + 16)

        def fn(e, out=out, in_=in_, kw=kw):
            try:
                return e.dma_start(out=out, in_=in_, **kw)
            except Exception:
                print("DMA FAILED out=", out.tensor.name, out.shape, out.ap, out.offset, " in=", in_.tensor.name, in_.shape, in_.ap, str(in_.offset)[-300:])
                raise

        self.q[eng].append((wl, fn, ("d", s)))
        self._commit(ev, reads, writes)
        if is_output:
            self.out_events.append(ev)
        return ev

    def gather(self, src, dst, reads=(), writes=()):
        eng = "gpsimd"
        wl = self._collect(eng, reads, writes)
        if self.ccn > 0 and self.seen[eng].get("cc", 0) < self.ccn:
            self.seen[eng]["cc"] = self.ccn
            wl.append(("cc", self.ccn))
        self.ccn += 1
        ev = ("cc", self.ccn)

        def fn(e, src=src, dst=dst):
            return e.collective_compute("AllGather", ALU.bypass, replica_groups=[list(range(8))],
                                        ins=[src.opt()], outs=[dst.opt()])

        self.q[eng].append((wl, fn, ("c",)))
        self._commit(ev, reads, writes)
        return ev

    def _semof(self, k):
        if isinstance(k, tuple):
            return self.dsem[k[1]]
        if k == "cc":
            return self.ccsem
        return self.sem[k]

    def build(self):
        best = {}
        for (k, v) in self.out_events:
            if best.get(k, 0) < v:
                best[k] = v
        self.q["sync"].append((list(best.items()), None, None))
        nc = self.nc
        with nc.Block() as block:
            def mk(ename):
                def body(e):
                    for (wl, fn, kind) in self.q[ename]:
                        for (k, v) in wl:
                            e.wait_ge(self._semof(k), v)
                        if fn is None:
                            continue
                        ins = fn(e)
                        if kind[0] == "e":
                            ins.then_inc(self.sem[kind[1]], 1)
                        elif kind[0] == "d":
                            ins.then_inc(self.dsem[kind[1]], 16)
                        else:
                            ins.then_inc(self.ccsem)
                return body
            block.sync(mk("sync"))
            block.scalar(mk("scalar"))
            block.vector(mk("vector"))
            block.gpsimd(mk("gpsimd"))
            block.tensor(mk("tensor"))
        self.stack.close()
        return nc


def dyn2d(t, off, rowstride, nrows, ncols):
    return bass.AP(t.tensor, off, [[rowstride, nrows], [1, ncols]])


def mm(P, out, lhsT, rhs, start, stop, reads, writes):
    return P.op("tensor", lambda e: e.matmul(out, lhsT, rhs, start=start, stop=stop),
                reads=reads, writes=writes, nosame=True)


class WRep:
    def __init__(self, P, name, R, C, cws):
        self.name, self.R, self.C = name, R, C
        self.chunks = []
        c0 = 0
        for j, cw in enumerate(cws):
            if SIM_FULLW:
                full = P.dram("%s_g%d" % (name, j), [R, cw], BF16)
                P.dma(full, P.inp("%s_%d" % (name, j), [R, cw]), writes=[("wg", name, j)], eng="gpsimd")
                self.chunks.append((c0, cw, full))
                c0 += cw
                continue
            sh = P.inp("%s_%d" % (name, j), [R // 8, cw])
            bo = P.dram("%s_b%d" % (name, j), [R // 8, cw], BF16)
            full = P.dram("%s_g%d" % (name, j), [R, cw], BF16)
            P.dma(bo, sh, writes=[("wb", name, j)], eng="gpsimd")
            P.gather(bo, full, reads=[("wb", name, j)], writes=[("wg", name, j)])
            self.chunks.append((c0, cw, full))
            c0 += cw
        assert c0 == C

    def src(self, k0, ktn, n0, ncols, chunk=None):
        if chunk is None:
            for j, (c0, cw, full) in enumerate(self.chunks):
                if c0 <= n0 and n0 + ncols <= c0 + cw:
                    chunk, n0 = j, n0 - c0
                    break
            assert chunk is not None, (self.name, n0, ncols)
            cs = slice(n0, n0 + ncols)
        else:
            cs = bass.ds(n0, ncols)
        full = self.chunks[chunk][2]
        return (full[k0 * 128:(k0 + ktn) * 128, cs].rearrange("(kt p) n -> p kt n", p=128),
                ("wg", self.name, chunk))


def shard_cols(W, cws, k):
    R = W.shape[0]
    out = []
    c0 = 0
    for cw in cws:
        out.append(np.ascontiguousarray(W[k * (R // 8):(k + 1) * (R // 8), c0:c0 + cw]))
        c0 += cw
    return out


class WStream:
    def __init__(self, P, ktn, ncols, nbuf=4, tag="w"):
        self.P = P
        self.bufs = [(P.bf(ktn * ncols).rearrange("p (k n) -> p k n", k=ktn), "%s%d" % (tag, i)) for i in range(nbuf)]
        self.i = 0

    def load(self, wrep, k0, ktn, n0, ncols, chunk=None):
        P = self.P
        buf, key = self.bufs[self.i]
        self.i = (self.i + 1) % len(self.bufs)
        src, skey = wrep.src(k0, ktn, n0, ncols, chunk)
        P.dma(buf[:, 0:ktn, 0:ncols], src, reads=[skey], writes=[key])
        return buf, key


def dense_group(P, wbuf, wkey, ktn, nchunks, act, act_res, epilogue, tiles=TT):
    for c in range(nchunks):
        for ti, (t0, tn, col) in enumerate(tiles):
            bank = P.bank()
            for kt in range(ktn):
                mm(P, bank[:, 0:tn], wbuf[:, kt, c * 128:(c + 1) * 128], act[:, kt, t0:t0 + tn],
                   kt == 0, kt == ktn - 1, reads=[wkey] + list(act_res), writes=[bank])
            epilogue(c, ti, bank)


def rmsnorm_rstd(P, xT, ones, rstd, eps_t):
    for ti, (t0, tn, col) in enumerate(TT):
        bank = P.bank()
        for kt in range(KT):
            s, sk = P.scr()
            P.op("scalar", lambda e, s=s, kt=kt, t0=t0, tn=tn: e.activation(
                out=s[:, 0:tn], in_=xT[:, kt, t0:t0 + tn], func=AF.Square), reads=[("x", kt)], writes=[sk])
            mm(P, bank[:, 0:tn], ones, s[:, 0:tn], kt == 0, kt == KT - 1, reads=[sk, "ones"], writes=[bank])
        P.op("scalar", lambda e, bank=bank, t0=t0, tn=tn: e.activation(
            out=rstd[:, t0:t0 + tn], in_=bank[:, 0:tn], func=AF.Sqrt, bias=eps_t, scale=1.0 / D),
            reads=[bank, "eps"], writes=[("rstd", ti)])
        P.op("vector", lambda e, t0=t0, tn=tn: e.reciprocal(out=rstd[:, t0:t0 + tn], in_=rstd[:, t0:t0 + tn]),
             reads=[("rstd", ti)], writes=[("rstd", ti)])


KINDS = ("s5", "hy", "gdn", "s5")
S5_TILES = (0, 1, 2, 3)
S5_POOL = False
FFN_CWS = (2816, 2816)


class State:
    pass


def cinp(P, S, name, shape):
    if name not in S.cin:
        S.cin[name] = P.inp(name, shape)
    return S.cin[name]


def setup_state(P):
    S = State()
    S.cin = {}
    S.ones = P.f32(128)
    S.eps = P.f32(1)
    S.mod = [P.f32(192).rearrange("p (f c) -> p f c", c=2) for _ in range(2)]
    ones_d = P.inp("ones", [128, 128])
    P.dma(S.ones, ones_d, writes=["ones"])
    P.op("vector", lambda e: e.memset(S.eps, EPS), writes=["eps"])
    P.abase = P.aoff
    S.Xs = P.dram("Xs", [128, KT, NTOK], F32)
    S.Hs = P.dram("Hs", [D, NTOK], BF16)
    S.Hg = P.dram("Hg", [2 * 8 * 1024, NTOK], BF16)
    S.Ms = [P.dram("Ms%d" % t, [128, LSEQ], BF16) for t in range(8)]
    S.Mg = [P.dram("Mg%d" % t, [8 * 128, LSEQ], BF16) for t in range(8)]
    S.Mloc = P.dram("Mloc", [32 * 128, NTOK], BF16)
    S.Hloc = P.dram("Hloc", [2 * 4 * 1024, NTOK], BF16)
    S.W = {}
    return S


def weights_for_A(P, S, i):
    if i > 0:
        prev = KINDS[i - 1]
        li = i - 1
        if prev == "s5":
            S.W["mix%d" % li] = WRep(P, "glu%d" % li, D, 2 * D, (4096,))
        elif prev == "hy":
            S.W["mix%d" % li] = WRep(P, "hyo", D, D, (2048,))
        else:
            S.W["mix%d" % li] = WRep(P, "gdo", 2 * D, D, (2048,))
        S.W["wg%d" % li] = WRep(P, "wg%d" % li, D, DFF, FFN_CWS)
        S.W["wu%d" % li] = WRep(P, "wu%d" % li, D, DFF, FFN_CWS)
        S.W["wd%d" % li] = WRep(P, "wd%d" % li, DFF, D, (2048,))
    if i < DEPTH:
        S.W["ada%d" % i] = WRep(P, "ada%d" % i, D, 6 * D, (4096, 4096, 4096))


def phase_A(P, S, i):
    prev = KINDS[i - 1] if i > 0 else None
    li = i - 1
    final = (i == DEPTH)
    xT = P.f32(KT * NTOK).rearrange("p (k t) -> p k t", k=KT)
    bufA = P.bf(28 * NTOK)
    mT = bufA[:, 0:16 * NTOK].rearrange("p (k t) -> p k t", k=16)
    aT = bufA[:, 16 * NTOK:28 * NTOK].rearrange("p (k t) -> p k t", k=12)
    rstd = P.f32(NTOK)
    P.scratch_pool(6)
    ws = WStream(P, 16, 256, nbuf=4, tag="w")
    ones, eps_t = S.ones, S.eps
    XR = lambda kt: ("x", kt)
    MRES = [("m", kt) for kt in range(16)]
    if i == 0:
        x_src = P.inp("xT0", [128, KT, NTOK])
    else:
        x_src = S.Xs
    for kt in range(KT):
        P.dma(xT[:, kt, :], x_src[:, kt, :], reads=["Xs"], writes=[XR(kt)])

    if prev is not None:
        modp = S.mod[li % 2]
        gt1 = lambda j, col: modp[:, 2 * 16 + j, col:col + 1]
        sh2 = lambda j, col: modp[:, 3 * 16 + j, col:col + 1]
        gt2 = lambda j, col: modp[:, 5 * 16 + j, col:col + 1]
        km = 32 if prev == "gdn" else 16
        tpc = km // 4
        Wm = S.W["mix%d" % li]

        for t in range(tpc):
            dst = S.Mloc[0:km * 128, :].rearrange("(q t p) n -> p q t n", t=tpc, p=128)[:, :, t, :]
            P.dma(dst[:, :, 64:NTOK], bass.AP(S.Mg[t].tensor, P.x_mlat, [[LSEQ, 128], [128 * LSEQ, 4], [1, 1024]]),
                  reads=[("Mg", t)], writes=["Mloc"])
            P.dma(dst[:, :, 0:64], bass.AP(S.Mg[t].tensor, P.x_mctx, [[LSEQ, 128], [128 * LSEQ, 4], [1, 64]]),
                  reads=[("Mg", t)], writes=["Mloc"])

        def load_m(half):
            P.dma(mT, S.Mloc[half * 2048:(half + 1) * 2048, :].rearrange("(k p) n -> p k n", p=128),
                  reads=["Mloc"], writes=MRES)

        if prev == "s5":
            load_m(0)
            bm = P.f32(32)
            P.dma(bm, P.inp("glu_b%d" % li, [128, 32]), writes=["bm"])
            for j in range(KT):
                wv, wvk = ws.load(Wm, 0, 16, j * 128, 128)
                wg, wgk = ws.load(Wm, 0, 16, D + j * 128, 128)
                for ti, (t0, tn, col) in enumerate(TT):
                    bv = P.bank()
                    bg = P.bank()
                    for kt in range(16):
                        mm(P, bg[:, 0:tn], wg[:, kt, 0:128], mT[:, kt, t0:t0 + tn], kt == 0, kt == 15,
                           reads=[wgk] + MRES, writes=[bg])
                    for kt in range(16):
                        mm(P, bv[:, 0:tn], wv[:, kt, 0:128], mT[:, kt, t0:t0 + tn], kt == 0, kt == 15,
                           reads=[wvk] + MRES, writes=[bv])
                    sg, sgk = P.scr()
                    tm, tmk = P.scr()
                    P.op("scalar", lambda e, sg=sg, bg=bg, j=j, tn=tn: e.activation(
                        out=sg[:, 0:tn], in_=bg[:, 0:tn], func=AF.Sigmoid, bias=bm[:, 16 + j:17 + j], scale=1.0),
                        reads=[bg, "bm"], writes=[sgk])
                    P.op("vector", lambda e, tm=tm, bv=bv, sg=sg, j=j, tn=tn: e.scalar_tensor_tensor(
                        out=tm[:, 0:tn], in0=bv[:, 0:tn], scalar=bm[:, j:j + 1], in1=sg[:, 0:tn],
                        op0=ALU.add, op1=ALU.mult), reads=[bv, sgk, "bm"], writes=[tmk])
                    P.op("vector", lambda e, tm=tm, j=j, t0=t0, tn=tn, col=col: e.scalar_tensor_tensor(
                        out=xT[:, j, t0:t0 + tn], in0=tm[:, 0:tn], scalar=gt1(j, col), in1=xT[:, j, t0:t0 + tn],
                        op0=ALU.mult, op1=ALU.add), reads=[tmk, ("mod", li % 2), XR(j)], writes=[XR(j)])
        else:
            if prev == "hy":
                bm = P.f32(16)
                P.dma(bm, P.inp("hy_out_b", [128, 16]), writes=["bm"])
            for half in range(km // 16):
                load_m(half)
                for jg in range(KT // 2):
                    wb, wbk = ws.load(Wm, half * 16, 16, jg * 256, 256)

                    def epi(c, ti, bank, jg=jg):
                        j = jg * 2 + c
                        t0, tn, col = TT[ti]
                        if prev == "hy":
                            tm, tmk = P.scr()
                            P.op("vector", lambda e: e.tensor_scalar(
                                out=tm[:, 0:tn], in0=bank[:, 0:tn], scalar1=bm[:, j:j + 1], scalar2=None, op0=ALU.add),
                                reads=[bank, "bm"], writes=[tmk])
                            src, srck = tm, tmk
                        else:
                            src, srck = bank, bank
                        P.op("vector", lambda e: e.scalar_tensor_tensor(
                            out=xT[:, j, t0:t0 + tn], in0=src[:, 0:tn], scalar=gt1(j, col), in1=xT[:, j, t0:t0 + tn],
                            op0=ALU.mult, op1=ALU.add), reads=[srck, ("mod", li % 2), XR(j)], writes=[XR(j)])
                    dense_group(P, wb, wbk, 16, 2, mT, MRES, epi)

        g2 = P.f32(KT)
        P.dma(g2, P.inp("g2_%d" % li, [128, KT]), writes=["g2"])
        coef2 = P.f32(2 * KT).rearrange("p (k c) -> p k c", c=2)
        for col in range(2):
            P.op("vector", lambda e, col=col: e.scalar_tensor_tensor(
                out=coef2[:, :, col], in0=modp[:, 64:80, col], scalar=1.0, in1=g2,
                op0=ALU.add, op1=ALU.mult), reads=[("mod", li % 2), "g2"], writes=["coef2"])
        Wg, Wu, Wd = S.W["wg%d" % li], S.W["wu%d" % li], S.W["wd%d" % li]
        rmsnorm_rstd(P, xT, ones, rstd, eps_t)
        h2 = mT
        for kt in range(KT):
            for ti, (t0, tn, col) in enumerate(TT):
                tm, tmk = P.scr()
                P.op("vector", lambda e, tm=tm, kt=kt, t0=t0, tn=tn: e.tensor_tensor(
                    out=tm[:, 0:tn], in0=xT[:, kt, t0:t0 + tn], in1=rstd[:, t0:t0 + tn], op=ALU.mult),
                    reads=[XR(kt), ("rstd", ti)], writes=[tmk])
                P.op("vector", lambda e, tm=tm, kt=kt, t0=t0, tn=tn, col=col: e.tensor_scalar(
                    out=h2[:, kt, t0:t0 + tn], in0=tm[:, 0:tn], scalar1=coef2[:, kt, col:col + 1],
                    scalar2=sh2(kt, col), op0=ALU.mult, op1=ALU.add),
                    reads=[tmk, "coef2", ("mod", li % 2)], writes=[("m", kt)])
        passes = ((0, 12), (12, 12), (24, 12), (36, 8))
        for (f0, nf) in passes:
            for g in range(nf // 2):
                fc0 = f0 + g * 2
                wgb, wgk = ws.load(Wg, 0, 16, fc0 * 128, 256)
                wub, wuk = ws.load(Wu, 0, 16, fc0 * 128, 256)
                for c in range(2):
                    a_idx = g * 2 + c
                    for ti, (t0, tn, col) in enumerate(TT):
                        bg = P.bank()
                        bu = P.bank()
                        for kt in range(16):
                            mm(P, bg[:, 0:tn], wgb[:, kt, c * 128:(c + 1) * 128], h2[:, kt, t0:t0 + tn],
                               kt == 0, kt == 15, reads=[wgk] + MRES, writes=[bg])
                        for kt in range(16):
                            mm(P, bu[:, 0:tn], wub[:, kt, c * 128:(c + 1) * 128], h2[:, kt, t0:t0 + tn],
                               kt == 0, kt == 15, reads=[wuk] + MRES, writes=[bu])
                        sg, sgk = P.scr()
                        P.op("scalar", lambda e, sg=sg, bg=bg, tn=tn: e.activation(
                            out=sg[:, 0:tn], in_=bg[:, 0:tn], func=AF.Silu), reads=[bg], writes=[sgk])
                        P.op("vector", lambda e, sg=sg, bu=bu, a_idx=a_idx, t0=t0, tn=tn: e.tensor_tensor(
                            out=aT[:, a_idx, t0:t0 + tn], in0=sg[:, 0:tn], in1=bu[:, 0:tn], op=ALU.mult),
                            reads=[sgk, bu], writes=[("a", a_idx)])
            ARES = [("a", k) for k in range(nf)]
            for jg in range(KT // 2):
                wdb, wdk = ws.load(Wd, f0, nf, jg * 256, 256)

                def epi2(c, ti, bank, jg=jg):
                    j = jg * 2 + c
                    t0, tn, col = TT[ti]
                    P.op("vector", lambda e: e.scalar_tensor_tensor(
                        out=xT[:, j, t0:t0 + tn], in0=bank[:, 0:tn], scalar=gt2(j, col), in1=xT[:, j, t0:t0 + tn],
                        op0=ALU.mult, op1=ALU.add), reads=[bank, ("mod", li % 2), XR(j)], writes=[XR(j)])
                dense_group(P, wdb, wdk, nf, 2, aT, ARES, epi2)

    if not final:
        for kt in range(KT):
            P.dma(S.Xs[:, kt, :], xT[:, kt, :], reads=[XR(kt)], writes=["Xs"])
        adaW = S.W["ada%d" % i]
        adab = P.f32(96)
        scT = P.f32(2 * KT).rearrange("p (k c) -> p k c", c=2)
        scTb = P.bf(2 * KT).rearrange("p (k c) -> p k c", c=2)
        g1 = P.f32(KT)
        modn = S.mod[i % 2]
        P.dma(adab, P.inp("ada_b%d" % i, [128, 96]), writes=["adab"])
        if i == 0:
            S.silu_in = P.inp("silu_in", [128, KT, 2])
        P.dma(scT, S.silu_in, writes=["scT"])
        P.dma(g1, P.inp("g1_%d" % i, [128, KT]), writes=["g1"])
        P.op("scalar", lambda e: e.activation(out=scTb, in_=scT, func=AF.Silu), reads=["scT"], writes=["scTb"])
        for gi in range(48):
            ab, abk = ws.load(adaW, 0, 16, gi * 256, 256)
            for c in range(2):
                fc = gi * 2 + c
                bank = P.bank()
                for kt in range(KT):
                    mm(P, bank[:, 0:2], ab[:, kt, c * 128:(c + 1) * 128], scTb[:, kt, :], kt == 0, kt == KT - 1,
                       reads=[abk, "scTb"], writes=[bank])
                P.op("vector", lambda e, bank=bank, fc=fc: e.tensor_scalar(
                    out=modn[:, fc, :], in0=bank[:, 0:2], scalar1=adab[:, fc:fc + 1], scalar2=None, op0=ALU.add),
                    reads=[bank, "adab"], writes=[("mod", i % 2)])
        coef1 = P.f32(2 * KT).rearrange("p (k c) -> p k c", c=2)
        for col in range(2):
            P.op("vector", lambda e, col=col: e.scalar_tensor_tensor(
                out=coef1[:, :, col], in0=modn[:, 16:32, col], scalar=1.0, in1=g1,
                op0=ALU.add, op1=ALU.mult), reads=[("mod", i % 2), "g1"], writes=["coef1"])
        rmsnorm_rstd(P, xT, ones, rstd, eps_t)
        hsb = [(P.bf(512), "hsb%d" % n) for n in range(3)]
        nh = 0
        for kt in range(KT):
            for ti, (t0, tn, col) in enumerate(TT):
                tm, tmk = P.scr()
                hs, hsk = hsb[nh % 3]
                nh += 1
                P.op("vector", lambda e, tm=tm, kt=kt, t0=t0, tn=tn: e.tensor_tensor(
                    out=tm[:, 0:tn], in0=xT[:, kt, t0:t0 + tn], in1=rstd[:, t0:t0 + tn], op=ALU.mult),
                    reads=[XR(kt), ("rstd", ti)], writes=[tmk])
                P.op("vector", lambda e, tm=tm, hs=hs, kt=kt, t0=t0, tn=tn, col=col: e.tensor_scalar(
                    out=hs[:, 0:tn], in0=tm[:, 0:tn], scalar1=coef1[:, kt, col:col + 1],
                    scalar2=modn[:, kt, col:col + 1], op0=ALU.mult, op1=ALU.add),
                    reads=[tmk, "coef1", ("mod", i % 2)], writes=[hsk])
                P.dma(S.Hs[kt * 128:(kt + 1) * 128, t0:t0 + tn], hs[:, 0:tn], reads=[hsk, ("Hg", kt // 8)],
                      writes=[("Hs", kt // 8)])
        P.phase_end()
        for c in range(2):
            P.gather(S.Hs[c * 1024:(c + 1) * 1024, :], S.Hg[c * 8192:(c + 1) * 8192, :])
    else:
        gf = P.f32(KT)
        P.dma(gf, P.inp("gf", [128, KT]), writes=["gf"])
        yT_out = P.outp("yT", [128, KT, SEQ // 4])
        rmsnorm_rstd(P, xT, ones, rstd, eps_t)
        for kt in range(KT):
            for ti, (t0, tn, col) in enumerate(TT):
                if ti == 0:
                    continue
                hs, hsk = P.scr()
                P.op("vector", lambda e, hs=hs, kt=kt, t0=t0, tn=tn: e.scalar_tensor_tensor(
                    out=hs[:, 0:tn], in0=xT[:, kt, t0:t0 + tn], scalar=gf[:, kt:kt + 1], in1=rstd[:, t0:t0 + tn],
                    op0=ALU.mult, op1=ALU.mult), reads=[XR(kt), ("rstd", ti), "gf"], writes=[hsk])
                P.dma(yT_out[:, kt, t0 - 64:t0 - 64 + tn], hs[:, 0:tn], reads=[hsk], is_output=True)
    P.phase_end()

NCORES = 8


def fm(a):
    ntok, C = a.shape
    return np.ascontiguousarray(a.T.reshape(C // 128, 128, ntok).transpose(1, 0, 2))


def unfm(a):
    p, kt, ntok = a.shape
    return np.ascontiguousarray(a.transpose(1, 0, 2).reshape(kt * 128, ntok).T)


def vec_fm(v):
    return np.ascontiguousarray(v.reshape(-1, 128).T)


def tok_shard(lat, ctx, k):
    b, r = k // 4, k % 4
    return np.concatenate([ctx[b, 64 * r:64 * r + 64], lat[b, 1024 * r:1024 * r + 1024]], axis=0)


def tok_unshard(parts, C):
    lat = np.empty((B, SEQ, C), np.float32)
    ctx = np.empty((B, CTX, C), np.float32)
    for k in range(NCORES):
        b, r = k // 4, k % 4
        ctx[b, 64 * r:64 * r + 64] = parts[k][:64]
        lat[b, 1024 * r:1024 * r + 1024] = parts[k][64:]
    return lat, ctx


_PROGS = {}


def get_prog(key, builder):
    return builder()


def launch(nc, in_maps):
    res = run_bass_kernel_spmd(nc, in_maps, core_ids=list(range(len(in_maps))))
    return res.results


def dump(P, name, ap, shape, dtype, reads):
    o = P.outp(name, shape, dtype)
    P.dma(o, ap, reads=reads, is_output=True)


def build_test_A0():
    P = Prog()
    S = setup_state(P)
    weights_for_A(P, S, 0)
    P.phase_end()
    phase_A(P, S, 0)
    dump(P, "o_Hg", S.Hg, [2 * 8 * 1024, NTOK], BF16, [("Hg", 0), ("Hg", 1)])
    dump(P, "o_Xs", S.Xs, [128, KT, NTOK], F32, ["Xs"])
    return P.build(), P


def build_test_A1(i=1):
    P = Prog()
    S = setup_state(P)
    li = i - 1
    P.dma(S.Xs, P.inp("inj_Xs", [128, KT, NTOK]), writes=["Xs"])
    P.dma(S.mod[li % 2], P.inp("inj_mod", [128, 96, 2]), writes=[("mod", li % 2)])
    nt = 8 if KINDS[li] == "gdn" else 4
    inj = P.inp("inj_Ms", [nt * 128, LSEQ], BF16)
    for t in range(nt):
        P.dma(S.Ms[t], inj[t * 128:(t + 1) * 128, :], writes=[("Ms", t)])
        P.gather(S.Ms[t], S.Mg[t], reads=[("Ms", t)], writes=[("Mg", t)])
    S.silu_in = P.inp("silu_in", [128, KT, 2])
    weights_for_A(P, S, i)
    P.phase_end()
    phase_A(P, S, i)
    if i < DEPTH:
        dump(P, "o_Hs", S.Hs, [D, NTOK], BF16, [("Hs", 0), ("Hs", 1)])
        dump(P, "o_Xs", S.Xs, [128, KT, NTOK], F32, ["Xs"])
    return P.build(), P


def _mkap(base, off, dims):
    return bass.AP(base.tensor, base.offset + off, [[base.ap[0][0], 128]] + [list(d_) for d_ in dims])


def s5_views(base, colmajor):
    fw, bw = [], []
    fw.append((_mkap(base, 0, [[1, 256]]), 0, 256, False))
    bw.append((_mkap(base, 255, [[-1, 256]]), 0, 256, False))
    for m in range(8):
        t0 = 256 + 512 * m
        if not colmajor:
            fw.append((_mkap(base, 256 + 512 * m, [[1, 512]]), t0, 512, False))
            bw.append((_mkap(base, 256 + 4095 - 512 * m, [[-1, 512]]), t0, 512, False))
        else:
            fw.append((_mkap(base, 256 + 8 * m, [[1, 8], [64, 64]]), t0, 512, True))
            bw.append((_mkap(base, 256 + 63 * 64 + 63 - 8 * m, [[-1, 8], [-64, 64]]), t0, 512, True))
    return fw, bw


def phase_M_S5(P, S, i):
    j = i // 3
    colmajor = (j % 2) == 1
    L = LSEQ
    par_d = P.inp("s5par%d" % j, [16 * 128, 1216])
    S.iota_d = cinp(P, S, "iota", [128, L])
    S.rowmask_d = cinp(P, S, "rowmask", [128, 8])
    S.sgn_d = cinp(P, S, "sgn1", [128, 1])
    iota = P.f32(L)
    rowmask = P.f32(8)
    sgn1 = P.f32(1)
    negmagic = P.f32(1)
    halfpi = P.f32(1)
    P.dma(iota, S.iota_d, writes=["iota"])
    P.dma(rowmask, S.rowmask_d, writes=["rowmask"])
    P.dma(sgn1, S.sgn_d, writes=["sgn1"])
    P.op("vector", lambda e: e.memset(negmagic, -MAGIC), writes=["negmagic"])
    P.op("vector", lambda e: e.memset(halfpi, PI / 2), writes=["halfpi"])
    u_all = P.bf(4 * L).rearrange("p (t l) -> p t l", t=4)
    yacc = P.f32(L)
    COS, SIN, TA, TB = P.f32(L), P.f32(L), P.f32(L), P.f32(L)
    par_all = P.f32(4 * 1216).rearrange("p (t w) -> p t w", t=4)
    blk_all = par_all[:, :, 0:640].rearrange("p t (d f s) -> p t d f s", d=2, f=5)
    st_all = par_all[:, :, 640:688].rearrange("p t (d g f) -> p t d g f", d=2, g=8)
    ct_all = par_all[:, :, 688:1200].rearrange("p t (d a m) -> p t d a m", d=2, a=2)
    d_all = par_all[:, :, 1200]
    nsm = [0]

    def sm():
        nsm[0] += 1
        return P.f32(128).rearrange("p (d s) -> p d s", d=2), "sm%d" % nsm[0]
    LA = P.f32(256).rearrange("p (d m) -> p d m", d=2)
    LB = P.f32(256).rearrange("p (d m) -> p d m", d=2)
    C1f = P.f32(256).rearrange("p (d m) -> p d m", d=2)
    C2f = P.f32(256).rearrange("p (d m) -> p d m", d=2)
    thp = P.f32(16)
    rrc = P.f32(16)
    stmp = [P.f32(16) for _ in range(3)]
    gw = [[(P.bf(128), "gw%d_%d" % (n, a)) for a in range(4)] for n in range(2)]
    rrt = [(P.f32(512), "rrt%d" % n) for n in range(2)]
    T1 = [(P.f32(512), "T1_0")] * 2
    T2 = [(P.f32(512), "T2_0")] * 2
    Wt = [(P.f32(512), "Wt_0")] * 2
    Gt = [(P.f32(512), "Gt_0")] * 2
    C1t = [(P.bf(512), "C1t_%d" % n) for n in range(2)]
    S1t = [(P.bf(512), "S1t_%d" % n) for n in range(2)]
    tmp_bl = [sm() for _ in range(16)]

    def vop(fn, reads, writes):
        return P.op("vector", fn, reads=reads, writes=writes)

    def aop(fn, reads, writes):
        return P.op("scalar", fn, reads=reads, writes=writes)

    def tt(out, a, b_, op, reads, writes):
        return vop(lambda e: e.tensor_tensor(out=out, in0=a, in1=b_, op=op), reads, writes)

    def ptt(out, a, b_, op, reads, writes):
        eng = "gpsimd" if S5_POOL else "vector"
        return P.op(eng, lambda e: e.tensor_tensor(out=out, in0=a, in1=b_, op=op), reads=reads, writes=writes)

    P.dma(S.Hloc[0:2048, :].rearrange("(r c) n -> c r n", r=4),
          bass.AP(S.Hg.tensor, P.x_h5, [[NTOK, 512], [1024 * NTOK, 4], [1, NTOK]]),
          reads=[("Hg", 0), ("Hg", 1)], writes=["Hloc"])
    for rp in range(4):
        srcv = S.Hloc[rp * 512:(rp + 1) * 512, :].rearrange("(t p) n -> p t n", p=128)
        P.dma(u_all[:, :, CTX + 1024 * rp:CTX + 1024 * rp + 1024], srcv[:, :, 64:NTOK], reads=["Hloc"], writes=["u_all"])
        P.dma(u_all[:, :, 64 * rp:64 * rp + 64], srcv[:, :, 0:64], reads=["Hloc"], writes=["u_all"])
    P.dma(par_all, bass.AP(par_d.tensor, P.x_q * 1216, [[1216, 128], [128 * 1216, 4], [1, 1216]]),
          writes=["blk", "stt", "ct", "dcol"])
    def s5_tile(t):
        ut, uk = u_all[:, t, :], "u_all"
        blk, stt, ct, dcol = blk_all[:, t], st_all[:, t], ct_all[:, t], d_all[:, t:t + 1]
        a_re, a_im, ldt, bTr, bTi = (blk[:, :, f, :] for f in range(5))
        (dt, dtk), (xr, xrk), (th, thk), (rr_, rrk), (y, yk), (k1, k1k), (fr, frk), (sn, snk), (cs, csk), (lbr, lbrk), \
            (lbi, lbik), (den, denk), (cre, crek), (cim, cimk), (q1, q1k), (q2, q2k) = tmp_bl
        aop(lambda e: e.activation(out=dt, in_=ldt, func=AF.Exp), ["blk"], [dtk])
        tt(xr, a_re, dt, ALU.mult, ["blk", dtk], [xrk])
        tt(th, a_im, dt, ALU.mult, ["blk", dtk], [thk])
        aop(lambda e: e.activation(out=rr_, in_=xr, func=AF.Exp), [xrk], [rrk])
        vop(lambda e: e.tensor_scalar(out=y, in0=th, scalar1=1.0 / (2 * PI), scalar2=None, op0=ALU.mult), [thk], [yk])
        vop(lambda e: e.tensor_scalar(out=k1, in0=y, scalar1=MAGIC, scalar2=None, op0=ALU.add), [yk], [k1k])
        vop(lambda e: e.tensor_scalar(out=k1, in0=k1, scalar1=-MAGIC, scalar2=None, op0=ALU.add), [k1k], [k1k])
        tt(fr, y, k1, ALU.subtract, [yk, k1k], [frk])
        aop(lambda e: e.activation(out=sn, in_=fr, func=AF.Sin, scale=2 * PI), [frk], [snk])
        aop(lambda e: e.activation(out=fr, in_=fr, func=AF.Abs), [frk, snk], [frk])
        aop(lambda e: e.activation(out=cs, in_=fr, func=AF.Sin, scale=-2 * PI, bias=halfpi), [frk, "halfpi"], [csk])
        tt(lbr, rr_, cs, ALU.mult, [rrk, csk], [lbrk])
        vop(lambda e: e.tensor_scalar(out=lbr, in0=lbr, scalar1=-1.0, scalar2=None, op0=ALU.add), [lbrk], [lbrk])
        tt(lbi, rr_, sn, ALU.mult, [rrk, snk], [lbik])
        tt(den, a_re, a_re, ALU.mult, ["blk"], [denk])
        tt(q1, a_im, a_im, ALU.mult, ["blk"], [q1k])
        tt(den, den, q1, ALU.add, [denk, q1k], [denk])
        vop(lambda e: e.reciprocal(out=den, in_=den), [denk], [denk])
        tt(cre, lbr, a_re, ALU.mult, [lbrk, "blk"], [crek])
        tt(q1, lbi, a_im, ALU.mult, [lbik, "blk"], [q1k])
        tt(cre, cre, q1, ALU.add, [crek, q1k], [crek])
        tt(cre, cre, den, ALU.mult, [crek, denk], [crek])
        tt(cim, lbi, a_re, ALU.mult, [lbik, "blk"], [cimk])
        tt(q1, lbr, a_im, ALU.mult, [lbrk, "blk"], [q1k])
        tt(cim, cim, q1, ALU.subtract, [cimk, q1k], [cimk])
        tt(cim, cim, den, ALU.mult, [cimk, denk], [cimk])
        tt(q1, cre, bTr, ALU.mult, [crek, "blk"], [q1k])
        tt(q2, cim, bTi, ALU.mult, [cimk, "blk"], [q2k])
        tt(LA[:, :, 0:64], q1, q2, ALU.subtract, [q1k, q2k], ["LA"])
        vop(lambda e: e.tensor_scalar(out=LB[:, :, 64:128], in0=LA[:, :, 0:64], scalar1=-1.0, scalar2=None, op0=ALU.mult),
            ["LA"], ["LB"])
        tt(q1, cre, bTi, ALU.mult, [crek, "blk"], [q1k])
        tt(q2, cim, bTr, ALU.mult, [cimk, "blk"], [q2k])
        tt(LA[:, :, 64:128], q1, q2, ALU.add, [q1k, q2k], ["LA"])
        vop(lambda e: e.tensor_copy(out=LB[:, :, 0:64], in_=LA[:, :, 64:128]), ["LA"], ["LB"])
        s0, s1_, s2 = (x_.rearrange("p (d g) -> p d g", d=2) for x_ in stmp)
        thp3 = thp.rearrange("p (d g) -> p d g", d=2)
        rrc3 = rrc.rearrange("p (d g) -> p d g", d=2)
        aop(lambda e: e.activation(out=s0, in_=stt[:, :, :, 2], func=AF.Exp), ["stt"], ["s0"])
        tt(s1_, stt[:, :, :, 1], s0, ALU.mult, ["stt", "s0"], ["s1"])
        vop(lambda e: e.tensor_scalar(out=thp3, in0=s1_, scalar1=1.0 / (2 * PI), scalar2=None, op0=ALU.mult), ["s1"], ["thp"])
        tt(s2, stt[:, :, :, 0], s0, ALU.mult, ["stt", "s0"], ["s2"])
        aop(lambda e: e.activation(out=rrc3, in_=s2, func=AF.Exp), ["s2"], ["rrc"])
        vop(lambda e: e.tensor_scalar(out=C1f, in0=ct[:, :, 0, :], scalar1=sgn1[:, 0:1], scalar2=None, op0=ALU.mult),
            ["ct", "sgn1"], ["C1f"])
        vop(lambda e: e.tensor_scalar(out=C2f, in0=ct[:, :, 1, :], scalar1=-1.0, scalar2=None, op0=ALU.mult), ["ct"], ["C2f"])
        vop(lambda e, ut=ut: e.tensor_scalar(out=yacc, in0=ut, scalar1=dcol[:, 0:1], scalar2=None, op0=ALU.mult),
            [uk, "dcol"], ["yacc"])
        ufw, ubw = s5_views(ut, colmajor)
        yfw, ybw = s5_views(yacc, colmajor)
        ng = 0
        for dr in range(2):
            uv = ufw if dr == 0 else ubw
            yv = yfw if dr == 0 else ybw
            for g8 in range(8):
                col = dr * 8 + g8
                (la, lak), (lb, lbk), (c1g, c1gk), (c2g, c2gk) = gw[ng % 2]
                rt, rtk = rrt[ng % 2]
                ng += 1
                la2, lb2 = la.rearrange("p (a m) -> p a m", a=1)[:, 0, :], lb
                vop(lambda e, la=la, dr=dr, g8=g8: e.tensor_scalar(out=la, in0=LA[:, dr, :], scalar1=rowmask[:, g8:g8 + 1],
                                                                    scalar2=None, op0=ALU.mult), ["LA", "rowmask"], [lak])
                vop(lambda e, lb=lb, dr=dr, g8=g8: e.tensor_scalar(out=lb, in0=LB[:, dr, :], scalar1=rowmask[:, g8:g8 + 1],
                                                                    scalar2=None, op0=ALU.mult), ["LB", "rowmask"], [lbk])
                vop(lambda e, c1g=c1g: e.memset(c1g, 0.0), [], [c1gk])
                vop(lambda e, c2g=c2g: e.memset(c2g, 0.0), [], [c2gk])
                vop(lambda e, c1g=c1g, dr=dr, g8=g8: e.tensor_copy(out=c1g[:, 16 * g8:16 * g8 + 16], in_=C1f[:, dr, 16 * g8:16 * g8 + 16]),
                    ["C1f"], [c1gk])
                vop(lambda e, c2g=c2g, dr=dr, g8=g8: e.tensor_copy(out=c2g[:, 16 * g8:16 * g8 + 16], in_=C2f[:, dr, 16 * g8:16 * g8 + 16]),
                    ["C2f"], [c2gk])
                vop(lambda e, rt=rt, col=col: e.tensor_scalar(out=rt, in0=iota[:, 0:512], scalar1=0.0, scalar2=rrc[:, col:col + 1],
                                                              op0=ALU.mult, op1=ALU.add), ["iota", "rrc"], [rtk])
                vop(lambda e, col=col: e.tensor_scalar(out=TA, in0=iota, scalar1=thp[:, col:col + 1], scalar2=MAGIC,
                                                       op0=ALU.mult, op1=ALU.add), ["iota", "thp"], ["TA"])
                aop(lambda e: e.activation(out=TA, in_=TA, func=AF.Identity, bias=negmagic, scale=1.0), ["TA", "negmagic"], ["TA"])
                vop(lambda e, col=col: e.scalar_tensor_tensor(out=TB, in0=iota, scalar=thp[:, col:col + 1], in1=TA,
                                                              op0=ALU.mult, op1=ALU.subtract), ["iota", "thp", "TA"], ["TB"])
                aop(lambda e: e.activation(out=SIN, in_=TB, func=AF.Sin, scale=2 * PI), ["TB"], ["SIN"])
                aop(lambda e: e.activation(out=TA, in_=TB, func=AF.Abs), ["TB"], ["TA"])
                aop(lambda e: e.activation(out=COS, in_=TA, func=AF.Sin, scale=-2 * PI, bias=halfpi), ["TA", "halfpi"], ["COS"])
                prevG = None
                for ti in range(9):
                    useg, tau0, n, is3 = uv[ti]
                    yseg = yv[ti][0]
                    ba, bb = P.bank(), P.bank()
                    v3 = (lambda ap: ap.rearrange("p (c r) -> p c r", c=8)) if is3 else (lambda ap: ap)
                    mm(P, v3(ba[:, 0:n]), la, useg, True, True, reads=[lak, uk], writes=[ba])
                    mm(P, v3(bb[:, 0:n]), lb, useg, True, True, reads=[lbk, uk], writes=[bb])
                    (t1, t1k), (t2, t2k), (w_, wk), (G, Gk) = T1[ti % 2], T2[ti % 2], Wt[ti % 2], Gt[ti % 2]
                    (c1, c1k), (s1, s1k) = C1t[ti % 2], S1t[ti % 2]
                    tt(t1[:, 0:n], ba[:, 0:n], COS[:, tau0:tau0 + n], ALU.mult, [ba, "COS"], [t1k])
                    tt(t2[:, 0:n], bb[:, 0:n], SIN[:, tau0:tau0 + n], ALU.mult, [bb, "SIN"], [t2k])
                    ptt(w_[:, 0:n], t1[:, 0:n], t2[:, 0:n], ALU.add, [t1k, t2k], [wk])
                    init = 0.0 if prevG is None else prevG[0]
                    vop(lambda e, G=G, rt=rt, w_=w_, n=n, init=init: e.tensor_tensor_scan(
                        out=G[:, 0:n], data0=rt[:, 0:n], data1=w_[:, 0:n], initial=init, op0=ALU.mult, op1=ALU.add),
                        [rtk, wk] + ([prevG[1]] if prevG else []), [Gk])
                    prevG = (G[:, n - 1:n], Gk)
                    ptt(c1[:, 0:n], G[:, 0:n], COS[:, tau0:tau0 + n], ALU.mult, [Gk, "COS"], [c1k])
                    ptt(s1[:, 0:n], G[:, 0:n], SIN[:, tau0:tau0 + n], ALU.mult, [Gk, "SIN"], [s1k])
                    by = P.bank()
                    mm(P, by[:, 0:n], c1g, c1[:, 0:n], True, False, reads=[c1gk, c1k], writes=[by])
                    mm(P, by[:, 0:n], c2g, s1[:, 0:n], False, True, reads=[c2gk, s1k], writes=[by])
                    tt(yseg, v3(by[:, 0:n]), yseg, ALU.add, [by, "yacc"], ["yacc"])
        aop(lambda e: e.activation(out=ut, in_=yacc, func=AF.Gelu), ["yacc"], [uk])

    for t in S5_TILES:
        s5_tile(t)
    P.phase_end(reset=False)
    for t in S5_TILES:
        P.dma(S.Ms[t][:, 0:SEQ], u_all[:, t, CTX:L])
        P.dma(S.Ms[t][:, SEQ:L], u_all[:, t, 0:CTX])
    P.phase_end()
    for t in range(4):
        P.gather(S.Ms[t], S.Mg[t])
    P.phase_end()


def s5_host(I, j):
    f = np.float32
    a_re, a_im, ldt = I["s5_a_re"][j], I["s5_a_im"][j], I["s5_log_dt"][j]
    b_re, b_im, c_re, c_im = I["s5_b_re"][j], I["s5_b_im"][j], I["s5_c_re"][j], I["s5_c_im"][j]
    blk = np.empty((16, 8, 16, 2, 5, 64), f)
    A = lambda a: a.reshape(2, 16, 8, 64).transpose(1, 2, 0, 3)[:, :, None, :, :]
    blk[:, :, :, :, 0, :] = A(a_re)
    blk[:, :, :, :, 1, :] = A(a_im)
    blk[:, :, :, :, 2, :] = ldt.reshape(2, 16, 8).transpose(1, 2, 0)[:, :, None, :, None]
    Bt = lambda b_: b_.reshape(2, 16, 8, 64, 16).transpose(1, 2, 4, 0, 3)
    blk[:, :, :, :, 3, :] = Bt(b_re)
    blk[:, :, :, :, 4, :] = Bt(b_im)
    st = np.empty((16, 2, 64, 2, 8, 3), f)
    As = lambda a: a.reshape(2, 16, 8, 64).transpose(1, 3, 0, 2)[:, None, :, :, :]
    st[..., 0] = As(a_re)
    st[..., 1] = As(a_im)
    st[..., 2] = ldt.reshape(2, 16, 8).transpose(1, 0, 2)[:, None, None, :, :]
    Ct = lambda c_: c_.reshape(2, 16, 8, 16, 64).transpose(1, 4, 0, 2, 3).reshape(16, 64, 2, 128)
    c = np.empty((16, 2, 64, 2, 2, 128), f)
    c[:, 0, :, :, 0, :] = Ct(c_re)
    c[:, 1, :, :, 0, :] = Ct(c_im)
    c[:, 0, :, :, 1, :] = Ct(c_im)
    c[:, 1, :, :, 1, :] = Ct(c_re)
    par = np.zeros((16 * 128, 1216), f)
    par[:, 0:640] = blk.reshape(16 * 128, 640)
    par[:, 640:688] = st.reshape(16 * 128, 48)
    par[:, 688:1200] = c.reshape(16 * 128, 512)
    par[:, 1200] = I["s5_d"][j].reshape(16 * 128)
    return {"s5par%d" % j: par}


def const_host():
    p = np.arange(128)
    return {"ones": np.ones((128, 128), np.float32),
            "iota": np.ascontiguousarray(np.broadcast_to(np.arange(LSEQ, dtype=np.float32), (128, LSEQ))),
            "rowmask": (p[:, None] // 16 == np.arange(8)[None, :]).astype(np.float32),
            "sgn1": np.where(p < 64, 1.0, -1.0).astype(np.float32)[:, None]}


def build_test_M(i):
    P = Prog()
    S = setup_state(P)
    P.dma(S.Hs, P.inp("inj_Hs", [D, NTOK], BF16), writes=[("Hs", 0), ("Hs", 1)])
    for c in range(2):
        P.gather(S.Hs[c * 1024:(c + 1) * 1024, :], S.Hg[c * 8192:(c + 1) * 8192, :], reads=[("Hs", c)], writes=[("Hg", c)])
    P.phase_end()
    kind = KINDS[i]
    if kind == "s5":
        phase_M_S5(P, S, i)
        nch = 2
    elif kind == "hy":
        weights_for_M(P, S, i)
        phase_M_HY(P, S, i)
        nch = 2
    else:
        weights_for_M(P, S, i)
        phase_M_GDN(P, S, i)
        nch = 4
    o = P.outp("o_Ms", [nch * 256, LSEQ], BF16)
    for t in range(nch * 2):
        P.dma(o[t * 128:(t + 1) * 128, :], S.Ms[t], reads=[("Ms", t)], is_output=True)
    return P.build(), P


HY_NCOL = 4224
HY_MAX_DECAY = math.log(1e-2) / 0.3
HY_MIN_DECAY = math.log(1e-2) / 1.5


def weights_for_M(P, S, i):
    kind = KINDS[i]
    if kind == "hy":
        S.W["hyin"] = WRep(P, "hyin", D, 3 * D, (2048, 2048, 2048))
    elif kind == "gdn":
        S.W["gdin"] = WRep(P, "gdin", D, 12288, (4096, 4096, 4096))


def trig_tables(P, tabc, tabs, nrt, ncol, rowsc, iota, negmagic, halfpi, bufs):
    TA, TB, SIN, COS, Cb, Sb = bufs

    def one(a):
        P.op("vector", lambda e: e.tensor_scalar(out=TA[:, 0:ncol], in0=iota[:, 0:ncol], scalar1=rowsc[:, a:a + 1], scalar2=MAGIC,
                                                 op0=ALU.mult, op1=ALU.add), reads=["iota", "rowsc"], writes=["TA"])
        P.op("scalar", lambda e: e.activation(out=TA[:, 0:ncol], in_=TA[:, 0:ncol], func=AF.Identity, bias=negmagic, scale=1.0),
             reads=["TA", "negmagic"], writes=["TA"])
        P.op("vector", lambda e: e.scalar_tensor_tensor(out=TB[:, 0:ncol], in0=iota[:, 0:ncol], scalar=rowsc[:, a:a + 1], in1=TA[:, 0:ncol],
                                                        op0=ALU.mult, op1=ALU.subtract), reads=["iota", "rowsc", "TA"], writes=["TB"])
        P.op("scalar", lambda e: e.activation(out=Sb[:, 0:ncol], in_=TB[:, 0:ncol], func=AF.Sin, scale=2 * PI), reads=["TB"], writes=["Sb"])
        P.op("scalar", lambda e: e.activation(out=TA[:, 0:ncol], in_=TB[:, 0:ncol], func=AF.Abs), reads=["TB"], writes=["TA"])
        P.op("scalar", lambda e: e.activation(out=Cb[:, 0:ncol], in_=TA[:, 0:ncol], func=AF.Sin, scale=-2 * PI, bias=halfpi),
             reads=["TA", "halfpi"], writes=["Cb"])
        P.dma(tabc[a * 128:(a + 1) * 128, :], Cb[:, 0:ncol], reads=["Cb"], writes=["tabc"])
        P.dma(tabs[a * 128:(a + 1) * 128, :], Sb[:, 0:ncol], reads=["Sb"], writes=["tabs"])
    for a in range(nrt):
        one(a)


def phase_M_HY(P, S, i):
    L = LSEQ
    Win = S.W["hyin"]
    NC = HY_NCOL
    U = P.dram("hyU", [12 * 128, L], F32)
    Z1 = P.dram("hyZ1", [4 * 128, L], F32)
    TABC = P.dram("hyTC", [NC, NC], BF16)
    TABS = P.dram("hyTS", [NC, NC], BF16)
    TABC2 = P.dram("hyTC2", [384, 384], BF16)
    TABS2 = P.dram("hyTS2", [384, 384], BF16)
    FL = P.dram("hyFL", [2 * 2 * SEQ, 512], BF16)
    FC = P.dram("hyFC", [2 * 2 * CTX, 512], BF16)
    par_d = P.inp("hypar", [4 * 128, 4 * 17])
    w4_d = P.inp("hyw4", [64, 4 * 2048])
    w1_d = P.inp("hyw1", [33, 64])
    w23_d = P.inp("hyw23", [64, 128])
    fe_l = P.inp("hyfeL", [33, SEQ])
    fe_c = P.inp("hyfeC", [33, CTX])
    tun_d = P.inp("hytun", [128, 34])
    dl_d = P.inp("hydelta", [4 * 128, 512])
    rs_d = P.inp("hyrowsc", [128, 36])
    wk_d = P.inp("hywk", [128, 36])
    id_d = cinp(P, S, "ident", [128, 128])
    io_d = cinp(P, S, "iota", [128, L])
    iota = P.f32(NC)
    negmagic, halfpi = P.f32(1), P.f32(1)
    rowsc = P.f32(36)
    P.dma(iota, io_d[:, 0:NC], writes=["iota"])
    P.dma(rowsc, rs_d, writes=["rowsc"])
    P.op("vector", lambda e: e.memset(negmagic, -MAGIC), writes=["negmagic"])
    P.op("vector", lambda e: e.memset(halfpi, PI / 2), writes=["halfpi"])
    bufs = (P.f32(NC), P.f32(NC), None, None, P.bf(NC), P.bf(NC))
    trig_tables(P, TABC, TABS, 33, NC, rowsc, iota, negmagic, halfpi, bufs)
    trig_tables(P, TABC2, TABS2, 3, 384, rowsc[:, 33:36], iota, negmagic, halfpi, bufs)
    P.phase_end()

    for c in range(2):
        P.dma(S.Hloc[c * 4096:(c + 1) * 4096, :].rearrange("(r k) n -> k r n", r=4),
              bass.AP(S.Hg.tensor, P.x_hb + c * 8192 * NTOK, [[NTOK, 1024], [1024 * NTOK, 4], [1, NTOK]]),
              writes=["Hloc"], eng="scalar")
    par = P.f32(68).rearrange("p (c f) -> p c f", c=4)
    P.dma(par, bass.AP(par_d.tensor, P.cq * (128 * 68), [[68, 128], [1, 68]]).rearrange("p (c f) -> p c f", c=4),
          writes=["par"], eng="scalar")
    wall = P.bf(16 * 1536).rearrange("p (k n) -> p k n", k=16)
    for s_ in range(3):
        src, skey = Win.src(0, 16, P.x_q, 512, chunk=s_)
        P.dma(wall[:, :, s_ * 512:(s_ + 1) * 512], src, reads=[skey], writes=["wall"], eng="scalar")
    P.scratch_pool(4)
    hb = [(P.bf(16 * 512).rearrange("p (k n) -> p k n", k=16), "hb%d" % n) for n in range(2)]
    stiles = [(0, 256, [(rp, 0, 64) for rp in range(4)])] + \
             [(256 + 512 * m, 512, [(m // 2, 64 + (m % 2) * 512, 512)]) for m in range(8)]

    def inproj_tile(n_, c0, w, pieces):
        h, hk = hb[n_ % 2]
        off = 0
        for (rp, lc, pw) in pieces:
            for c in range(2):
                P.dma(h[:, c * 8:(c + 1) * 8, off:off + pw],
                      S.Hloc[c * 4096 + rp * 1024:c * 4096 + (rp + 1) * 1024, lc:lc + pw].rearrange("(k p) n -> p k n", p=128),
                      reads=["Hloc"], writes=[hk])
            off += pw
        for j in range(12):
            bank = P.bank()
            for kt in range(16):
                mm(P, bank[:, 0:w], wall[:, kt, j * 128:(j + 1) * 128], h[:, kt, 0:w], kt == 0, kt == 15,
                   reads=["wall", hk], writes=[bank])
            st, stk = P.scr()
            s_, ct = j // 4, j % 4
            P.op("vector", lambda e, bank=bank, st=st, s_=s_, ct=ct: e.tensor_scalar(
                out=st[:, 0:w], in0=bank[:, 0:w], scalar1=par[:, ct, s_:s_ + 1], scalar2=None, op0=ALU.add),
                reads=[bank, "par"], writes=[stk])
            P.dma(U[j * 128:(j + 1) * 128, c0:c0 + w], st[:, 0:w], reads=[stk], writes=[("U", j)])
    for n_, (c0, w, pieces) in enumerate(stiles):
        inproj_tile(n_, c0, w, pieces)
    P.phase_end(reset=False)

    ua = [(P.f32(L), "ua%d" % n) for n in range(2)]
    ub = [(P.f32(L), "ub%d" % n) for n in range(2)]

    def dw_tile(j):
        (a, ak), (b_, bk) = ua[j % 2], ub[j % 2]
        s_, ct = j // 4, j % 4
        cw = lambda k: par[:, ct, 3 + 3 * k + s_:4 + 3 * k + s_]
        cb = par[:, ct, 12 + s_:13 + s_]
        P.dma(a, U[j * 128:(j + 1) * 128, :], reads=[("U", j)], writes=[ak])
        P.op("vector", lambda e: e.tensor_scalar(out=b_, in0=a, scalar1=cw(1), scalar2=cb, op0=ALU.mult, op1=ALU.add),
             reads=[ak, "par"], writes=[bk])
        for (s0, s1) in ((0, CTX), (CTX, L)):
            P.op("vector", lambda e, s0=s0, s1=s1: e.scalar_tensor_tensor(
                out=b_[:, s0 + 1:s1], in0=a[:, s0:s1 - 1], scalar=cw(0), in1=b_[:, s0 + 1:s1], op0=ALU.mult, op1=ALU.add),
                reads=[ak, bk, "par"], writes=[bk])
            P.op("vector", lambda e, s0=s0, s1=s1: e.scalar_tensor_tensor(
                out=b_[:, s0:s1 - 1], in0=a[:, s0 + 1:s1], scalar=cw(2), in1=b_[:, s0:s1 - 1], op0=ALU.mult, op1=ALU.add),
                reads=[ak, bk, "par"], writes=[bk])
        P.dma(U[j * 128:(j + 1) * 128, :], b_, reads=[bk], writes=[("U", j)])
    for j in range(12):
        dw_tile(j)
    P.phase_end()

    par = P.f32(68).rearrange("p (c f) -> p c f", c=4)
    P.dma(par, bass.AP(par_d.tensor, P.cq * (128 * 68), [[68, 128], [1, 68]]).rearrange("p (c f) -> p c f", c=4),
          writes=["par"], eng="scalar")
    ident = P.f32(128)
    w1 = P.f32(64)
    w23 = P.f32(128)
    mlpb = P.f32(8)
    tun = P.f32(34)
    wk = P.f32(36)
    w4s = P.f32(4 * 512).rearrange("p (o n) -> p o n", o=4)
    delta = P.f32(512)
    mask0 = P.f32(1)
    P.dma(ident, id_d, writes=["ident"])
    P.dma(w1[0:33, :], w1_d, writes=["w1"])
    P.dma(w23[0:64, :], w23_d, writes=["w23"])
    P.dma(mlpb[0:64, 0:4], P.inp("hymlpb", [64, 4]), writes=["mlpb"])
    P.op("vector", lambda e: e.tensor_scalar(out=mlpb[0:64, 3:4], in0=mlpb[0:64, 3:4], scalar1=1.0 / (2 * PI), scalar2=None,
                                             op0=ALU.mult), reads=["mlpb"], writes=["mlpb"])
    P.dma(tun, tun_d, writes=["tun"])
    P.dma(wk, wk_d, writes=["wk"])
    P.dma(w4s[0:64], bass.AP(w4_d.tensor, P.x_q, [[8192, 64], [2048, 4], [1, 512]]), writes=["w4s"], eng="scalar")
    P.dma(delta, bass.AP(dl_d.tensor, P.cq * (128 * 512), [[512, 128], [1, 512]]), writes=["delta"], eng="scalar")
    P.dma(mask0, P.inp("hymask0", [128, 1]), writes=["mask0"])
    abase_save = P.aoff

    def seq_stage(n, fe_d, tcol0, wcol0, TC, TS, NK, F_d, c_seq0, ms_col0):
        NT = n // 128
        WC = min(512, n)
        P.scratch_pool(6)
        Ha, Hb = P.f32(n), P.f32(n)
        fe = P.f32(n)
        P.dma(fe[0:33, :], fe_d, writes=["fe"])

        def layer(li_, X, K, Wt_, Hout, hkey_in, hkey_out):
            for c0 in range(0, n, WC):
                bank = P.bank()
                mm(P, bank[0:64, 0:WC], Wt_, X[0:K, c0:c0 + WC], True, True, reads=[hkey_in, "w1", "w23"], writes=[bank])
                y_, yk = P.scr()
                k_, kk = P.scr()
                P.op("vector", lambda e, bank=bank, y_=y_: e.tensor_scalar(
                    out=y_[0:64, 0:WC], in0=bank[0:64, 0:WC], scalar1=mlpb[0:64, li_:li_ + 1], scalar2=mlpb[0:64, 3:4],
                    op0=ALU.add, op1=ALU.mult), reads=[bank, "mlpb"], writes=[yk])
                P.op("vector", lambda e, y_=y_, k_=k_: e.tensor_scalar(out=k_[0:64, 0:WC], in0=y_[0:64, 0:WC], scalar1=MAGIC,
                                                                      scalar2=None, op0=ALU.add), reads=[yk], writes=[kk])
                P.op("vector", lambda e, k_=k_: e.tensor_scalar(out=k_[0:64, 0:WC], in0=k_[0:64, 0:WC], scalar1=-MAGIC,
                                                                scalar2=None, op0=ALU.add), reads=[kk], writes=[kk])
                P.op("vector", lambda e, y_=y_, k_=k_: e.tensor_tensor(out=y_[0:64, 0:WC], in0=y_[0:64, 0:WC], in1=k_[0:64, 0:WC],
                                                                       op=ALU.subtract), reads=[yk, kk], writes=[yk])
                P.op("scalar", lambda e, y_=y_, c0=c0: e.activation(out=Hout[0:64, c0:c0 + WC], in_=y_[0:64, 0:WC], func=AF.Sin,
                                                                    scale=2 * PI), reads=[yk], writes=[hkey_out])
        layer(0, fe, 33, w1[0:33, :], Ha, "fe", "Ha")
        layer(1, Ha, 64, w23[0:64, 0:64], Hb, "Ha", "Hb")
        layer(2, Hb, 64, w23[0:64, 64:128], Ha, "Hb", "Ha")
        nb = [P.banks[6], P.banks[7]]
        P.bank_i = 0
        old_bank = P.bank

        def bank6():
            b_ = P.banks[P.bank_i]
            P.bank_i = (P.bank_i + 1) % 6
            return b_
        P.bank = bank6
        dec = [(P.f32(512), "dec%d" % k) for k in range(2)]
        rn = [P.f32(512), P.f32(512)]
        fsd = [(P.bf(512), "fsd%d" % k) for k in range(4)]
        nfs = [0]

        def filt_tile(lt, pass_):
            dc, dck = dec[lt % 2]
            P.op("scalar", lambda e: e.activation(out=dc, in_=delta, func=AF.Exp, scale=tun[:, tcol0 + lt:tcol0 + lt + 1]),
                 reads=["delta", "tun"], writes=[dck])
            ft = []
            for od in range(4):
                bank = P.bank()
                mm(P, bank[:, 0:512], Ha[0:64, lt * 128:(lt + 1) * 128], w4s[0:64, od, :], True, True, reads=["Ha", "w4s"], writes=[bank])
                f_, fk = P.scr()
                P.op("vector", lambda e, bank=bank, f_=f_: e.tensor_tensor(out=f_, in0=bank[:, 0:512], in1=dc, op=ALU.mult),
                     reads=[bank, dck], writes=[fk])
                if lt == 0 and od % 2 == 1:
                    P.op("vector", lambda e, f_=f_: e.tensor_scalar(out=f_, in0=f_, scalar1=mask0[:, 0:1], scalar2=None, op0=ALU.mult),
                         reads=[fk, "mask0"], writes=[fk])
                ft.append((f_, fk, od))
                if pass_ == 1:
                    P.op("scalar", lambda e, f_=f_: e.activation(out=f_, in_=f_, func=AF.Abs), reads=[fk], writes=[fk])
                    o_ = od // 2
                    mm(P, nb[o_][:, 0:512], S.ones, f_, lt == 0 and od % 2 == 0, lt == NT - 1 and od % 2 == 1,
                       reads=[fk, "ones"], writes=[nb[o_]])
                elif od % 2 == 1:
                    o_ = od // 2
                    (fw, fwk, _), (bw, bwk, _) = ft[od - 1], ft[od]
                    for sd, op_ in ((0, ALU.add), (1, ALU.subtract)):
                        t_, tk = P.scr()
                        ob, obk = fsd[nfs[0] % 4]
                        nfs[0] += 1
                        P.op("vector", lambda e, t_=t_, fw=fw, bw=bw, op_=op_: e.tensor_tensor(out=t_, in0=fw, in1=bw, op=op_),
                             reads=[fwk, bwk], writes=[tk])
                        P.op("vector", lambda e, t_=t_, ob=ob, o_=o_: e.tensor_tensor(out=ob, in0=t_, in1=rn[o_], op=ALU.mult),
                             reads=[tk, "rn"], writes=[obk])
                        P.dma(F_d[(o_ * 2 + sd) * n + lt * 128:(o_ * 2 + sd) * n + (lt + 1) * 128, :], ob, reads=[obk], writes=["F"])
        for lt in range(NT):
            filt_tile(lt, 1)
        for o_ in range(2):
            P.op("vector", lambda e, o_=o_: e.reciprocal(out=rn[o_], in_=nb[o_][:, 0:512]), reads=[nb[o_]], writes=["rn"])
        for lt in range(NT):
            filt_tile(lt, 2)
        P.bank = old_bank
        P.phase_end(reset=False)
        P.res = {}
        P.aoff = abase_save
        P.scratch_pool(6)
        fsd = [(P.bf(512), "fsd%d" % k) for k in range(4)]

        ztok = P.bf(NT * 256).rearrange("p (t c) -> p t c", t=NT)
        fs = P.bf(NT * 256).rearrange("p (t c) -> p t c", t=NT)
        fd = P.bf(NT * 256).rearrange("p (t c) -> p t c", t=NT)
        Am = P.bf(NK * 256).rearrange("p (k c) -> p k c", k=NK)
        Bm = P.bf(NK * 256).rearrange("p (k c) -> p k c", k=NK)
        zT = [(P.f32(n), "zT%d" % k) for k in range(2)]
        tabf = [[(P.bf(NT * 128).rearrange("p (t c) -> p t c", t=NT), "tf%d_%d" % (k, cs)) for cs in range(2)] for k in range(2)]
        tabi = [[(P.bf(WC), "ti%d_%d" % (k, cs)) for cs in range(2)] for k in range(2)]
        hcs = [(P.f32(256), "hcs%d" % k) for k in range(2)]
        NTT = n // WC

        def conv(o_, hh):
            zsrc = U if o_ == 0 else Z1
            gsrc_row0 = (1 + o_) * 4 * 128
            for c in range(2):
                ct = 2 * hh + c
                z_, zk = zT[c]
                P.dma(z_, zsrc[ct * 128:(ct + 1) * 128, c_seq0:c_seq0 + n], reads=[("U", ct), ("Z1", ct)], writes=[zk])
                for lt in range(NT):
                    bank = P.bank()
                    P.op("tensor", lambda e, bank=bank, z_=z_, lt=lt: e.transpose(bank[:, 0:128], z_[:, lt * 128:(lt + 1) * 128], ident),
                         reads=[zk, "ident"], writes=[bank], nosame=True)
                    P.op("vector", lambda e, bank=bank, lt=lt, c=c: e.tensor_copy(out=ztok[:, lt, c * 128:(c + 1) * 128], in_=bank[:, 0:128]),
                         reads=[bank], writes=["ztok"])
            P.dma(fs, F_d[(o_ * 2) * n:(o_ * 2 + 1) * n, hh * 256:(hh + 1) * 256].rearrange("(t p) c -> p t c", p=128),
                  reads=["F"], writes=["fs"])
            P.dma(fd, F_d[(o_ * 2 + 1) * n:(o_ * 2 + 2) * n, hh * 256:(hh + 1) * 256].rearrange("(t p) c -> p t c", p=128),
                  reads=["F"], writes=["fd"])

            def fwd(kt):
                (tc, tck), (ts_, tsk) = tabf[kt % 2]
                P.dma(tc, TC[0:n, kt * 128:(kt + 1) * 128].rearrange("(t p) c -> p t c", p=128), reads=["tabc"], writes=[tck])
                P.dma(ts_, TS[0:n, kt * 128:(kt + 1) * 128].rearrange("(t p) c -> p t c", p=128), reads=["tabs"], writes=[tsk])
                bz, bs, bh, bg = P.bank(), P.bank(), P.bank(), P.bank()
                for (bk_, tb_, tbk, rh, rhk) in ((bz, tc, tck, ztok, "ztok"), (bs, ts_, tsk, ztok, "ztok"),
                                                 (bh, tc, tck, fs, "fs"), (bg, ts_, tsk, fd, "fd")):
                    for lt in range(NT):
                        mm(P, bk_[:, 0:256], tb_[:, lt, :], rh[:, lt, :], lt == 0, lt == NT - 1, reads=[tbk, rhk], writes=[bk_])
                (hc, hck), (hs_, hsk) = hcs
                wcol = wk[:, wcol0 + kt:wcol0 + kt + 1]
                P.op("scalar", lambda e: e.activation(out=hc, in_=bh[:, 0:256], func=AF.Copy, scale=wcol), reads=[bh, "wk"], writes=[hck])
                P.op("scalar", lambda e: e.activation(out=hs_, in_=bg[:, 0:256], func=AF.Copy, scale=wcol), reads=[bg, "wk"], writes=[hsk])
                t1, t1k = P.scr()
                t2, t2k = P.scr()
                P.op("vector", lambda e: e.tensor_tensor(out=t1[:, 0:256], in0=bz[:, 0:256], in1=hc, op=ALU.mult), reads=[bz, hck], writes=[t1k])
                P.op("vector", lambda e: e.tensor_tensor(out=t2[:, 0:256], in0=bs[:, 0:256], in1=hs_, op=ALU.mult), reads=[bs, hsk], writes=[t2k])
                P.op("vector", lambda e: e.tensor_tensor(out=Am[:, kt, :], in0=t1[:, 0:256], in1=t2[:, 0:256], op=ALU.subtract),
                     reads=[t1k, t2k], writes=["Am"])
                t3, t3k = P.scr()
                t4, t4k = P.scr()
                P.op("vector", lambda e: e.tensor_tensor(out=t3[:, 0:256], in0=bz[:, 0:256], in1=hs_, op=ALU.mult), reads=[bz, hsk], writes=[t3k])
                P.op("vector", lambda e: e.tensor_tensor(out=t4[:, 0:256], in0=bs[:, 0:256], in1=hc, op=ALU.mult), reads=[bs, hck], writes=[t4k])
                P.op("vector", lambda e: e.tensor_tensor(out=Bm[:, kt, :], in0=t3[:, 0:256], in1=t4[:, 0:256], op=ALU.add),
                     reads=[t3k, t4k], writes=["Bm"])
            for kt in range(NK):
                fwd(kt)

            def inv(tt_):
                bo = [P.bank(), P.bank()]
                for kt in range(NK):
                    (tc, tck), (ts_, tsk) = tabi[kt % 2]
                    P.dma(tc, TC[kt * 128:(kt + 1) * 128, tt_ * WC:(tt_ + 1) * WC], reads=["tabc"], writes=[tck])
                    P.dma(ts_, TS[kt * 128:(kt + 1) * 128, tt_ * WC:(tt_ + 1) * WC], reads=["tabs"], writes=[tsk])
                    for c in range(2):
                        mm(P, bo[c][:, 0:WC], Am[:, kt, c * 128:(c + 1) * 128], tc, kt == 0, False, reads=["Am", tck], writes=[bo[c]])
                        mm(P, bo[c][:, 0:WC], Bm[:, kt, c * 128:(c + 1) * 128], ts_, False, kt == NK - 1, reads=["Bm", tsk], writes=[bo[c]])
                for c in range(2):
                    ct = 2 * hh + c
                    z_, zk = zT[c]
                    g_, gk = P.scr()
                    y_, yk = P.scr()
                    cols = slice(c_seq0 + tt_ * WC, c_seq0 + (tt_ + 1) * WC)
                    P.dma(g_[:, 0:WC], U[gsrc_row0 + ct * 128:gsrc_row0 + (ct + 1) * 128, cols], reads=[("U", 4 * (1 + o_) + ct)], writes=[gk])
                    P.op("vector", lambda e, c=c, z_=z_, y_=y_, ct=ct: e.scalar_tensor_tensor(
                        out=y_[:, 0:WC], in0=z_[:, tt_ * WC:(tt_ + 1) * WC], scalar=par[:, ct, 15 + o_:16 + o_], in1=bo[c][:, 0:WC],
                        op0=ALU.mult, op1=ALU.add), reads=[zk, "par", bo[c]], writes=[yk])
                    if o_ == 0:
                        P.op("vector", lambda e, y_=y_, g_=g_: e.tensor_tensor(out=y_[:, 0:WC], in0=y_[:, 0:WC], in1=g_[:, 0:WC], op=ALU.mult),
                             reads=[yk, gk], writes=[yk])
                        P.dma(Z1[ct * 128:(ct + 1) * 128, cols], y_[:, 0:WC], reads=[yk], writes=[("Z1w", ct)])
                    else:
                        ob, obk = fsd[nfs[0] % 4]
                        nfs[0] += 1
                        P.op("vector", lambda e, y_=y_, g_=g_, ob=ob: e.tensor_tensor(out=ob[:, 0:WC], in0=y_[:, 0:WC], in1=g_[:, 0:WC], op=ALU.mult),
                             reads=[yk, gk], writes=[obk])
                        P.dma(S.Ms[ct][:, ms_col0 + tt_ * WC:ms_col0 + (tt_ + 1) * WC], ob[:, 0:WC], reads=[obk], writes=[("Ms", ct)])
            for tt_ in range(NTT):
                inv(tt_)
        for o_ in range(2):
            for hh in range(2):
                conv(o_, hh)
            P.phase_end(reset=False)
        P.phase_end(reset=False)
        P.res = {}
        P.aoff = abase_save

    seq_stage(SEQ, fe_l, 0, 0, TABC, TABS, 33, FL, CTX, 0)
    seq_stage(CTX, fe_c, 32, 33, TABC2, TABS2, 3, FC, 0, SEQ)
    P.phase_end()
    for t in range(4):
        P.gather(S.Ms[t], S.Mg[t])
    P.phase_end()


def hy_host(I):
    f = np.float32
    out = {}
    in_b, cw, cb, bias = I["hy_in_b"][0], I["hy_conv_w"][0], I["hy_conv_b"][0], I["hy_bias"][0]
    par = np.zeros((4, 128, 4, 17), f)
    for q in range(4):
        for ct in range(4):
            ch = 512 * q + 128 * ct + np.arange(128)
            for s_ in range(3):
                par[q, :, ct, s_] = in_b[s_ * 2048 + ch]
                for k in range(3):
                    par[q, :, ct, 3 + 3 * k + s_] = cw[k, s_ * 2048 + ch]
                par[q, :, ct, 12 + s_] = cb[s_ * 2048 + ch]
            for o in range(2):
                par[q, :, ct, 15 + o] = bias[o, ch]
    out["hypar"] = par.reshape(4 * 128, 68)
    out["hyw4"] = np.ascontiguousarray(I["hy_f_w4"][0])
    out["hyw1"] = np.ascontiguousarray(I["hy_f_w1"][0])
    out["hyw23"] = np.ascontiguousarray(np.concatenate([I["hy_f_w2"][0], I["hy_f_w3"][0]], axis=1))
    out["hymlpb"] = np.ascontiguousarray(np.stack([I["hy_f_b1"][0], I["hy_f_b2"][0], I["hy_f_b3"][0], I["hy_f_freq"][0]], axis=1))

    def feats(n):
        t = np.arange(n, dtype=np.float64)
        tu = t / max(n - 1, 1)
        bands = np.linspace(1e-4, 15, 16)
        ang = (2.0 * np.pi / n) * t[:, None] * bands[None, :]
        return np.concatenate([tu[:, None], np.cos(ang), -np.sin(ang)], axis=-1).T.astype(f), tu
    feL, tuL = feats(SEQ)
    feC, tuC = feats(CTX)
    out["hyfeL"], out["hyfeC"] = np.ascontiguousarray(feL), np.ascontiguousarray(feC)
    tun = np.zeros((128, 34), f)
    tun[:, 0:32] = -tuL.reshape(32, 128).T
    tun[:, 32:34] = -tuC.reshape(2, 128).T
    out["hytun"] = tun
    deltas = np.abs(np.linspace(HY_MIN_DECAY, HY_MAX_DECAY, D)).astype(f)
    out["hydelta"] = np.ascontiguousarray(np.broadcast_to(deltas.reshape(4, 1, 512), (4, 128, 512))).reshape(512, 512)
    p = np.arange(128)
    rs = np.zeros((128, 36), f)
    wk = np.zeros((128, 36), f)
    for a in range(33):
        k = a * 128 + p
        rs[:, a] = k / 8192.0
        wk[:, a] = np.where(k > 4096, 0.0, np.where((k == 0) | (k == 4096), 1.0 / 8192, 2.0 / 8192))
    for a in range(3):
        k = a * 128 + p
        rs[:, 33 + a] = k / 512.0
        wk[:, 33 + a] = np.where(k > 256, 0.0, np.where((k == 0) | (k == 256), 1.0 / 512, 2.0 / 512))
    out["hyrowsc"], out["hywk"] = rs, wk
    out["ident"] = np.eye(128, dtype=f)
    m0 = np.ones((128, 1), f)
    m0[0, 0] = 0.0
    out["hymask0"] = m0
    return out


def mix_weight_shards(I, i, k):
    kind = KINDS[i]
    out = {}
    if kind == "hy":
        for j, a in enumerate(shard_cols(I["hy_in_w"][0], (2048, 2048, 2048), k)):
            out["hyin_%d" % j] = a
    elif kind == "gdn":
        for j, a in enumerate(shard_cols(I["gdn_in_w"][0][:, 0:12288], (4096, 4096, 4096), k)):
            out["gdin_%d" % j] = a
    return out


GD_NCH = LSEQ // 128


def gd_pchunk(dr, c):
    if dr == 0:
        return 128 * c, 1
    if c < 2:
        return 255 - 128 * c, -1
    return LSEQ - 1 - 128 * (c - 2), -1


def phase_M_GDN(P, S, i):
    L = LSEQ
    Win = S.W["gdin"]
    G = P.dram("gdG", [25 * 128, L], F32)
    par_d = P.inp("gdpar", [4 * 128, 16 * 5])
    gp_d = P.inp("gdgp", [4 * 64, 2])
    ng_d = P.inp("gdng", [128, 1])
    abw_d = P.inp("gdabw", [4 * 2048, 64])
    id_d = cinp(P, S, "ident", [128, 128])
    tri_d = P.inp("triu", [128, 128])
    mlo_d = P.inp("mlow", [128, 128])

    for c in range(2):
        P.dma(S.Hloc[c * 4096:(c + 1) * 4096, :].rearrange("(r k) n -> k r n", r=4),
              bass.AP(S.Hg.tensor, P.x_hb + c * 8192 * NTOK, [[NTOK, 1024], [1024 * NTOK, 4], [1, NTOK]]),
              writes=["Hloc"], eng="scalar")
    P.scratch_pool(4)
    wall = P.bf(16 * 1088).rearrange("p (k n) -> p k n", k=16)
    hb = [(P.bf(16 * 512).rearrange("p (k n) -> p k n", k=16), "hb%d" % n) for n in range(2)]
    stiles = [(0, 256, [(rp, 0, 64) for rp in range(4)])] + \
             [(256 + 512 * m, 512, [(m // 2, 64 + (m % 2) * 512, 512)]) for m in range(8)]
    x2q = P.cq * 1024
    passes = [
        ([(0, P.x_q, 512, 0), (0, P.x_q + 2048, 512, 512)], 0, 8, True),
        ([(1, x2q, 1024, 0)], 8, 8, False),
        ([(2, x2q, 1024, 0)], 16, 8, False),
    ]
    nh = [0]
    for (wsrc, tile0, ntile, with_ab) in passes:
        for (chunk, coff, ncols, d0) in wsrc:
            src, skey = Win.src(0, 16, coff, ncols, chunk=chunk)
            P.dma(wall[:, :, d0:d0 + ncols], src, reads=[skey], writes=["wall"], eng="scalar")
        if with_ab:
            P.dma(wall[:, :, 1024:1088], bass.AP(abw_d.tensor, P.cq * (2048 * 64), [[64, 128], [128 * 64, 16], [1, 64]]),
                  writes=["wall"], eng="gpsimd")

        def inproj_tile(c0, w, pieces, tile0=tile0, ntile=ntile, with_ab=with_ab):
            h, hk = hb[nh[0] % 2]
            nh[0] += 1
            off = 0
            for (rp, lc, pw) in pieces:
                for c in range(2):
                    P.dma(h[:, c * 8:(c + 1) * 8, off:off + pw],
                          S.Hloc[c * 4096 + rp * 1024:c * 4096 + (rp + 1) * 1024, lc:lc + pw].rearrange("(k p) n -> p k n", p=128),
                          reads=["Hloc"], writes=[hk])
                off += pw
            for j in range(ntile + (1 if with_ab else 0)):
                bank = P.bank()
                m_ = 64 if j == ntile else 128
                for kt in range(16):
                    mm(P, bank[0:m_, 0:w], wall[:, kt, j * 128:j * 128 + m_], h[:, kt, 0:w], kt == 0, kt == 15,
                       reads=["wall", hk], writes=[bank])
                st, stk = P.scr()
                P.op("scalar", lambda e, bank=bank, st=st, m_=m_: e.activation(out=st[0:m_, 0:w], in_=bank[0:m_, 0:w], func=AF.Copy),
                     reads=[bank], writes=[stk])
                row0 = (24 if j == ntile else tile0 + j) * 128
                P.dma(G[row0:row0 + m_, c0:c0 + w], st[0:m_, 0:w], reads=[stk], writes=["G"])
        for (c0, w, pieces) in stiles:
            inproj_tile(c0, w, pieces)
        P.phase_end(reset=False)
    P.phase_end()

    ones, eps_t = S.ones, S.eps
    par = P.f32(80).rearrange("p (t k) -> p t k", t=16)
    P.dma(par, bass.AP(par_d.tensor, P.cq * (128 * 80), [[80, 128], [1, 80]]).rearrange("p (t k) -> p t k", t=16),
          writes=["par"], eng="scalar")
    ua = [(P.f32(L), "ua%d" % n) for n in range(2)]
    ub = [(P.f32(L), "ub%d" % n) for n in range(2)]
    P.scratch_pool(4)
    e6 = P.f32(1)
    P.op("vector", lambda e: e.memset(e6, 1e-6), writes=["e6"])

    def dw_tile(j):
        (a, ak), (b_, bk) = ua[j % 2], ub[j % 2]
        P.dma(a, G[j * 128:(j + 1) * 128, :], reads=["G"], writes=[ak])
        P.op("vector", lambda e: e.tensor_scalar(out=b_, in0=a, scalar1=par[:, j, 2:3], scalar2=None, op0=ALU.mult),
             reads=[ak, "par"], writes=[bk])
        for (s0, s1) in ((0, CTX), (CTX, L)):
            for k in (0, 1, 3, 4):
                sh = k - 2
                if sh < 0:
                    o_sl, i_sl = slice(s0 - sh, s1), slice(s0, s1 + sh)
                else:
                    o_sl, i_sl = slice(s0, s1 - sh), slice(s0 + sh, s1)
                P.op("vector", lambda e, o_sl=o_sl, i_sl=i_sl, k=k: e.scalar_tensor_tensor(
                    out=b_[:, o_sl], in0=a[:, i_sl], scalar=par[:, j, k:k + 1], in1=b_[:, o_sl], op0=ALU.mult, op1=ALU.add),
                    reads=[ak, bk, "par"], writes=[bk])
        P.op("scalar", lambda e: e.activation(out=b_, in_=b_, func=AF.Silu), reads=[bk], writes=[bk])
        if j < 8:
            for c0 in range(0, L, 512):
                w = min(512, L - c0)
                sq, sqk = P.scr()
                rs, rsk = P.scr()
                bank = P.bank()
                P.op("scalar", lambda e, sq=sq, c0=c0, w=w: e.activation(out=sq[:, 0:w], in_=b_[:, c0:c0 + w], func=AF.Square),
                     reads=[bk], writes=[sqk])
                mm(P, bank[:, 0:w], ones, sq[:, 0:w], True, True, reads=[sqk, "ones"], writes=[bank])
                P.op("scalar", lambda e, rs=rs, bank=bank, w=w: e.activation(out=rs[:, 0:w], in_=bank[:, 0:w], func=AF.Sqrt, bias=e6, scale=1.0),
                     reads=[bank, "e6"], writes=[rsk])
                P.op("vector", lambda e, rs=rs, w=w: e.reciprocal(out=rs[:, 0:w], in_=rs[:, 0:w]), reads=[rsk], writes=[rsk])
                sc_ = (128.0 ** -0.5) if j < 4 else 1.0
                P.op("vector", lambda e, rs=rs, c0=c0, w=w, sc_=sc_: e.scalar_tensor_tensor(
                    out=b_[:, c0:c0 + w], in0=b_[:, c0:c0 + w], scalar=sc_, in1=rs[:, 0:w], op0=ALU.mult, op1=ALU.mult),
                    reads=[bk, rsk], writes=[bk])
        P.dma(G[j * 128:(j + 1) * 128, :], b_, reads=[bk], writes=["G"])
    for j in range(16):
        dw_tile(j)
    gt, gtk = ua[0]
    gp = P.f32(2)
    nA = P.f32(1)
    P.dma(gt[0:64, :], G[24 * 128:24 * 128 + 64, :], reads=["G"], writes=[gtk])
    P.dma(gp[0:64, :], bass.AP(gp_d.tensor, P.cq * 128, [[2, 64], [1, 2]]), writes=["gp"], eng="scalar")
    P.op("scalar", lambda e: e.activation(out=nA[0:16, :], in_=gp[0:16, 0:1], func=AF.Exp), reads=["gp"], writes=["nA"])
    P.op("vector", lambda e: e.tensor_scalar(out=nA[0:16, :], in0=nA[0:16, :], scalar1=-1.0, scalar2=None, op0=ALU.mult), reads=["nA"], writes=["nA"])
    P.op("scalar", lambda e: e.activation(out=gt[0:16, :], in_=gt[0:16, :], func=AF.Exp, bias=gp[0:16, 1:2], scale=1.0), reads=[gtk, "gp"], writes=[gtk])
    P.op("scalar", lambda e: e.activation(out=gt[0:16, :], in_=gt[0:16, :], func=AF.Ln, bias=ones[0:16, 0:1], scale=1.0), reads=[gtk, "ones"], writes=[gtk])
    P.op("vector", lambda e: e.tensor_scalar(out=gt[0:16, :], in0=gt[0:16, :], scalar1=nA[0:16, 0:1], scalar2=None, op0=ALU.mult), reads=[gtk, "nA"], writes=[gtk])
    P.op("scalar", lambda e: e.activation(out=gt[32:48, :], in_=gt[32:48, :], func=AF.Sigmoid), reads=[gtk], writes=[gtk])
    P.dma(G[24 * 128:24 * 128 + 64, :], gt[0:64, :], reads=[gtk], writes=["G"])
    P.phase_end()

    ones, eps_t = S.ones, S.eps
    ident, triu, mlow = P.f32(128), P.f32(128), P.f32(128)
    ngc = P.f32(1)
    P.dma(ident, id_d, writes=["ident"])
    P.dma(triu, tri_d, writes=["triu"])
    P.dma(mlow, mlo_d, writes=["mlow"])
    P.dma(ngc, ng_d, writes=["ngc"])
    GT = [P.f32(GD_NCH * 64).rearrange("p (c m) -> p c m", c=GD_NCH) for _ in range(2)]
    kin_buf = P.f32(L)
    gsb = kin_buf
    P.dma(gsb[0:64, :], G[24 * 128:24 * 128 + 64, :], reads=["G"], writes=["gsb"])
    P.scratch_pool(8, width=128)

    def rev128(ap, start, step):
        return _mkap(ap, start, [[step, 128]])
    for dr in range(2):
        for c in range(GD_NCH):
            st_, sp_ = gd_pchunk(dr, c)
            bank = P.bank()
            gtmp, gtmpk = P.scr()
            P.op("vector", lambda e, gtmp=gtmp, st_=st_, sp_=sp_: e.tensor_copy(out=gtmp[0:64, :], in_=rev128(gsb, st_, sp_)[0:64, :]),
                 reads=["gsb"], writes=[gtmpk])
            P.op("tensor", lambda e, bank=bank, gtmp=gtmp: e.transpose(bank[:, 0:64], gtmp[0:64, :], ident[0:64, 0:64]),
                 reads=[gtmpk, "ident"], writes=[bank], nosame=True)
            P.op("vector", lambda e, bank=bank, dr=dr, c=c: e.tensor_copy(out=GT[dr][:, c, :], in_=bank[:, 0:64]), reads=[bank], writes=["GT"])
    P.phase_end(reset=False)
    kin, qin = (kin_buf, "kin"), (P.f32(L), "qin")
    vins = [(P.f32(L), "vin%d" % n) for n in range(2)]
    oaccs = [(P.f32(L), "oacc%d" % n) for n in range(2)]
    Ssts = [(P.f32(128), "S%d" % n) for n in range(4)]

    def ev(eng, out, in_, reads, writes):
        if eng == "scalar":
            P.op("scalar", lambda e: e.activation(out=out, in_=in_, func=AF.Copy), reads=reads, writes=writes)
        else:
            P.op("vector", lambda e: e.tensor_copy(out=out, in_=in_), reads=reads, writes=writes)

    def tt(out, a, b_, op, reads, writes):
        P.op("vector", lambda e: e.tensor_tensor(out=out, in0=a, in1=b_, op=op), reads=reads, writes=writes)

    def ts(out, a, s1, op, reads, writes):
        P.op("vector", lambda e: e.tensor_scalar(out=out, in0=a, scalar1=s1, scalar2=None, op0=op), reads=reads, writes=writes)

    def mmf(out, lhsT, rhs, reads, writes, start=True, stop=True):
        mm(P, out, lhsT, rhs, start, stop, reads=reads, writes=writes)

    names = ["kT", "qT", "vT", "ktok", "vtok", "gcc", "gb", "gcrow", "Xp", "Xm", "E", "ET", "Lm", "U0", "LpA", "UpA", "LpB", "UpB",
             "Pm", "egr", "qgT", "attnT", "kbg", "vb", "u", "wT", "vnew", "kg", "otok", "sc1", "sc2", "sc3"]
    TLs = [{nm: (P.f32(128), "%s_%d" % (nm, ch)) for nm in names} for ch in range(4)]

    def chunk(ch, hh, h8, dr, c):
        TL = TLs[ch]
        Sst, Sk = Ssts[ch]
        vin = vins[hh]
        oacc, oacck = oaccs[hh]
        st_, sp_ = gd_pchunk(dr, c)
        (kT, kTk), (qT, qTk), (vT, vTk) = TL["kT"], TL["qT"], TL["vT"]
        for (dst, dk_), (src, sk_) in (((kT, kTk), kin), ((qT, qTk), qin), ((vT, vTk), vin)):
            P.op("vector", lambda e, dst=dst, src=src: e.tensor_copy(out=dst, in_=rev128(src, st_, sp_)), reads=[sk_], writes=[dk_])
        gcol = GT[dr][:, c, dr * 8 + h8:dr * 8 + h8 + 1]
        bcol = GT[dr][:, c, 32 + dr * 8 + h8:32 + dr * 8 + h8 + 1]
        (ktok, ktokk), (vtok, vtokk) = TL["ktok"], TL["vtok"]
        for (src, sk_, dst, dk_) in ((kT, kTk, ktok, ktokk), (vT, vTk, vtok, vtokk)):
            bank = P.bank()
            P.op("tensor", lambda e, bank=bank, src=src: e.transpose(bank[:, 0:128], src, ident), reads=[sk_, "ident"], writes=[bank], nosame=True)
            ev("scalar", dst, bank[:, 0:128], [bank], [dk_])
        yield
        (gcc, gcck), (gb, gbk), (gcrow, gcrowk) = TL["gcc"], TL["gb"], TL["gcrow"]
        bank = P.bank()
        mmf(bank[:, 0:1], triu, gcol, ["triu", "GT"], [bank])
        ev("vector", gcc[:, 0:1], bank[:, 0:1], [bank], [gcck])
        ts(gb, ones, gcol, ALU.mult, ["ones", "GT"], [gbk])
        bank = P.bank()
        mmf(bank[:, 0:128], gb, triu, [gbk, "triu"], [bank])
        ev("scalar", gcrow, bank[:, 0:128], [bank], [gcrowk])
        yield
        (Xp, Xpk), (Xm, Xmk), (E, Ek), (ET, ETk) = TL["Xp"], TL["Xm"], TL["E"], TL["ET"]
        P.op("vector", lambda e: e.tensor_scalar(out=Xp, in0=gcrow, scalar1=gcc[:, 0:1], scalar2=0.0, op0=ALU.subtract, op1=ALU.max),
             reads=[gcrowk, gcck], writes=[Xpk])
        P.op("vector", lambda e: e.tensor_scalar(out=Xm, in0=gcrow, scalar1=gcc[:, 0:1], scalar2=0.0, op0=ALU.subtract, op1=ALU.min),
             reads=[gcrowk, gcck], writes=[Xmk])
        P.op("scalar", lambda e: e.activation(out=E, in_=Xp, func=AF.Exp, scale=-1.0), reads=[Xpk], writes=[Ek])
        P.op("scalar", lambda e: e.activation(out=ET, in_=Xm, func=AF.Exp), reads=[Xmk], writes=[ETk])
        tt(E, E, mlow, ALU.mult, [Ek, "mlow"], [Ek])
        tt(ET, ET, triu, ALU.mult, [ETk, "triu"], [ETk])
        yield
        (Lm, Lmk), (U0, U0k) = TL["Lm"], TL["U0"]
        bank = P.bank()
        mmf(bank[:, 0:128], kT, kT, [kTk], [bank])
        P.op("vector", lambda e, bank=bank: e.scalar_tensor_tensor(out=Lm, in0=bank[:, 0:128], scalar=bcol, in1=E, op0=ALU.mult, op1=ALU.mult),
             reads=[bank, "GT", Ek], writes=[Lmk])
        bank = P.bank()
        P.op("tensor", lambda e, bank=bank: e.transpose(bank[:, 0:128], Lm, ident), reads=[Lmk, "ident"], writes=[bank], nosame=True)
        ev("scalar", U0, bank[:, 0:128], [bank], [U0k])
        (Pm, Pmk) = TL["Pm"]
        tt(Pm, ident, U0, ALU.subtract, ["ident", U0k], [Pmk])
        yield
        Lp, Lpk, Up, Upk = Lm, Lmk, U0, U0k
        pp = [(TL["LpA"], TL["UpA"]), (TL["LpB"], TL["UpB"])]
        for s_ in range(6):
            (Ln_, Lnk), (Un_, Unk) = pp[s_ % 2]
            b1 = P.bank()
            mmf(b1[:, 0:128], Up, Lp, [Upk, Lpk], [b1])
            ev("scalar", Ln_, b1[:, 0:128], [b1], [Lnk])
            if s_ < 5:
                b2 = P.bank()
                mmf(b2[:, 0:128], Lp, Up, [Lpk, Upk], [b2])
                ev("vector", Un_, b2[:, 0:128], [b2], [Unk])
            b3 = P.bank()
            mmf(b3[:, 0:128], Ln_, Pm, [Lnk, Pmk], [b3])
            tt(Pm, Pm, b3[:, 0:128], ALU.add, [Pmk, b3], [Pmk])
            Lp, Lpk, Up, Upk = Ln_, Lnk, Un_, Unk
            yield
        (sc1, sc1k), (sc2, sc2k) = TL["sc1"], TL["sc2"]
        P.op("scalar", lambda e: e.activation(out=sc1[:, 0:1], in_=gcc[:, 0:1], func=AF.Exp), reads=[gcck], writes=[sc1k])
        tt(sc1[:, 1:2], sc1[:, 0:1], bcol, ALU.mult, [sc1k, "GT"], [sc1k])
        (kbg, kbgk), (vb, vbk), (u_, uk_), (wT, wTk) = TL["kbg"], TL["vb"], TL["u"], TL["wT"]
        ts(kbg, ktok, sc1[:, 1:2], ALU.mult, [ktokk, sc1k], [kbgk])
        ts(vb, vtok, bcol, ALU.mult, [vtokk, "GT"], [vbk])
        bank = P.bank()
        mmf(bank[:, 0:128], Pm, vb, [Pmk, vbk], [bank])
        ev("scalar", u_, bank[:, 0:128], [bank], [uk_])
        bank = P.bank()
        mmf(bank[:, 0:128], kbg, Pm, [kbgk, Pmk], [bank])
        ev("vector", wT, bank[:, 0:128], [bank], [wTk])
        yield
        (attnT, attnTk), (egr, egrk), (qgT, qgTk), (kg, kgk) = TL["attnT"], TL["egr"], TL["qgT"], TL["kg"]
        bank = P.bank()
        mmf(bank[:, 0:128], kT, qT, [kTk, qTk], [bank])
        tt(attnT, bank[:, 0:128], ET, ALU.mult, [bank, ETk], [attnTk])
        P.op("scalar", lambda e: e.activation(out=egr, in_=gcrow, func=AF.Exp), reads=[gcrowk], writes=[egrk])
        tt(qgT, qT, egr, ALU.mult, [qTk, egrk], [qgTk])
        P.op("scalar", lambda e: e.activation(out=sc2[:, 0:1], in_=gcc[:, 0:1], func=AF.Exp, scale=-1.0, bias=gcrow[:, 127:128]),
             reads=[gcck, gcrowk], writes=[sc2k])
        P.op("scalar", lambda e: e.activation(out=sc2[:, 1:2], in_=gcrow[:, 127:128], func=AF.Exp), reads=[gcrowk, sc2k], writes=[sc2k])
        ts(kg, ktok, sc2[:, 0:1], ALU.mult, [ktokk, sc2k], [kgk])
        yield
        (vnew, vnewk), (otok, otokk) = TL["vnew"], TL["otok"]
        bank = P.bank()
        mmf(bank[:, 0:128], wT, Sst, [wTk, Sk], [bank])
        tt(vnew, u_, bank[:, 0:128], ALU.subtract, [uk_, bank], [vnewk])
        yield
        bank = P.bank()
        mmf(bank[:, 0:128], qgT, Sst, [qgTk, Sk], [bank], True, False)
        mmf(bank[:, 0:128], attnT, vnew, [attnTk, vnewk], [bank], False, True)
        ev("scalar", otok, bank[:, 0:128], [bank], [otokk])
        bank = P.bank()
        mmf(bank[:, 0:128], kg, vnew, [kgk, vnewk], [bank])
        P.op("vector", lambda e, bank=bank: e.scalar_tensor_tensor(out=Sst, in0=Sst, scalar=sc2[:, 1:2], in1=bank[:, 0:128],
                                                                   op0=ALU.mult, op1=ALU.add), reads=[Sk, sc2k, bank], writes=[Sk])
        yield
        bank = P.bank()
        P.op("tensor", lambda e, bank=bank: e.transpose(bank[:, 0:128], otok, ident), reads=[otokk, "ident"], writes=[bank], nosame=True)
        ov = rev128(oacc, st_, sp_)
        tt(ov, ov, bank[:, 0:128], ALU.add, [oacck, bank], [oacck])
        yield

    def chain(ch, hh, h8, dr):
        for c in range(GD_NCH):
            yield from chunk(ch, hh, h8, dr, c)

    def head_pair(t_):
        kh = t_ // 2
        P.dma(kin[0], G[(4 + kh) * 128:(5 + kh) * 128, :], reads=["G"], writes=["kin"])
        P.dma(qin[0], G[kh * 128:(kh + 1) * 128, :], reads=["G"], writes=["qin"])
        heads = [t_]
        for hh, h8 in enumerate(heads):
            P.dma(vins[hh][0], G[(8 + h8) * 128:(9 + h8) * 128, :], reads=["G"], writes=[vins[hh][1]])
            P.op("vector", lambda e, hh=hh: e.memset(oaccs[hh][0], 0.0), writes=[oaccs[hh][1]])
        gens = []
        for hh, h8 in enumerate(heads):
            for dr in range(2):
                ch = hh * 2 + dr
                P.op("vector", lambda e, ch=ch: e.memset(Ssts[ch][0], 0.0), writes=[Ssts[ch][1]])
                gens.append(chain(ch, hh, h8, dr))
        live = list(gens)
        while live:
            for g_ in list(live):
                try:
                    next(g_)
                except StopIteration:
                    live.remove(g_)
        P.phase_end(reset=False)
        for hh, h8 in enumerate(heads):
            zt, ztk = (kin, qin)[hh]
            oacc = oaccs[hh][0]
            ob = vins[hh][0].bitcast(BF16)[:, 0:L]
            P.dma(zt, G[(16 + h8) * 128:(17 + h8) * 128, :], reads=["G"], writes=[ztk])
            P.op("scalar", lambda e, zt=zt: e.activation(out=zt, in_=zt, func=AF.Silu), reads=[ztk], writes=[ztk])

            def norm_piece(c0, zt=zt, ztk=ztk, oacc=oacc, ob=ob):
                sq, sqk = P.scr()
                rs, rsk = P.scr()
                bank = P.bank()
                P.op("scalar", lambda e: e.activation(out=sq, in_=oacc[:, c0:c0 + 128], func=AF.Square), reads=[], writes=[sqk])
                mm(P, bank[:, 0:128], ones, sq, True, True, reads=[sqk, "ones"], writes=[bank])
                P.op("scalar", lambda e: e.activation(out=rs, in_=bank[:, 0:128], func=AF.Sqrt, bias=eps_t, scale=1.0 / 128),
                     reads=[bank, "eps"], writes=[rsk])
                P.op("vector", lambda e: e.reciprocal(out=rs, in_=rs), reads=[rsk], writes=[rsk])
                P.op("vector", lambda e: e.scalar_tensor_tensor(out=rs, in0=oacc[:, c0:c0 + 128], scalar=ngc[:, 0:1], in1=rs,
                                                                op0=ALU.mult, op1=ALU.mult), reads=[rsk, "ngc"], writes=[rsk])
                P.op("vector", lambda e: e.tensor_tensor(out=ob[:, c0:c0 + 128], in0=rs, in1=zt[:, c0:c0 + 128], op=ALU.mult),
                     reads=[rsk, ztk], writes=["ob"])
            for c0 in range(0, L, 128):
                norm_piece(c0)
        P.phase_end(reset=False)
        for hh, h8 in enumerate(heads):
            ob = vins[hh][0].bitcast(BF16)[:, 0:L]
            P.dma(S.Ms[h8][:, 0:SEQ], ob[:, CTX:L])
            P.dma(S.Ms[h8][:, SEQ:L], ob[:, 0:CTX])
        P.phase_end(reset=False)
    for t_ in range(GD_HEADS):
        head_pair(t_)
    P.phase_end()
    for t in range(8):
        P.gather(S.Ms[t], S.Mg[t])
    P.phase_end()


GD_HEADS = 8


def gdn_host(I):
    f = np.float32
    out = {}
    cw = I["gdn_conv_w"][0]
    par = np.zeros((4, 128, 16, 5), f)
    for q in range(4):
        for j in range(16):
            if j < 4:
                ch = 512 * q + 128 * j + np.arange(128)
            elif j < 8:
                ch = 2048 + 512 * q + 128 * (j - 4) + np.arange(128)
            else:
                ch = 4096 + 1024 * q + 128 * (j - 8) + np.arange(128)
            par[q, :, j, :] = cw[:, ch].T
    out["gdpar"] = par.reshape(4 * 128, 80)
    gp = np.zeros((4, 64, 2), f)
    a_log, dtb = I["gdn_a_log"][0], I["gdn_dt_bias"][0]
    for q in range(4):
        for dr in range(2):
            gp[q, dr * 8:dr * 8 + 8, 0] = a_log[dr, 8 * q:8 * q + 8]
            gp[q, dr * 8:dr * 8 + 8, 1] = dtb[dr, 8 * q:8 * q + 8]
    out["gdgp"] = gp.reshape(4 * 64, 2)
    out["gdng"] = np.ascontiguousarray(I["gdn_norm_g"][0].reshape(128, 1))
    W = I["gdn_in_w"][0]
    abw = np.zeros((4, 2048, 64), f)
    for q in range(4):
        for dr in range(2):
            abw[q, :, dr * 8:dr * 8 + 8] = W[:, 12288 + dr * 32 + 8 * q:12288 + dr * 32 + 8 * q + 8]
            abw[q, :, 32 + dr * 8:32 + dr * 8 + 8] = W[:, 12288 + 64 + dr * 32 + 8 * q:12288 + 64 + dr * 32 + 8 * q + 8]
    out["gdabw"] = abw.reshape(4 * 2048, 64)
    p = np.arange(128)
    out["ident"] = np.eye(128, dtype=f)
    out["triu"] = (p[:, None] <= p[None, :]).astype(f)
    out["mlow"] = (p[:, None] > p[None, :]).astype(f)
    return out


def build_full():
    P = Prog()
    S = setup_state(P)
    for i in range(DEPTH + 1):
        weights_for_A(P, S, i)
    for i in range(DEPTH):
        if KINDS[i] != "s5":
            weights_for_M(P, S, i)
    P.phase_end()
    for i in range(DEPTH):
        phase_A(P, S, i)
        {"s5": phase_M_S5, "hy": phase_M_HY, "gdn": phase_M_GDN}[KINDS[i]](P, S, i)
    phase_A(P, S, DEPTH)
    return P.build(), P


def host_inputs(I):
    shared = {}
    shared.update(const_host())
    shared.update(hy_host(I))
    shared.update(gdn_host(I))
    shared.update(s5_host(I, 0))
    shared.update(s5_host(I, 1))
    for i in range(DEPTH):
        shared["ada_b%d" % i] = vec_fm(I["ada_b"][i])
        shared["g1_%d" % i] = vec_fm(I["norm_g"][i, 0])
        shared["g2_%d" % i] = vec_fm(I["norm_g"][i, 1])
    shared["gf"] = vec_fm(I["final_g"])
    shared["glu_b0"] = vec_fm(I["s5_glu_b"][0])
    shared["glu_b3"] = vec_fm(I["s5_glu_b"][1])
    shared["hy_out_b"] = vec_fm(I["hy_out_b"][0])
    big = []
    for i in range(DEPTH):
        big.append(("ada%d" % i, I["ada_w"][i], (4096, 4096, 4096)))
        big.append(("wg%d" % i, I["ffn_w_gate"][i], FFN_CWS))
        big.append(("wu%d" % i, I["ffn_w_up"][i], FFN_CWS))
        big.append(("wd%d" % i, I["ffn_w_down"][i], (2048,)))
    big += [("glu0", I["s5_glu_w"][0], (4096,)), ("glu3", I["s5_glu_w"][1], (4096,)), ("hyo", I["hy_out_w"][0], (2048,)),
            ("gdo", I["gdn_out_w"][0], (2048,)), ("hyin", I["hy_in_w"][0], (2048, 2048, 2048)),
            ("gdin", I["gdn_in_w"][0][:, 0:12288], (4096, 4096, 4096))]
    maps = []
    for k in range(NCORES):
        b = k // 4
        m = dict(shared)
        m["xT0"] = fm(tok_shard(I["x"], I["ctx"], k))
        cc = np.stack([I["c"][b], I["c_ctx"]], axis=1)
        m["silu_in"] = np.ascontiguousarray(cc.reshape(16, 128, 2).transpose(1, 0, 2))
        for (nm, W, cws) in big:
            for j, a in enumerate(shard_cols(W, cws, k)):
                m["%s_%d" % (nm, j)] = a
        maps.append(m)
    return maps


def kernel(**inputs):
    I = {k: np.asarray(v) for k, v in inputs.items()}
    nc, P = get_prog("full", build_full)
    maps = host_inputs(I)
    maps = [{k: np.ascontiguousarray(m[k], dtype=np.float32) for k in P.inputs} for m in maps]
    res = run_bass_kernel_spmd(nc, maps, core_ids=list(range(NCORES))).results
    out = np.empty((B, SEQ, D), np.float32)
    for k in range(NCORES):
        b, r = k // 4, k % 4
        out[b, 1024 * r:1024 * (r + 1), :] = unfm(res[k]["yT"])
    return out
```
